# Optimizing a Trainium2 kernel written in Bass

```python
import math
import jax, jax.numpy as jnp
from jax import lax
import numpy as np

D_MODEL = 1024
BATCH = 4
SEQ = 4096
DEPTH = 4
DEC_BATCH = 8
DEC_SEQ = 16
PAST_LEN = 4096

CHUNK = 64
Q_BLOCK = 128
N_BRANCH = 4
A_HEADS = 4
A_NOPE = 64
A_ROPE = 32
A_V = 64
A_Q_LORA = 192
A_KV_LORA = 128
A_SCALE = (A_NOPE + A_ROPE) ** -0.5
ROPE_BASE = 10000.0
B_HEADS = 4
B_DIM = 64
B_SCALE = B_DIM ** -0.5
IDX_HEADS = 4
IDX_DIM = 32
IDX_SCALE = (IDX_HEADS ** -0.5) * (IDX_DIM ** -0.5)
TOPK_MAX = 256
REL_BUCKETS = 32
REL_MAX_DIST = 128
POOL_WINDOWS = (2, 4, 8, 16)
POOL_WIDTH = 256
POOL_GROUP = POOL_WIDTH // len(POOL_WINDOWS)
POOL_STATE = max(POOL_WINDOWS) - 1
CONV_WIDTH = 256
CONV_K = 31
D_FF = 2816
FFN_K = 3
ALPHA = (2 * DEPTH) ** 0.25
BETA = (8 * DEPTH) ** -0.25
LN_EPS = 1e-5
F32 = jnp.float32

IN_SPLITS = (A_Q_LORA, A_KV_LORA, A_ROPE,
             B_HEADS * B_DIM, B_HEADS * B_DIM, B_HEADS * B_DIM,
             IDX_HEADS * IDX_DIM, IDX_DIM, IDX_HEADS,
             POOL_WIDTH, 2 * CONV_WIDTH, N_BRANCH * D_MODEL)
D_IN = sum(IN_SPLITS)
IN_OFFSETS = tuple(int(v) for v in np.cumsum(IN_SPLITS)[:-1])

kernel_name = 'hybrid_streaming_encoder_step'


def layer_norm(x, g, b):
    xf = x.astype(F32)
    mu = jnp.mean(xf, axis=-1, keepdims=True)
    var = jnp.mean(jnp.square(xf - mu), axis=-1, keepdims=True)
    return ((xf - mu) * lax.rsqrt(var + LN_EPS)).astype(x.dtype) * g + b


def rms_norm(x, g):
    xf = x.astype(F32)
    ms = jnp.mean(jnp.square(xf), axis=-1, keepdims=True)
    return (xf * lax.rsqrt(ms + LN_EPS)).astype(x.dtype) * g


def rope(x, pos):
    half = x.shape[-1] // 2
    freqs = ROPE_BASE ** (-jnp.arange(half, dtype=F32) / half)
    ang = pos.astype(F32)[:, None] * freqs[None, :]
    cos = jnp.cos(ang)[:, None, :].astype(x.dtype)
    sin = jnp.sin(ang)[:, None, :].astype(x.dtype)
    x1, x2 = x[..., :half], x[..., half:]
    return jnp.concatenate([x1 * cos - x2 * sin, x1 * sin + x2 * cos], axis=-1)


def rel_bucket(rel):
    nb = REL_BUCKETS // 2
    max_exact = nb // 2
    ret = jnp.where(rel > 0, nb, 0)
    n = jnp.abs(rel)
    nf = jnp.maximum(n, 1).astype(F32)
    large = max_exact + (jnp.log(nf / max_exact) / math.log(REL_MAX_DIST / max_exact)
                         * (nb - max_exact)).astype(jnp.int32)
    large = jnp.minimum(large, nb - 1)
    return ret + jnp.where(n < max_exact, n, large)


def chunk_visible(q_pos, k_pos):
    return (k_pos // CHUNK) <= (q_pos // CHUNK)


def over_query_blocks(fn, q_pos, *q_args):
    T = q_pos.shape[0]
    if T <= Q_BLOCK:
        return fn(q_pos, *q_args)
    nb = T // Q_BLOCK
    def split(a):
        return jnp.moveaxis(a.reshape((a.shape[0], nb, Q_BLOCK) + a.shape[2:]), 1, 0)
    xs = (q_pos.reshape(nb, Q_BLOCK),) + tuple(split(a) for a in q_args)
    out = lax.map(lambda args: fn(*args), xs)
    out = jnp.moveaxis(out, 0, 1)
    return out.reshape((out.shape[0], T) + out.shape[3:])


def mla_attend(q_pos, qb, k, v, k_pos):
    s = jnp.einsum('bqhd,bshd->bhqs', qb, k).astype(F32) * A_SCALE
    vis = chunk_visible(q_pos[:, None], k_pos[None, :])
    s = jnp.where(vis[None, None], s, -jnp.inf)
    p = jax.nn.softmax(s, axis=-1).astype(v.dtype)
    return jnp.einsum('bhqs,bshd->bqhd', p, v)


def dsa_attend(q_pos, qb, qib, wib, k, v, k_idx, k_pos, rel_bias, topk):
    B = qb.shape[0]
    vis = chunk_visible(q_pos[:, None], k_pos[None, :])
    iscore = jax.nn.relu(jnp.einsum('bqhd,bsd->bqhs', qib, k_idx).astype(F32))
    iscore = jnp.einsum('bqh,bqhs->bqs', wib.astype(F32), iscore)
    iscore = jnp.where(vis[None], iscore, -jnp.inf)
    _, sel = lax.top_k(iscore, topk)
    bidx = jnp.arange(B)[:, None, None]
    k_sel = k[bidx, sel]
    v_sel = v[bidx, sel]
    sel_pos = k_pos[sel]
    sel_vis = chunk_visible(q_pos[None, :, None], sel_pos)
    logits = jnp.einsum('bqhd,bqkhd->bqhk', qb, k_sel).astype(F32) * B_SCALE
    bias = rel_bias[rel_bucket(sel_pos - q_pos[None, :, None])].astype(F32)
    logits = logits + jnp.moveaxis(bias, -1, 2)
    logits = jnp.where(sel_vis[:, :, None, :], logits, -jnp.inf)
    p = jax.nn.softmax(logits, axis=-1).astype(v.dtype)
    return jnp.einsum('bqhk,bqkhd->bqhd', p, v_sel)


def pool_mix(u, prev, q_pos, pool_w, pool_scale):
    B, T, C = u.shape
    ext = jnp.concatenate([prev, u], axis=1)
    cs = jnp.concatenate([jnp.zeros((B, 1, C), F32), jnp.cumsum(ext.astype(F32), axis=1)], axis=1)
    outs = []
    for g, w in enumerate(POOL_WINDOWS):
        lo, hi = g * POOL_GROUP, (g + 1) * POOL_GROUP
        s = (cs[:, POOL_STATE + 1:POOL_STATE + 1 + T, lo:hi]
             - cs[:, POOL_STATE + 1 - w:POOL_STATE + 1 - w + T, lo:hi])
        cnt = jnp.minimum(w, q_pos + 1).astype(F32)[None, :, None]
        outs.append(s / cnt)
    pooled = jnp.concatenate(outs, axis=-1).astype(u.dtype) - u
    pooled = pooled.reshape(B, T, len(POOL_WINDOWS), POOL_GROUP)
    mixed = jnp.einsum('btgc,gcd->btgd', pooled, pool_w).reshape(B, T, C)
    return mixed * pool_scale, ext[:, -POOL_STATE:]


def causal_dwconv(u, prev, w, b):
    C = u.shape[-1]
    ext = jnp.concatenate([prev, u], axis=1)
    out = lax.conv_general_dilated(ext, w[:, None, :], window_strides=(1,), padding='VALID',
                                   dimension_numbers=('NWC', 'WIO', 'NWC'),
                                   feature_group_count=C)
    return out + b, ext[:, -(w.shape[0] - 1):]


def trunk_layer(x, past_ckv, past_krope, past_bk, past_bv, past_bkidx,
                past_pool, past_conv, past_ffn, lw, rel_bias, topk):
    (w_in, a_q_norm, a_kv_norm, a_w_qup, a_w_kvup, pool_w, pool_scale,
     conv_w, conv_b, conv_ln_g, conv_ln_b, w_branch, w_out, ln1_g, ln1_b,
     w_up, ffn_conv_w, ffn_conv_b, w_down, ln2_g, ln2_b) = lw
    B, T, _ = x.shape
    P = past_ckv.shape[1]
    q_pos = P + jnp.arange(T, dtype=jnp.int32)
    k_pos = jnp.arange(P + T, dtype=jnp.int32)

    proj = x @ w_in
    (cq, ckv, krope, bq, bk, bv, qidx, kidx, widx, upool, uconv, gates) = jnp.split(proj, IN_OFFSETS, axis=-1)

    cq = rms_norm(cq, a_q_norm)
    ckv = rms_norm(ckv, a_kv_norm)
    krope = rope(krope[:, :, None, :], q_pos)[:, :, 0, :]
    qa = (cq @ a_w_qup).reshape(B, T, A_HEADS, A_NOPE + A_ROPE)
    qa = jnp.concatenate([qa[..., :A_NOPE], rope(qa[..., A_NOPE:], q_pos)], axis=-1)
    ckv_all = jnp.concatenate([past_ckv, ckv], axis=1)
    krope_all = jnp.concatenate([past_krope, krope], axis=1)
    kv = (ckv_all @ a_w_kvup).reshape(B, P + T, A_HEADS, A_NOPE + A_V)
    ka = jnp.concatenate([kv[..., :A_NOPE],
                          jnp.broadcast_to(krope_all[:, :, None, :], (B, P + T, A_HEADS, A_ROPE))], axis=-1)
    va = kv[..., A_NOPE:]
    out_a = over_query_blocks(lambda qp, qb: mla_attend(qp, qb, ka, va, k_pos), q_pos, qa)

    bq = bq.reshape(B, T, B_HEADS, B_DIM)
    bk = bk.reshape(B, T, B_HEADS, B_DIM)
    bv = bv.reshape(B, T, B_HEADS, B_DIM)
    bk_all = jnp.concatenate([past_bk, bk], axis=1)
    bv_all = jnp.concatenate([past_bv, bv], axis=1)
    kidx_all = jnp.concatenate([past_bkidx, kidx], axis=1)
    qidx = qidx.reshape(B, T, IDX_HEADS, IDX_DIM)
    widx = widx * IDX_SCALE
    out_b = over_query_blocks(
        lambda qp, qb, qib, wib: dsa_attend(qp, qb, qib, wib, bk_all, bv_all, kidx_all, k_pos, rel_bias, topk),
        q_pos, bq, qidx, widx)

    out_c, new_pool = pool_mix(upool, past_pool, q_pos, pool_w, pool_scale)

    ga, gg = jnp.split(uconv, 2, axis=-1)
    glu = ga * jax.nn.sigmoid(gg)
    conv_out, new_conv = causal_dwconv(glu, past_conv, conv_w, conv_b)
    out_d = jax.nn.silu(layer_norm(conv_out, conv_ln_g, conv_ln_b))

    branches = jnp.stack([out_a.reshape(B, T, A_HEADS * A_V), out_b.reshape(B, T, B_HEADS * B_DIM),
                          out_c, out_d], axis=2)
    br = jnp.einsum('btnc,ncd->btnd', branches, w_branch)
    g = jax.nn.sigmoid(gates.reshape(B, T, N_BRANCH, D_MODEL))
    mixed = jnp.sum(g * br, axis=2) @ w_out
    x = layer_norm(ALPHA * x + mixed, ln1_g, ln1_b)

    up = x @ w_up
    upc, new_ffn = causal_dwconv(up, past_ffn, ffn_conv_w, ffn_conv_b)
    val, gate = jnp.split(upc, 2, axis=-1)
    h = (jax.nn.silu(gate) * val) @ w_down
    x = layer_norm(ALPHA * x + h, ln2_g, ln2_b)
    return x, (ckv, krope, bk, bv, kidx, new_pool, new_conv, new_ffn)


def setup_inputs(seed: int = 0) -> dict:
    key = jax.random.key(seed)
    ks = iter(jax.random.split(key, 40))
    def nrm(shape, scale):
        return jax.random.normal(next(ks), shape, F32) * scale
    def gain(shape):
        return 1.0 + nrm(shape, 0.05)
    L = DEPTH
    return {
        'x_prompt': nrm((BATCH, SEQ, D_MODEL), 1.0),
        'x_sample': nrm((DEC_BATCH, DEC_SEQ, D_MODEL), 1.0),
        'cache_a_ckv': nrm((L, DEC_BATCH, PAST_LEN, A_KV_LORA), 1.0),
        'cache_a_krope': nrm((L, DEC_BATCH, PAST_LEN, A_ROPE), 1.0),
        'cache_b_k': nrm((L, DEC_BATCH, PAST_LEN, B_HEADS, B_DIM), 1.0),
        'cache_b_v': nrm((L, DEC_BATCH, PAST_LEN, B_HEADS, B_DIM), 1.0),
        'cache_b_kidx': nrm((L, DEC_BATCH, PAST_LEN, IDX_DIM), 1.0),
        'state_pool': nrm((L, DEC_BATCH, POOL_STATE, POOL_WIDTH), 1.0),
        'state_conv': nrm((L, DEC_BATCH, CONV_K - 1, CONV_WIDTH), 1.0),
        'state_ffn': nrm((L, DEC_BATCH, FFN_K - 1, 2 * D_FF), 1.0),
        'rel_bias': nrm((REL_BUCKETS, B_HEADS), 0.5),
        'ln_in_g': gain((D_MODEL,)),
        'ln_in_b': nrm((D_MODEL,), 0.02),
        'w_in': nrm((L, D_MODEL, D_IN), D_MODEL ** -0.5),
        'a_q_norm': gain((L, A_Q_LORA)),
        'a_kv_norm': gain((L, A_KV_LORA)),
        'a_w_qup': nrm((L, A_Q_LORA, A_HEADS * (A_NOPE + A_ROPE)), A_Q_LORA ** -0.5),
        'a_w_kvup': nrm((L, A_KV_LORA, A_HEADS * (A_NOPE + A_V)), A_KV_LORA ** -0.5),
        'pool_w': nrm((L, len(POOL_WINDOWS), POOL_GROUP, POOL_GROUP), POOL_GROUP ** -0.5),
        'pool_scale': 1.0 + nrm((L, POOL_WIDTH), 0.1),
        'conv_w': nrm((L, CONV_K, CONV_WIDTH), CONV_K ** -0.5),
        'conv_b': nrm((L, CONV_WIDTH), 0.02),
        'conv_ln_g': gain((L, CONV_WIDTH)),
        'conv_ln_b': nrm((L, CONV_WIDTH), 0.02),
        'w_branch': nrm((L, N_BRANCH, 256, D_MODEL), BETA * 256 ** -0.5),
        'w_out': nrm((L, D_MODEL, D_MODEL), BETA * D_MODEL ** -0.5),
        'ln1_g': gain((L, D_MODEL)),
        'ln1_b': nrm((L, D_MODEL), 0.02),
        'w_up': nrm((L, D_MODEL, 2 * D_FF), D_MODEL ** -0.5),
        'ffn_conv_w': nrm((L, FFN_K, 2 * D_FF), FFN_K ** -0.5),
        'ffn_conv_b': nrm((L, 2 * D_FF), 0.02),
        'w_down': nrm((L, D_FF, D_MODEL), BETA * D_FF ** -0.5),
        'ln2_g': gain((L, D_MODEL)),
        'ln2_b': nrm((L, D_MODEL), 0.02),
    }


def reference(x_prompt, x_sample, cache_a_ckv, cache_a_krope, cache_b_k, cache_b_v, cache_b_kidx,
              state_pool, state_conv, state_ffn, rel_bias, ln_in_g, ln_in_b, w_in, a_q_norm, a_kv_norm,
              a_w_qup, a_w_kvup, pool_w, pool_scale, conv_w, conv_b, conv_ln_g, conv_ln_b, w_branch,
              w_out, ln1_g, ln1_b, w_up, ffn_conv_w, ffn_conv_b, w_down, ln2_g, ln2_b):
    xp = layer_norm(x_prompt, ln_in_g, ln_in_b)
    xs = layer_norm(x_sample, ln_in_g, ln_in_b)
    dt = xp.dtype
    Bp, Tp = xp.shape[0], xp.shape[1]
    Ts = xs.shape[1]
    topk_p = min(TOPK_MAX, Tp // 4)
    topk_s = min(TOPK_MAX, (PAST_LEN + Ts) // 4)
    z_ckv = jnp.zeros((Bp, 0, A_KV_LORA), dt)
    z_krope = jnp.zeros((Bp, 0, A_ROPE), dt)
    z_bk = jnp.zeros((Bp, 0, B_HEADS, B_DIM), dt)
    z_kidx = jnp.zeros((Bp, 0, IDX_DIM), dt)
    z_pool = jnp.zeros((Bp, POOL_STATE, POOL_WIDTH), dt)
    z_conv = jnp.zeros((Bp, CONV_K - 1, CONV_WIDTH), dt)
    z_ffn = jnp.zeros((Bp, FFN_K - 1, 2 * D_FF), dt)
    p_states, s_states = [], []
    for l in range(DEPTH):
        lw = (w_in[l], a_q_norm[l], a_kv_norm[l], a_w_qup[l], a_w_kvup[l], pool_w[l], pool_scale[l],
              conv_w[l], conv_b[l], conv_ln_g[l], conv_ln_b[l], w_branch[l], w_out[l], ln1_g[l], ln1_b[l],
              w_up[l], ffn_conv_w[l], ffn_conv_b[l], w_down[l], ln2_g[l], ln2_b[l])
        xp, st_p = trunk_layer(xp, z_ckv, z_krope, z_bk, z_bk, z_kidx, z_pool, z_conv, z_ffn,
                               lw, rel_bias, topk_p)
        xs, st_s = trunk_layer(xs, cache_a_ckv[l], cache_a_krope[l], cache_b_k[l], cache_b_v[l],
                               cache_b_kidx[l], state_pool[l], state_conv[l], state_ffn[l],
                               lw, rel_bias, topk_s)
        p_states.append(st_p)
        s_states.append(st_s)
    p_ckv, p_krope, p_bk, p_bv, p_kidx, p_pool, p_conv, p_ffn = [jnp.stack(a) for a in zip(*p_states)]
    s_ckv, s_krope, s_bk, s_bv, s_kidx, s_pool, s_conv, s_ffn = [jnp.stack(a) for a in zip(*s_states)]
    return (xp, xs, p_ckv, p_krope, p_bk, p_bv, p_kidx, p_pool, p_conv, p_ffn,
            s_ckv, s_krope, s_bk, s_bv, s_kidx, s_pool, s_conv, s_ffn)
```

```python
import math
from contextlib import ExitStack

import numpy as np
import concourse.bass as bass
import concourse.mybir as mybir
from concourse.bass_utils import run_bass_kernel_spmd

F32 = mybir.dt.float32
BF16 = mybir.dt.bfloat16
ALU = mybir.AluOpType
AF = mybir.ActivationFunctionType
AX = mybir.AxisListType

D = 1024
DIN = 6148
DFF = 2816
NFC = DFF // 128
O_CQ, O_CKV, O_KR, O_BQ, O_BK, O_BV, O_QI, O_KI, O_WI, O_UP, O_UC, O_G = (
    0, 192, 320, 352, 608, 864, 1120, 1248, 1280, 1284, 1540, 2052)
A_SCALE = 96 ** -0.5
B_SCALE = 64 ** -0.5
IDX_SCALE = (4 ** -0.5) * (32 ** -0.5)
LN_EPS = 1e-5
NEG_BIG = -1.0e30
NIT = 12


class Cfg:
    def __init__(self, SEQ=4096, DEPTH=4, PAST=4096, DEC_SEQ=16, TB=512, GQ=2):
        self.SEQ, self.L, self.PAST, self.TS, self.TB, self.GQ = SEQ, DEPTH, PAST, DEC_SEQ, TB, GQ
        self.ALPHA = (2 * DEPTH) ** 0.25
        self.NB = SEQ // TB
        self.KP = min(256, SEQ // 4)
        self.KS = min(256, (PAST + DEC_SEQ) // 4)
        self.NKMAX = max(SEQ, PAST + 128)


class Dep:
    __slots__ = ("w", "r")

    def __init__(self):
        self.w = None
        self.r = []


class T:
    __slots__ = ("t", "d", "ex")

    def __init__(self, t, d=None, ex=False):
        self.t = t
        self.d = d if d is not None else Dep()
        self.ex = ex

    def __getitem__(self, k):
        return self.t[k]


class KB:
    NR = 8

    def __init__(self, nc, es):
        self.nc = nc
        self.es = es
        self.E = {'pe': nc.tensor, 'act': nc.scalar, 'dve': nc.vector, 'pool': nc.gpsimd, 'sp': nc.sync}
        self.sem = {e: es.enter_context(nc.semaphore("s_" + e)) for e in ['pe', 'act', 'dve', 'pool']}
        self.cnt = {e: 0 for e in self.sem}
        self.waited = {e: {} for e in self.E}
        self.dring = {q: [es.enter_context(nc.semaphore("d_%s%d" % (q, i))) for i in range(self.NR)]
                      for q in ['sp', 'pool']}
        self.dcnt = {q: 0 for q in self.dring}
        self.n_ins = 0
        self.n_wait = 0
        self._pend = []
        self.pending_dma = []
        self.ps_banks = [T(es.enter_context(nc.psum_tensor("psb%d" % i, [128, 512], F32)), ex=True) for i in range(8)]
        self.ps_groups = {'any': list(range(8))}
        self.ps_ctr = {}

    def sb(self, name, shape, dt=F32, es=None):
        self._nm = getattr(self, "_nm", 0) + 1
        return T((es or self.es).enter_context(self.nc.sbuf_tensor("%s_%d" % (name, self._nm), list(shape), dt)))

    def ps(self, group='any'):
        banks = self.ps_groups[group]
        i = self.ps_ctr.get(group, 0)
        self.ps_ctr[group] = i + 1
        return self.ps_banks[banks[i % len(banks)]]

    def _wait(self, e, evt):
        if evt is None:
            return
        key, sem, val = evt
        if key == e and e == 'pe':
            return
        w = self.waited[e]
        if w.get(key, 0) >= val:
            return
        self.E[e].wait_ge(sem, val)
        w[key] = val
        self.n_wait += 1

    def _deps(self, e, reads, writes):
        for t in reads:
            self._wait(e, t.d.w)
        for t in writes:
            self._wait(e, t.d.w)
            for ev in t.d.r:
                self._wait(e, ev)

    @staticmethod
    def _mark(evt, reads, writes):
        for t in reads:
            t.d.r.append(evt)
        for t in writes:
            t.d.w = evt
            t.d.r = []

    def op(self, e, fn, reads=(), writes=(), inc=True):
        ex = [t for t in reads if t.ex]
        if ex:
            reads = [t for t in reads if not t.ex]
            writes = list(writes) + ex
        self._deps(e, reads, writes)
        ins = fn(self.E[e])
        self.n_ins += 1
        if not inc:
            self._pend.append((list(reads), list(writes)))
            return None
        ins.then_inc(self.sem[e], 1)
        self.cnt[e] += 1
        evt = (e, self.sem[e], self.cnt[e])
        if e == 'pe' and self._pend:
            for (r_, w_) in self._pend:
                self._mark(evt, r_, w_)
            self._pend = []
        self._mark(evt, reads, writes)
        return evt

    def dma(self, q, out, in_, reads=(), writes=(), **kw):
        self._deps(q, reads, writes)
        i = self.dcnt[q] % self.NR
        gen = self.dcnt[q] // self.NR + 1
        sem = self.dring[q][i]
        key = "d_%s%d" % (q, i)
        if gen > 1:
            self._wait(q, (key, sem, 16 * (gen - 1)))
        self.E[q].dma_start(out=out, in_=in_, **kw).then_inc(sem, 16)
        self.dcnt[q] += 1
        evt = (key, sem, 16 * gen)
        self._mark(evt, reads, writes)
        self.pending_dma.append(evt)
        self.n_ins += 1
        return evt

    def barrier(self, engines=('pe', 'act', 'dve', 'sp', 'pool')):
        evs = [(e, self.sem[e], self.cnt[e]) for e in self.sem if self.cnt[e] > 0]
        evs += self.pending_dma
        self.pending_dma = []
        for e in engines:
            for ev in evs:
                self._wait(e, ev)

    def finish(self):
        self.barrier(engines=('sp',))


class Gen:
    def __init__(self, cfg):
        self.cfg = cfg

    def declare(self, nc):
        c = self.cfg
        L = c.L
        I = {}
        O = {}

        def inp(name, shape):
            I[name] = nc.dram_tensor(name, list(shape), F32, kind="ExternalInput").ap()

        def outp(name, shape):
            O[name] = nc.dram_tensor(name, list(shape), F32, kind="ExternalOutput").ap()

        inp("xp", [c.SEQ, D]); inp("xs", [c.TS, D])
        inp("c_ckv", [L, c.PAST, 128]); inp("c_krope", [L, c.PAST, 32]); inp("c_bk", [L, c.PAST, 256])
        inp("c_bv", [L, c.PAST, 256]); inp("c_kidx", [L, c.PAST, 32])
        inp("st_pool", [L, 15, 256]); inp("st_conv", [L, 30, 256]); inp("st_ffn", [L, 2, 2 * DFF])
        inp("rel_bias", [1, 128]); inp("ln_in_g", [1, D]); inp("ln_in_b", [1, D])
        inp("w_in", [L, D, DIN]); inp("a_q_norm", [L, 192]); inp("a_kv_norm", [L, 128])
        inp("a_w_qup", [L, 192, 384]); inp("a_w_kvup", [L, 128, 512]); inp("pool_w", [L, 4, 64, 64])
        inp("pool_scale", [L, 256]); inp("conv_w", [L, 31, 256]); inp("conv_b", [L, 256])
        inp("conv_ln_g", [L, 256]); inp("conv_ln_b", [L, 256]); inp("w_branch", [L, 4, 256, D])
        inp("w_out", [L, D, D]); inp("ln1_g", [L, D]); inp("ln1_b", [L, D]); inp("w_up", [L, D, 2 * DFF])
        inp("ffn_conv_w", [L, 3, 2 * DFF]); inp("ffn_conv_b", [L, 2 * DFF]); inp("w_down", [L, DFF, D])
        inp("ln2_g", [L, D]); inp("ln2_b", [L, D])
        inp("rope_p", [c.SEQ, 64]); inp("rope_s", [c.TS, 64]); inp("bidx", [128, 258]); inp("icnt0", [128, 2 * c.TB])
        outp("y_p", [c.SEQ, D]); outp("y_s", [c.TS, D])
        for pre, n in (("p", c.SEQ), ("s", c.TS)):
            outp(pre + "_ckv", [L, n, 128]); outp(pre + "_krope", [L, n, 32]); outp(pre + "_bk", [L, n, 256])
            outp(pre + "_bv", [L, n, 256]); outp(pre + "_kidx", [L, n, 32])
            outp(pre + "_pool", [L, 15, 256]); outp(pre + "_conv", [L, 30, 256]); outp(pre + "_ffn", [L, 2, 2 * DFF])
        self.I, self.O = I, O

    def mm(self, ps, out_ap, lt, lhsT, rt, rhs, start, stop, extra_r=(), inc=True, **kw):
        self.kb.op('pe', lambda e: e.matmul(out_ap, lhsT=lhsT, rhs=rhs, start=start, stop=stop, **kw),
                   reads=[lt, rt] + list(extra_r), writes=[ps], inc=inc)

    def tr(self, ps, out_ap, it, in_ap, rows, bf=True):
        idt = self.ident16 if bf else self.ident
        self.kb.op('pe', lambda e: e.transpose(out=out_ap, in_=in_ap, identity=idt[:rows, :rows]),
                   reads=[it, idt], writes=[ps])

    def build(self, nc):
        cfg = self.cfg
        self.declare(nc)
        es = ExitStack()
        with es:
            kb = KB(nc, es)
            self.kb = kb
            self.nc = nc
            self.setup_persistent()
            self.wplan = []
            self.wpos = 0
            self.plan_weights()
            self.wissue = 0
            for j in range(cfg.NB):
                self.block('p', j)
            self.block('s', 0)
            assert self.wpos == len(self.wplan), (self.wpos, len(self.wplan))
            kb.finish()
            self.stats = (kb.n_ins, kb.n_wait)
        return nc

    def setup_persistent(self):
        kb, cfg, I = self.kb, self.cfg, self.I
        L = cfg.L
        sb = kb.sb
        self.ident = sb("ident", [128, 128])
        self.ident16 = sb("ident16", [128, 128], BF16)
        kb.op('dve', lambda e: e.memset(self.ident[:], 1.0), writes=[self.ident])
        kb.op('pool', lambda e: e.affine_select(out=self.ident[:], in_=self.ident[:], pattern=[[-1, 128]],
                                                compare_op=ALU.is_equal, fill=0.0, base=0, channel_multiplier=1),
              reads=[self.ident], writes=[self.ident])
        kb.op('dve', lambda e: e.tensor_copy(out=self.ident16[:], in_=self.ident[:]), reads=[self.ident],
              writes=[self.ident16])
        self.ones1024 = sb("ones1024", [128, 128])
        self.ones256 = sb("ones256", [128, 128])
        kb.op('dve', lambda e: e.memset(self.ones1024[:], 1.0 / 1024), writes=[self.ones1024])
        kb.op('dve', lambda e: e.memset(self.ones256[:], 1.0 / 256), writes=[self.ones256])
        o16a, o16b = sb("ones1024b", [128, 128], BF16), sb("ones256b", [128, 128], BF16)
        kb.op('dve', lambda e: e.memset(o16a[:], 1.0 / 1024), writes=[o16a])
        kb.op('dve', lambda e: e.memset(o16b[:], 1.0 / 256), writes=[o16b])
        self.ones16 = {self.ones1024: o16a, self.ones256: o16b}
        self.esel = sb("esel", [32, 96], BF16)
        kb.op('dve', lambda e: e.memset(self.esel[:], 0.0), writes=[self.esel])
        kb.op('dve', lambda e: e.tensor_copy(out=self.esel[:, 64:96], in_=self.ident[0:32, 0:32]),
              reads=[self.ident, self.esel], writes=[self.esel])
        TB = cfg.TB
        self.ropet = sb("ropet", [128, 4, 64])
        self.x = sb("x", [128, 8, TB])
        self.xT = sb("xT", [128, 8, TB], BF16)
        self.qn = sb("qn", [128, L, 320])
        for l in range(L):
            kb.dma('sp', self.qn[:, l, 0:192], I["a_q_norm"][l:l + 1, :].partition_broadcast(128), writes=[self.qn])
            kb.dma('sp', self.qn[:, l, 192:320], I["a_kv_norm"][l:l + 1, :].partition_broadcast(128), writes=[self.qn])
        self.rb = sb("rb", [128, 128])
        kb.dma('sp', self.rb[:], I["rel_bias"].partition_broadcast(128), writes=[self.rb])
        self.pv_ln = sb("pv_ln", [128, L, 4, 8])
        self.pv_lnin = sb("pv_lnin", [128, 2, 8])
        self.pv_c = sb("pv_c", [128, L, 2, 35])
        self.pv_f = sb("pv_f", [128, L, 44, 4])
        self.pcar = {k: sb("pcar_" + k, [128, L, 2, 15]) for k in 'ps'}
        self.ccar = {k: sb("ccar_" + k, [128, L, 2, 30]) for k in 'ps'}
        self.fcar = {k: sb("fcar_" + k, [128, L, 44, 2]) for k in 'ps'}
        for t in (self.pcar['p'], self.ccar['p'], self.fcar['p']):
            kb.op('dve', lambda e, t=t: e.memset(t[:], 0.0), writes=[t])
        self.gt = sb("gt", [128, 4, 258])
        self.NW = 4
        self.wring = [sb("wring%d" % i, [128, 4096], BF16) for i in range(self.NW)]
        self._wsm_ctr = 0
        self._wr_ctr = 0
        self._wbr_ctr = 0
        self.w_released = 0
        self.w_ringreq = 0
        self.wsm = [sb("wsm%d" % i, [128, 1536], BF16) for i in range(2)]
        for t in self.wsm:
            kb.op('dve', lambda e, t=t: e.memset(t[:], 0.0), writes=[t])
        with ExitStack() as pes:
            stg = kb.sb("pstage", [32, 2 * DFF], F32, es=pes)

            def colvec(dst_fn, src_ap, R, C, dst_t):
                kb.dma('sp', stg[:R, 0:C * 128], src_ap, writes=[stg])
                c0 = 0
                while c0 < C:
                    nch = min(512 // max(R, 1), C - c0, 16)
                    nch = max(1, min(nch, 512 // R))
                    ps = kb.ps()
                    for i in range(nch):
                        self.tr(ps, ps[:, i * R:(i + 1) * R], stg, stg[:R, (c0 + i) * 128:(c0 + i + 1) * 128], R, bf=False)
                    for i in range(nch):
                        kb.op('act', lambda e, i=i: e.copy(out=dst_fn(c0 + i), in_=ps[:, i * R:(i + 1) * R]),
                              reads=[ps], writes=[dst_t])
                    c0 += nch

            colvec(lambda c: self.pv_lnin[:, 0, c:c + 1], I["ln_in_g"], 1, 8, self.pv_lnin)
            colvec(lambda c: self.pv_lnin[:, 1, c:c + 1], I["ln_in_b"], 1, 8, self.pv_lnin)
            for l in range(L):
                for i, nm in enumerate(("ln1_g", "ln1_b", "ln2_g", "ln2_b")):
                    colvec(lambda c, i=i: self.pv_ln[:, l, i, c:c + 1], I[nm][l:l + 1, :], 1, 8, self.pv_ln)
                colvec(lambda c: self.pv_c[:, l, c, 0:31], I["conv_w"][l], 31, 2, self.pv_c)
                for i, nm in enumerate(("conv_b", "conv_ln_g", "conv_ln_b", "pool_scale")):
                    colvec(lambda c, i=i: self.pv_c[:, l, c, 31 + i:32 + i], I[nm][l:l + 1, :], 1, 2, self.pv_c)
                colvec(lambda c: self.pv_f[:, l, c, 0:3], I["ffn_conv_w"][l], 3, 44, self.pv_f)
                colvec(lambda c: self.pv_f[:, l, c, 3:4], I["ffn_conv_b"][l:l + 1, :], 1, 44, self.pv_f)
                colvec(lambda c: self.pcar['s'][:, l, c, :], I["st_pool"][l], 15, 2, self.pcar['s'])
                colvec(lambda c: self.ccar['s'][:, l, c, :], I["st_conv"][l], 30, 2, self.ccar['s'])
                colvec(lambda c: self.fcar['s'][:, l, c, :], I["st_ffn"][l], 2, 44, self.fcar['s'])
            bidx = kb.sb("bidx_sb", [128, 258], F32, es=pes)
            tmpg = kb.sb("tmpg", [128, 258], F32, es=pes)
            kb.dma('sp', bidx[:], I["bidx"], writes=[bidx])
            for h in range(4):
                kb.op('dve', lambda e, h=h: e.tensor_scalar(out=self.gt[:, h, :], in0=bidx[:], scalar1=0.0,
                                                            scalar2=self.rb[:, 15 * 4 + h:15 * 4 + h + 1],
                                                            op0=ALU.mult, op1=ALU.subtract),
                      reads=[bidx, self.rb], writes=[self.gt])
                for b in range(32):
                    kb.op('dve', lambda e, h=h, b=b: e.tensor_scalar(out=tmpg[:], in0=bidx[:], scalar1=float(b),
                                                                    scalar2=self.rb[:, b * 4 + h:b * 4 + h + 1],
                                                                    op0=ALU.is_equal, op1=ALU.mult),
                          reads=[bidx, self.rb], writes=[tmpg])
                    kb.op('dve', lambda e, h=h: e.tensor_tensor(out=self.gt[:, h, :], in0=self.gt[:, h, :], in1=tmpg[:],
                                                                op=ALU.add),
                          reads=[tmpg, self.gt], writes=[self.gt])
            kb.barrier()
    def plan_weights(self):
        cfg, I = self.cfg, self.I
        plan = []
        nblocks = cfg.NB + 1
        for b in range(nblocks):
            for l in range(cfg.L):
                w_in = I["w_in"][l].rearrange("(k p) n -> p k n", p=128)
                plan.append(("wsm", l, None))
                plan.append(("wtok_a0", l, [(w_in[:, :, 0:352], (8, 352), 0)]))
                plan.append(("wtok_a1", l, [(w_in[:, :, 352:864], (8, 512), 0)]))
                plan.append(("wtok_b", l, [(w_in[:, :, 864:1284], (8, 420), 0)]))
                plan.append(("wpc0", l, [(w_in[:, :, 1284:1668], (8, 384), 0)]))
                plan.append(("wpc1", l, [(w_in[:, :, 1668:2052], (8, 384), 0)]))
                wbr = I["w_branch"][l].rearrange("n (k p) d -> p (n k) d", p=128)
                for hf in range(2):
                    for n in range(4):
                        c0 = O_G + n * 1024 + hf * 512
                        plan.append(("wbr%d%d" % (n, hf), l, [(wbr[:, 2 * n:2 * n + 2, hf * 512:(hf + 1) * 512], (2, 512), 0)]))
                        plan.append(("wg%d%d" % (n, hf), l, [(w_in[:, :, c0:c0 + 512], (8, 512), 0)]))
                w_o = I["w_out"][l].rearrange("(k p) n -> p k n", p=128)
                for hf in range(2):
                    plan.append(("wout%d" % hf, l, [(w_o[:, :, hf * 512:(hf + 1) * 512], (8, 512), 0)]))
                w_up = I["w_up"][l].rearrange("(k p) n -> p k n", p=128)
                for g in range(6):
                    nch = 4 if g < 5 else 2
                    plan.append(("wupv%d" % g, l, [(w_up[:, :, g * 512:g * 512 + nch * 128], (8, nch * 128), 0)]))
                    plan.append(("wupg%d" % g, l, [(w_up[:, :, DFF + g * 512:DFF + g * 512 + nch * 128], (8, nch * 128), 0)]))
                w_dn = I["w_down"][l].rearrange("(k p) n -> p k n", p=128)
                for g in range(8):
                    plan.append(("wdn%d" % g, l, [(w_dn[:, :, g * 128:(g + 1) * 128], (NFC, 128), 0)]))
        self.wplan = plan
        self.wslot = [None] * len(plan)

    def issue_next_weight(self):
        i = self.wissue
        if i >= len(self.wplan):
            return False
        kb, I = self.kb, self.I
        tag, l, parts = self.wplan[i]
        if tag == "wsm":
            t = self.wsm[self._wsm_ctr % 2]
            self._wsm_ctr += 1
            kb.dma('pool', t[:, 0:384], I["a_w_qup"][l, 0:128, :], writes=[t])
            kb.dma('pool', t[0:64, 384:768], I["a_w_qup"][l, 128:192, :], writes=[t])
            kb.dma('pool', t[:, 768:1280], I["a_w_kvup"][l], writes=[t])
            for g in range(4):
                cc, hh = g // 2, g % 2
                kb.dma('pool', t[hh * 64:hh * 64 + 64, 1280 + cc * 128 + hh * 64:1280 + cc * 128 + hh * 64 + 64],
                       I["pool_w"][l, g], writes=[t])
            self.wslot[i] = t
        else:
            if self._wr_ctr - self.NW >= self.w_released:
                return False
            t = self.wring[self._wr_ctr % self.NW]
            self._wr_ctr += 1
            for (src, (a, b), off) in parts:
                dst = t[:, off:off + a * b].rearrange("p (a b) -> p a b", a=a)
                kb.dma('pool', dst, src, writes=[t])
            self.wslot[i] = t
        self.wissue += 1
        return True

    def getw(self, tag, l, hold=0):
        ptag, pl, _ = self.wplan[self.wpos]
        assert ptag == tag and pl == l, (ptag, pl, tag, l)
        is_ring = tag != "wsm"
        if is_ring:
            self.w_ringreq += 1
        self.w_released = max(self.w_released, self.w_ringreq - hold - (1 if is_ring else 0))
        while self.wissue < len(self.wplan) and self.wissue <= self.wpos + 6:
            if self.issue_next_weight() is False:
                break
        assert self.wissue > self.wpos, (tag, l)
        t = self.wslot[self.wpos]
        self.wpos += 1
        return t

    def lnfm(self, es, s, C, T, ones, gfn, bfn, gb_t, outs, func=AF.Identity):
        kb = self.kb
        sqs = [kb.sb("ln_sq%d" % i, [128, T], BF16, es=es) for i in range(2)]
        s16s = [kb.sb("ln_s16%d" % i, [128, T], BF16, es=es) for i in range(2)]
        ones16 = self.ones16[ones]
        pm = kb.ps()
        pq = kb.ps()
        for c in range(C):
            sq, s16 = sqs[c % 2], s16s[c % 2]
            kb.op('dve', lambda e: e.tensor_copy(out=s16[:, :], in_=s[:, c, 0:T]), reads=[s], writes=[s16])
            kb.op('act', lambda e: e.activation(out=sq[:, :], in_=s[:, c, 0:T], func=AF.Square), reads=[s], writes=[sq])
            self.mm(pm, pm[:, 0:T], ones16, ones16[:, :], s16, s16[:, :], c == 0, c == C - 1)
            self.mm(pq, pq[:, 0:T], ones16, ones16[:, :], sq, sq[:, :], c == 0, c == C - 1)
        mean = kb.sb("ln_mean", [128, T], F32, es=es)
        rstd = kb.sb("ln_rstd", [128, T], F32, es=es)
        kb.op('act', lambda e: e.copy(out=mean[:], in_=pm[:, 0:T]), reads=[pm], writes=[mean])
        kb.op('dve', lambda e: e.tensor_tensor(out=rstd[:], in0=mean[:], in1=mean[:], op=ALU.mult), reads=[mean], writes=[rstd])
        kb.op('dve', lambda e: e.tensor_tensor(out=rstd[:], in0=pq[:, 0:T], in1=rstd[:], op=ALU.subtract), reads=[pq, rstd], writes=[rstd])
        kb.op('dve', lambda e: e.tensor_scalar(out=rstd[:], in0=rstd[:], scalar1=0.0, scalar2=LN_EPS, op0=ALU.max, op1=ALU.add),
              reads=[rstd], writes=[rstd])
        kb.op('act', lambda e: e.activation(out=rstd[:], in_=rstd[:], func=AF.Sqrt), reads=[rstd], writes=[rstd])
        kb.op('dve', lambda e: e.reciprocal(out=rstd[:], in_=rstd[:]), reads=[rstd], writes=[rstd])
        for c in range(C):
            sc = type(s)(s.t)
            kb.op('dve', lambda e: e.tensor_tensor(out=s[:, c, 0:T], in0=s[:, c, 0:T], in1=mean[:], op=ALU.subtract),
                  reads=[s, mean], writes=[sc])
            kb.op('dve', lambda e: e.tensor_tensor(out=s[:, c, 0:T], in0=s[:, c, 0:T], in1=rstd[:], op=ALU.mult),
                  reads=[sc, rstd], writes=[sc])
            for (ot, ofn) in outs:
                kb.op('act', lambda e: e.activation(out=ofn(c), in_=s[:, c, 0:T], func=func, bias=bfn(c), scale=gfn(c)),
                      reads=[sc, gb_t], writes=[ot])

    def block(self, kind, j):
        cfg, kb, I, O = self.cfg, self.kb, self.I, self.O
        L = cfg.L
        if kind == 'p':
            Tn = cfg.TB
            P = j * cfg.TB
            xin = I["xp"][P:P + Tn, :]
            rope = I["rope_p"][P:P + Tn, :]
            pre = "p"
            row0 = P
            ktop = cfg.KP
        else:
            Tn = cfg.TS
            P = cfg.PAST
            xin = I["xs"]
            rope = I["rope_s"]
            pre = "s"
            row0 = 0
            ktop = cfg.KS
        self.kind, self.Tn, self.P, self.pre, self.row0, self.ktop, self.bj = kind, Tn, P, pre, row0, ktop, j
        qts = [(q0, min(128, Tn - q0)) for q0 in range(0, Tn, 128)]
        self.qts = qts
        if kind == 'p':
            self.ktiles = [('o', t * 128, 128, t * 128) for t in range((P + Tn) // 128)]
        else:
            self.ktiles = [('c', t * 128, 128, t * 128) for t in range(P // 128)] + [('o', 0, Tn, P)]
        self.last_block = (kind == 's') or (j == cfg.NB - 1)
        x, xT = self.x, self.xT
        with ExitStack() as pes:
            for qi, (q0, r) in enumerate(qts):
                kb.dma('sp', self.ropet[:r, qi, :], rope[q0:q0 + r, :], writes=[self.ropet])
            xraw = kb.sb("xraw", [128, 8, Tn], F32, es=pes)
            xtoks = [kb.sb("xtok%d" % i, [128, D], F32, es=pes) for i in range(2)]
            for qi, (q0, r) in enumerate(qts):
                xtok = xtoks[qi % 2]
                kb.dma('sp', xtok[:r, :], xin[q0:q0 + r, :], writes=[xtok])
                for half in range(2):
                    ps = kb.ps()
                    for c in range(4):
                        cc = half * 4 + c
                        self.tr(ps, ps[:, c * 128:c * 128 + r], xtok, xtok[:r, cc * 128:(cc + 1) * 128], r, bf=False)
                    kb.op('act', lambda e, half=half, ps=ps, q0=q0, r=r: e.copy(
                        out=xraw[:, half * 4:half * 4 + 4, q0:q0 + r],
                        in_=ps[:, :].rearrange("p (c t) -> p c t", c=4)[:, :, 0:r]), reads=[ps], writes=[xraw])
            self.lnfm(pes, xraw, 8, Tn, self.ones1024,
                      lambda c: self.pv_lnin[:, 0, c:c + 1], lambda c: self.pv_lnin[:, 1, c:c + 1], self.pv_lnin,
                      [(x, lambda c: x[:, c, 0:Tn]), (xT, lambda c: xT[:, c, 0:Tn])])
            kb.barrier()
        for l in range(L):
            self.layer(l)

    def layer(self, l):
        cfg, kb, I, O = self.cfg, self.kb, self.I, self.O
        kind, Tn, P, pre, row0, qts = self.kind, self.Tn, self.P, self.pre, self.row0, self.qts
        x, xT = self.x, self.xT
        NQ = len(qts)
        wsm = self.getw("wsm", l)
        wqup = lambda kc, ksz: wsm[0:ksz, kc * 384:(kc + 1) * 384]
        with ExitStack() as mes:
            brT = kb.sb("brT", [128, 8, Tn], BF16, es=mes)
            qaT = kb.sb("qaT", [96, 4, Tn], BF16, es=mes)
            bqT = kb.sb("bqT", [128, 2, Tn], BF16, es=mes)
            qiT = kb.sb("qiT", [32, 4, Tn], BF16, es=mes)
            wqs = kb.sb("wqs", [128, NQ, 4], F32, es=mes)
            wn = kb.sb("wn", [128, 4, 96], BF16, es=mes)
            wv = kb.sb("wv", [128, 256], BF16, es=mes)
            kvv = wsm[:, 768:1280].rearrange("p (h c) -> p h c", h=4)
            kb.op('dve', lambda e: e.memset(wn[:], 0.0), writes=[wn])
            kb.op('dve', lambda e: e.tensor_copy(out=wn[:, :, 0:64], in_=kvv[:, :, 0:64]), reads=[wsm, wn], writes=[wn])
            kb.op('dve', lambda e: e.tensor_copy(out=wv[:, :].rearrange("p (h c) -> p h c", h=4), in_=kvv[:, :, 64:128]),
                  reads=[wsm], writes=[wv])
            self.wn, self.wv = wn, wv
            odeps = {nm: T(None) for nm in ("ckv", "krope", "bk", "bv", "kidx")}
            wa0 = self.getw("wtok_a0", l)
            wa1 = self.getw("wtok_a1", l, hold=1)
            wb = self.getw("wtok_b", l, hold=2)
            wa03 = wa0[:, 0:8 * 352].rearrange("p (k n) -> p k n", k=8)
            wa13 = wa1[:, 0:8 * 512].rearrange("p (k n) -> p k n", k=8)
            wb3 = wb[:, 0:8 * 420].rearrange("p (k n) -> p k n", k=8)
            with ExitStack() as pes:
                NQ_ = len(qts)
                rp = self.ropet
                Bf = []
                for i in range(NQ_):
                    sfx = str(i)
                    Bf.append(dict(
                        toka=kb.sb("toka" + sfx, [128, 352], F32, es=pes), tokb=kb.sb("tokb" + sfx, [128, 512], F32, es=pes),
                        tokc=kb.sb("tokc" + sfx, [128, 420], F32, es=pes), rows=kb.sb("rowsA" + sfx, [128, 160], F32, es=pes),
                        ss=kb.sb("ss" + sfx, [128, 4], F32, es=pes), junk=kb.sb("junk" + sfx, [128, 192], F32, es=pes),
                        cqn=kb.sb("cqn" + sfx, [128, 192], BF16, es=pes), bq16=kb.sb("bq16" + sfx, [128, 384], BF16, es=pes),
                        cqT=kb.sb("cqT" + sfx, [128, 2, 128], BF16, es=pes), qa16=kb.sb("qa16" + sfx, [128, 4, 96], BF16, es=pes),
                        rt1=kb.sb("rt1" + sfx, [128, 4, 32], F32, es=pes), rt2=kb.sb("rt2" + sfx, [128, 4, 32], F32, es=pes)))
                for qi, (q0, r) in enumerate(qts):
                    b_ = Bf[qi]
                    tpa, tpb, tpc = kb.ps(), kb.ps(), kb.ps()
                    for (ps, wt, w3, c0, c1) in ((tpa, wa0, wa03, 0, 352), (tpb, wa1, wa13, 0, 512), (tpc, wb, wb3, 0, 420)):
                        for k in range(8):
                            self.mm(ps, ps[:r, 0:c1 - c0], xT, xT[:, k, q0:q0 + r], wt, w3[:, k, c0:c1], k == 0, k == 7, inc=(k == 7))
                    kb.op('act', lambda e: e.copy(out=b_['toka'][:r, :], in_=tpa[:r, 0:352]), reads=[tpa], writes=[b_['toka']])
                    kb.op('dve', lambda e: e.tensor_copy(out=b_['tokb'][:r, :], in_=tpb[:r, :]), reads=[tpb], writes=[b_['tokb']])
                    kb.op('act', lambda e: e.copy(out=b_['tokc'][:r, :], in_=tpc[:r, 0:420]), reads=[tpc], writes=[b_['tokc']])
                pc = self.phase_poolconv(l, brT)
                next(pc, None)
                for qi, (q0, r) in enumerate(qts):
                    b_ = Bf[qi]
                    ss, junk, toka = b_['ss'], b_['junk'], b_['toka']
                    kb.op('dve', lambda e: e.memset(ss[:], 0.0), writes=[ss])
                    kb.op('act', lambda e: e.activation(out=junk[:r, 0:192], in_=toka[:r, 0:192], func=AF.Square,
                                                        scale=192 ** -0.5, accum_out=ss[:r, 0:1]), reads=[toka, ss], writes=[junk, ss])
                    kb.op('act', lambda e: e.activation(out=junk[:r, 0:128], in_=toka[:r, 192:320], func=AF.Square,
                                                        scale=128 ** -0.5, accum_out=ss[:r, 1:2]), reads=[toka, ss], writes=[junk, ss])
                for qi, (q0, r) in enumerate(qts):
                    ss = Bf[qi]['ss']
                    kb.op('dve', lambda e: e.tensor_scalar(out=ss[:r, 2:4], in0=ss[:r, 0:2], scalar1=LN_EPS, scalar2=1.0, op0=ALU.add, op1=ALU.mult),
                          reads=[ss], writes=[ss])
                for qi, (q0, r) in enumerate(qts):
                    ss = Bf[qi]['ss']
                    kb.op('act', lambda e: e.activation(out=ss[:r, 2:4], in_=ss[:r, 2:4], func=AF.Sqrt), reads=[ss], writes=[ss])
                for qi, (q0, r) in enumerate(qts):
                    ss = Bf[qi]['ss']
                    kb.op('dve', lambda e: e.reciprocal(out=ss[:r, 2:4], in_=ss[:r, 2:4]), reads=[ss], writes=[ss])
                next(pc, None)
                for qi, (q0, r) in enumerate(qts):
                    b_ = Bf[qi]
                    ss, toka, tokb, tokc, rows, cqn, rt1, rt2, bq16 = (b_[k] for k in ('ss', 'toka', 'tokb', 'tokc', 'rows', 'cqn', 'rt1', 'rt2', 'bq16'))
                    kb.op('dve', lambda e: e.scalar_tensor_tensor(out=cqn[:r, :], in0=toka[:r, 0:192], scalar=ss[:r, 2:3],
                                                                  in1=self.qn[:r, l, 0:192], op0=ALU.mult, op1=ALU.mult),
                          reads=[toka, ss, self.qn], writes=[cqn])
                    kb.op('dve', lambda e: e.scalar_tensor_tensor(out=rows[:r, 0:128], in0=toka[:r, 192:320], scalar=ss[:r, 3:4],
                                                                  in1=self.qn[:r, l, 192:320], op0=ALU.mult, op1=ALU.mult),
                          reads=[toka, ss, self.qn], writes=[rows])
                    kb.op('dve', lambda e: e.tensor_tensor(out=rt1[:r, 0, :], in0=toka[:r, 320:352], in1=rp[:r, qi, 0:32], op=ALU.mult),
                          reads=[toka, rp], writes=[rt1])
                    kb.op('dve', lambda e: e.tensor_tensor(out=rt2[:r, 0, 0:16], in0=toka[:r, 336:352], in1=rp[:r, qi, 32:48], op=ALU.mult),
                          reads=[toka, rp], writes=[rt2])
                    kb.op('dve', lambda e: e.tensor_tensor(out=rt2[:r, 0, 16:32], in0=toka[:r, 320:336], in1=rp[:r, qi, 48:64], op=ALU.mult),
                          reads=[toka, rp, rt2], writes=[rt2])
                    kb.op('dve', lambda e: e.tensor_tensor(out=rows[:r, 128:160], in0=rt1[:r, 0, :], in1=rt2[:r, 0, :], op=ALU.add),
                          reads=[rt1, rt2, rows], writes=[rows])
                    rr = slice(row0 + q0, row0 + q0 + r)
                    kb.dma('sp', O[pre + "_ckv"][l, rr, :], rows[:r, 0:128], reads=[rows], writes=[odeps["ckv"]])
                    kb.dma('sp', O[pre + "_krope"][l, rr, :], rows[:r, 128:160], reads=[rows], writes=[odeps["krope"]])
                    kb.dma('sp', O[pre + "_bk"][l, rr, :], tokb[:r, 256:512], reads=[tokb], writes=[odeps["bk"]])
                    kb.dma('sp', O[pre + "_bv"][l, rr, :], tokc[:r, 0:256], reads=[tokc], writes=[odeps["bv"]])
                    kb.dma('sp', O[pre + "_kidx"][l, rr, :], tokc[:r, 384:416], reads=[tokc], writes=[odeps["kidx"]])
                    kb.op('dve', lambda e: e.tensor_copy(out=bq16[:r, 0:256], in_=tokb[:r, 0:256]), reads=[tokb], writes=[bq16])
                    kb.op('dve', lambda e: e.tensor_copy(out=bq16[:r, 256:384], in_=tokc[:r, 256:384]), reads=[tokc, bq16], writes=[bq16])
                    kb.op('dve', lambda e: e.tensor_scalar(out=wqs[:r, qi, :], in0=tokc[:r, 416:420], scalar1=IDX_SCALE, scalar2=0.0,
                                                           op0=ALU.mult, op1=ALU.add), reads=[tokc], writes=[wqs])
                next(pc, None)
                pts = []
                for qi, (q0, r) in enumerate(qts):
                    b_ = Bf[qi]
                    bq16, cqn = b_['bq16'], b_['cqn']
                    pt = kb.ps()
                    ptb = pt[:, :].bitcast(BF16)
                    for pr in range(2):
                        self.tr(pt, ptb[:, pr * 128:pr * 128 + r], bq16, bq16[:r, pr * 128:(pr + 1) * 128], r)
                    for h in range(4):
                        self.tr(pt, ptb[0:32, 256 + h * 128:256 + h * 128 + r], bq16, bq16[:r, 256 + h * 32:256 + (h + 1) * 32], r)
                    self.tr(pt, ptb[:, 768:768 + r], cqn, cqn[:r, 0:128], r)
                    self.tr(pt, ptb[0:64, 896:896 + r], cqn, cqn[:r, 128:192], r)
                    pts.append((pt, ptb))
                for qi, (q0, r) in enumerate(qts):
                    pt, ptb = pts[qi]
                    cqT = Bf[qi]['cqT']
                    kb.op('act', lambda e: e.copy(out=bqT[:, :, q0:q0 + r],
                                                  in_=ptb[:, 0:256].rearrange("p (a b) -> p a b", a=2)[:, :, 0:r]),
                          reads=[pt], writes=[bqT])
                    kb.op('dve', lambda e: e.tensor_copy(out=qiT[:, :, q0:q0 + r],
                                                         in_=ptb[0:32, 256:768].rearrange("p (a b) -> p a b", a=4)[:, :, 0:r]),
                          reads=[pt], writes=[qiT])
                    kb.op('act', lambda e: e.copy(out=cqT[:, 0, 0:r], in_=ptb[:, 768:768 + r]), reads=[pt], writes=[cqT])
                    kb.op('dve', lambda e: e.tensor_copy(out=cqT[0:64, 1, 0:r], in_=ptb[0:64, 896:896 + r]), reads=[pt, cqT], writes=[cqT])
                next(pc, None)
                pqs = []
                for qi, (q0, r) in enumerate(qts):
                    cqT = Bf[qi]['cqT']
                    pq = kb.ps()
                    self.mm(pq, pq[:r, 0:384], cqT, cqT[:, 0, 0:r], wsm, wqup(0, 128), True, False)
                    self.mm(pq, pq[:r, 0:384], cqT, cqT[0:64, 1, 0:r], wsm, wqup(1, 64), False, True)
                    pqs.append(pq)
                for qi, (q0, r) in enumerate(qts):
                    b_ = Bf[qi]
                    qa16, rt1, rt2 = b_['qa16'], b_['rt1'], b_['rt2']
                    pq = pqs[qi]
                    pq3 = pq[:, 0:384].rearrange("p (h c) -> p h c", h=4)
                    kb.op('act', lambda e: e.copy(out=qa16[:r, :, 0:64], in_=pq3[:r, :, 0:64]), reads=[pq], writes=[qa16])
                    for h in range(4):
                        kb.op('dve', lambda e: e.tensor_tensor(out=rt1[:r, h, :], in0=pq3[:r, h, 64:96], in1=rp[:r, qi, 0:32], op=ALU.mult),
                              reads=[pq, rp, rt1], writes=[rt1])
                        kb.op('dve', lambda e: e.tensor_tensor(out=rt2[:r, h, 0:16], in0=pq3[:r, h, 80:96], in1=rp[:r, qi, 32:48], op=ALU.mult),
                              reads=[pq, rp, rt2], writes=[rt2])
                        kb.op('dve', lambda e: e.tensor_tensor(out=rt2[:r, h, 16:32], in0=pq3[:r, h, 64:80], in1=rp[:r, qi, 48:64], op=ALU.mult),
                              reads=[pq, rp, rt2], writes=[rt2])
                    kb.op('dve', lambda e: e.tensor_tensor(out=qa16[:r, :, 64:96], in0=rt1[:r, :, :], in1=rt2[:r, :, :], op=ALU.add),
                          reads=[rt1, rt2, qa16], writes=[qa16])
                next(pc, None)
                pts = []
                for qi, (q0, r) in enumerate(qts):
                    qa16 = Bf[qi]['qa16']
                    pt3 = kb.ps()
                    pt3b = pt3[:, :].bitcast(BF16)
                    for h in range(4):
                        self.tr(pt3, pt3b[0:96, h * 128:h * 128 + r], qa16, qa16[:r, h, :], r)
                    pts.append((pt3, pt3b))
                for qi, (q0, r) in enumerate(qts):
                    pt3, pt3b = pts[qi]
                    kb.op('act', lambda e: e.copy(out=qaT[:, :, q0:q0 + r],
                                                  in_=pt3b[0:96, 0:512].rearrange("p (a b) -> p a b", a=4)[:, :, 0:r]),
                          reads=[pt3], writes=[qaT])
                for _ in pc:
                    pass
                kb.barrier()
            self.odeps = odeps
            kb.ps_groups.update({'acc': [0, 1, 2, 3], 'qk': [4, 5], 'misc': [6, 7]})
            with ExitStack() as des:
                st = self.dsa_open(l, des, qiT, wqs)
                def chain():
                    for t0 in range(0, len(qts), 2):
                        yield from self.dsa_pass1(l, st, t0)
                self.phase_mla(l, brT, qaT, bg=chain())
                self.dsa_rest(l, des, st, brT, bqT, True)
                kb.barrier()
            self.phase_merge(l, brT)
        self.phase_ffn(l)

    def phase_poolconv(self, l, brT):
        cfg, kb, I, O = self.cfg, self.kb, self.I, self.O
        kind, Tn, pre = self.kind, self.Tn, self.pre
        xT = self.xT
        with ExitStack() as pes:
            wpcs = [self.getw("wpc0", l), self.getw("wpc1", l, hold=1)]
            wpc3 = [w[:, 0:8 * 384].rearrange("p (k n) -> p k n", k=8) for w in wpcs]
            wsm = self.cur_wsm(l)
            pext = kb.sb("pext", [128, 2, 15 + Tn], F32, es=pes)
            cext = kb.sb("cext", [128, 2, 30 + Tn], F32, es=pes)
            sg = kb.sb("sgl", [128, 2, Tn], F32, es=pes)
            pcar, ccar = self.pcar[kind], self.ccar[kind]
            kb.op('dve', lambda e: e.tensor_copy(out=pext[:, :, 0:15], in_=pcar[:, l, :, :]), reads=[pcar], writes=[pext])
            kb.op('dve', lambda e: e.tensor_copy(out=cext[:, :, 0:30], in_=ccar[:, l, :, :]), reads=[ccar], writes=[cext])
            pss = []
            for c in range(6):
                ps = kb.ps()
                for k in range(8):
                    self.mm(ps, ps[:, 0:Tn], wpcs[c // 3], wpc3[c // 3][:, k, (c % 3) * 128:(c % 3 + 1) * 128], xT, xT[:, k, 0:Tn], k == 0, k == 7, inc=(k == 7))
                pss.append(ps)
                if c < 2:
                    kb.op('act', lambda e, c=c, ps=ps: e.copy(out=pext[:, c, 15:15 + Tn], in_=ps[:, 0:Tn]), reads=[ps, pext], writes=[pext])
                elif c >= 4:
                    kb.op('act', lambda e, c=c, ps=ps: e.activation(out=sg[:, c - 4, :], in_=ps[:, 0:Tn], func=AF.Sigmoid),
                          reads=[ps, sg], writes=[sg])
                    kb.op('dve', lambda e, c=c: e.tensor_tensor(out=cext[:, c - 4, 30:30 + Tn], in0=pss[c - 2][:, 0:Tn], in1=sg[:, c - 4, :],
                                                                op=ALU.mult), reads=[pss[c - 2], sg, cext], writes=[cext])
            yield
            kb.op('dve', lambda e: e.tensor_copy(out=pcar[:, l, :, :], in_=pext[:, :, Tn:Tn + 15]), reads=[pext, pcar], writes=[pcar])
            kb.op('dve', lambda e: e.tensor_copy(out=ccar[:, l, :, :], in_=cext[:, :, Tn:Tn + 30]), reads=[cext, ccar], writes=[ccar])
            if self.last_block:
                st = kb.sb("st_out", [32, 512], F32, es=pes)
                ps = kb.ps()
                for c in range(2):
                    self.tr(ps, ps[0:15, c * 128:(c + 1) * 128], pcar, pcar[:, l, c, :], 128, bf=False)
                    self.tr(ps, ps[0:30, 256 + c * 128:256 + (c + 1) * 128], ccar, ccar[:, l, c, :], 128, bf=False)
                kb.op('act', lambda e: e.copy(out=st[0:30, :], in_=ps[0:30, :]), reads=[ps], writes=[st])
                kb.dma('sp', O[pre + "_pool"][l], st[0:15, 0:256], reads=[st])
                kb.dma('sp', O[pre + "_conv"][l], st[0:30, 256:512], reads=[st])
            yield
            bA = kb.sb("paA", [128, 2, 15 + Tn], F32, es=pes)
            bB = kb.sb("paB", [128, 2, 15 + Tn], F32, es=pes)
            n = 15 + Tn
            pl = kb.sb("pl", [128, 2, Tn], F32, es=pes)
            pl16 = kb.sb("pl16", [128, 2, Tn], BF16, es=pes)
            first = (kind == 'p' and self.bj == 0)
            if first:
                self.icnt0 = kb.sb("icnt0", [128, 2, cfg.TB], F32, es=pes)
                kb.dma('sp', self.icnt0[:], I["icnt0"].rearrange("p (c t) -> p c t", c=2), writes=[self.icnt0])

            def pooled(g, src, wdt):
                c, lo = g // 2, (g % 2) * 64
                if first:
                    kb.op('dve', lambda e: e.tensor_tensor(out=pl[lo:lo + 64, c, :], in0=src[lo:lo + 64, c, 15:15 + Tn],
                                                           in1=self.icnt0[lo:lo + 64, c, 0:Tn], op=ALU.mult),
                          reads=[src, self.icnt0, pl], writes=[pl])
                    kb.op('dve', lambda e: e.tensor_tensor(out=pl[lo:lo + 64, c, :], in0=pl[lo:lo + 64, c, :],
                                                           in1=pext[lo:lo + 64, c, 15:15 + Tn], op=ALU.subtract),
                          reads=[pl, pext], writes=[pl])
                else:
                    kb.op('dve', lambda e: e.scalar_tensor_tensor(
                        out=pl[lo:lo + 64, c, :], in0=src[lo:lo + 64, c, 15:15 + Tn], scalar=1.0 / wdt,
                        in1=pext[lo:lo + 64, c, 15:15 + Tn], op0=ALU.mult, op1=ALU.subtract), reads=[src, pext, pl], writes=[pl])

            kb.op('dve', lambda e: e.tensor_tensor(out=bA[:, :, 1:n], in0=pext[:, :, 1:n], in1=pext[:, :, 0:n - 1], op=ALU.add),
                  reads=[pext], writes=[bA])
            kb.op('dve', lambda e: e.tensor_tensor(out=bB[:, :, 3:n], in0=bA[:, :, 3:n], in1=bA[:, :, 1:n - 2], op=ALU.add),
                  reads=[bA], writes=[bB])
            pooled(0, bA, 2)
            pooled(1, bB, 4)
            yield
            kb.op('dve', lambda e: e.tensor_tensor(out=bA[:, :, 7:n], in0=bB[:, :, 7:n], in1=bB[:, :, 3:n - 4], op=ALU.add),
                  reads=[bB, bA], writes=[bA])
            kb.op('dve', lambda e: e.tensor_tensor(out=bB[:, :, 15:n], in0=bA[:, :, 15:n], in1=bA[:, :, 7:n - 8], op=ALU.add),
                  reads=[bA, bB], writes=[bB])
            pooled(2, bA, 8)
            pooled(3, bB, 16)
            yield
            kb.op('dve', lambda e: e.tensor_copy(out=pl16[:], in_=pl[:]), reads=[pl], writes=[pl16])
            pvc = self.pv_c
            for c in range(2):
                ps = kb.ps()
                self.mm(ps, ps[:, 0:Tn], wsm, wsm[:, 1280 + c * 128:1280 + (c + 1) * 128], pl16, pl16[:, c, :], True, True)
                kb.op('act', lambda e, c=c, ps=ps: e.activation(out=brT[:, 4 + c, 0:Tn], in_=ps[:, 0:Tn], func=AF.Copy,
                                                                scale=pvc[:, l, c, 34:35]), reads=[ps, pvc, brT], writes=[brT])
            yield
            cacc = kb.sb("cacc", [128, 2, Tn], F32, es=pes)
            cext16 = kb.sb("cext16", [128, 2, 30 + Tn], BF16, es=pes)
            kb.op('act', lambda e: e.copy(out=cext16[:], in_=cext[:]), reads=[cext], writes=[cext16])
            for c in range(2):
                dg = kb.sb("dg%d" % c, [128, 31, 128], BF16, es=pes)
                for k in range(31):
                    kb.op('dve', lambda e: e.tensor_scalar(out=dg[:, k, :], in0=self.ident16[:, :], scalar1=pvc[:, l, c, k:k + 1],
                                                            scalar2=0.0, op0=ALU.mult, op1=ALU.add),
                          reads=[self.ident16, pvc, dg], writes=[dg])
                yield
                ps = kb.ps()
                for k in range(31):
                    self.mm(ps, ps[:, 0:Tn], dg, dg[:, k, :], cext16, cext16[:, c, k:k + Tn], k == 0, k == 30, inc=(k == 30))
                kb.op('act', lambda e: e.activation(out=cacc[:, c, :], in_=ps[:, 0:Tn], func=AF.Identity, bias=pvc[:, l, c, 31:32]),
                      reads=[ps, pvc, cacc], writes=[cacc])
            yield
            self.lnfm(pes, cacc, 2, Tn, self.ones256, lambda c: pvc[:, l, c, 32:33], lambda c: pvc[:, l, c, 33:34], pvc,
                      [(brT, lambda c: brT[:, 6 + c, 0:Tn])], func=AF.Silu)
            kb.barrier()

    def cur_wsm(self, l):
        for i in range(self.wpos - 1, -1, -1):
            if self.wplan[i][0] == "wsm":
                return self.wslot[i]
        raise AssertionError

    def key_supers(self):
        sups, cur = [], []
        for kt in self.ktiles:
            if kt[2] < 128 or (cur and cur[0][0] != kt[0]):
                if cur:
                    sups.append(cur)
                    cur = []
            cur.append(kt)
            if len(cur) == 4 or kt[2] < 128:
                sups.append(cur)
                cur = []
        if cur:
            sups.append(cur)
        return sups

    def ksrc(self, name, l, sup):
        I, O = self.I, self.O
        kindsrc, r0, kr, _ = sup[0]
        nrows = sum(t[2] for t in sup)
        if kindsrc == 'c':
            ap = I["c_" + name][l, r0:r0 + nrows, :]
            dep = []
        else:
            ap = O[self.pre + "_" + name][l, r0:r0 + nrows, :]
            dep = [self.odeps[name]]
        if kr == 128:
            return ap.rearrange("(t p) f -> p t f", p=128), dep, 128
        return ap.rearrange("(t p) f -> p t f", t=1), dep, kr

    def vis(self, abs_pos, kr):
        if self.kind == 's':
            return 0, None
        a = (abs_pos - self.P) // 128
        if a < 0:
            return 0, None
        return a * 128, a

    def phase_mla(self, l, brT, qaT, bg=None):
        cfg, kb = self.cfg, self.kb
        Tn, qts = self.Tn, self.qts
        NQ = len(qts)
        wn, wv = self.wn, self.wv
        with ExitStack() as pes:
            accs = [kb.ps('acc') for _ in range(4)]
            first_acc = [True] * 4
            kin16 = [kb.sb("kinA16_%d" % i, [128, 4, 160], BF16, es=pes) for i in range(2)]
            kin2 = [T(k.t) for k in kin16]
            ckvT = [kb.sb("ckvT%d" % i, [128, 512], BF16, es=pes) for i in range(2)]
            krT = [kb.sb("krT%d" % i, [32, 512], BF16, es=pes) for i in range(2)]
            KA = [kb.sb("KA%d" % i, [96, 4, 512], BF16, es=pes) for i in range(2)]
            VA = [kb.sb("VA%d" % i, [128, 4, 4, 65], BF16, es=pes) for i in range(2)]
            PT = [kb.sb("PTa%d" % i, [128, 512], BF16, es=pes) for i in range(4)]
            for v in VA:
                kb.op('dve', lambda e, v=v: e.memset(v[:, :, :, 64:65], 1.0), writes=[v])
            pti = 0
            pend = []
            for si, sup in enumerate(self.key_supers()):
                b = si % 2
                nt = len(sup)
                a1, d1, kr = self.ksrc("ckv", l, sup)
                a2, d2, _ = self.ksrc("krope", l, sup)
                nk = sum(t[2] for t in sup)
                kb.dma('pool', kin16[b][:kr, 0:nt, 0:128], a1, reads=d1, writes=[kin16[b]])
                kb.dma('pool', kin16[b][:kr, 0:nt, 128:160], a2, reads=d2, writes=[kin2[b]])
                pt = kb.ps('misc')
                ptb = pt[:, :].bitcast(BF16)
                for t in range(nt):
                    self.tr(pt, ptb[:, t * 128:t * 128 + kr], kin16[b], kin16[b][:kr, t, 0:128], kr)
                    self.tr(pt, ptb[0:32, 512 + t * 128:512 + t * 128 + kr], kin2[b], kin16[b][:kr, t, 128:160], kr)
                kb.op('act', lambda e: e.copy(out=ckvT[b][:, 0:nk], in_=ptb[:, 0:nk]), reads=[pt], writes=[ckvT[b]])
                kb.op('act', lambda e: e.copy(out=krT[b][:, 0:nk], in_=ptb[0:32, 512:512 + nk]), reads=[pt], writes=[krT[b]])
                for h in range(4):
                    pk = kb.ps('misc')
                    self.mm(pk, pk[0:96, 0:nk], wn, wn[:, h, :], ckvT[b], ckvT[b][:, 0:nk], True, False)
                    self.mm(pk, pk[0:96, 0:nk], self.esel, self.esel[:, :], krT[b], krT[b][:, 0:nk], False, True)
                    kb.op('act', lambda e, h=h, pk=pk: e.copy(out=KA[b][:, h, 0:nk], in_=pk[0:96, 0:nk]), reads=[pk, KA[b]], writes=[KA[b]])
                for t in range(nt):
                    pvp = kb.ps('misc')
                    self.mm(pvp, pvp[:kr, 0:256], ckvT[b], ckvT[b][:, t * 128:t * 128 + kr], wv, wv[:, :], True, True)
                    kb.op('act', lambda e, t=t, pvp=pvp: e.copy(out=VA[b][:kr, t, :, 0:64],
                                                                in_=pvp[:kr, 0:256].rearrange("p (h c) -> p h c", h=4)),
                          reads=[pvp, VA[b]], writes=[VA[b]])
                for t, (_, _, kr_t, apos) in enumerate(sup):
                    qlo, diag = self.vis(apos, kr_t)
                    if qlo >= Tn:
                        continue
                    nq = Tn - qlo
                    for h in range(4):
                        psq = kb.ps('qk')
                        self.mm(psq, psq[:kr_t, 0:nq], KA[b], KA[b][:, h, t * 128:t * 128 + kr_t], qaT, qaT[:, h, qlo:Tn], True, True)
                        p = PT[pti % len(PT)]
                        pti += 1
                        kb.op('act', lambda e: e.activation(out=p[:kr_t, 0:nq], in_=psq[:kr_t, 0:nq], func=AF.Exp, scale=A_SCALE),
                              reads=[psq], writes=[p])
                        if diag is not None:
                            kb.op('pool', lambda e: e.memset(p[64:128, 0:64], 0.0), reads=[p], writes=[p])

                        def pv(p=p, h=h, t=t, b=b, kr_t=kr_t, qlo=qlo):
                            vq = [(qi, q0, r) for qi, (q0, r) in enumerate(qts) if q0 >= qlo]
                            for j, (qi, q0, r) in enumerate(vq):
                                self.mm(accs[h], accs[h][:r, qi * 65:qi * 65 + 65], p, p[:kr_t, q0 - qlo:q0 - qlo + r],
                                        VA[b], VA[b][:kr_t, t, h, :], first_acc[h], True, inc=(j == len(vq) - 1), skip_group_check=True)
                                first_acc[h] = False
                        pend.append(pv)
                        if len(pend) > 1:
                            pend.pop(0)()
                        if bg is not None:
                            next(bg, None)
            while pend:
                pend.pop(0)()
            if bg is not None:
                for _ in bg:
                    pass
            self.attn_finish(pes, accs, brT, 0)
            kb.barrier()

    def attn_finish(self, pes, accs, brT, chunk0, qsel=None):
        kb = self.kb
        qts = self.qts if qsel is None else qsel
        recs = [kb.sb("rec%d" % i, [128, 4], F32, es=pes) for i in range(2)]
        obs = [kb.sb("ob%d" % i, [128, 256], BF16, es=pes) for i in range(2)]
        for gi, (q0, r) in enumerate(qts):
            rec, ob = recs[gi % 2], obs[gi % 2]
            for h in range(4):
                kb.op('dve', lambda e, h=h: e.reciprocal(out=rec[:r, h:h + 1], in_=accs[h][:r, gi * 65 + 64:gi * 65 + 65]),
                      reads=[accs[h], rec], writes=[rec])
                kb.op('act', lambda e, h=h: e.activation(out=ob[:r, h * 64:(h + 1) * 64], in_=accs[h][:r, gi * 65:gi * 65 + 64],
                                                         func=AF.Copy, scale=rec[:r, h:h + 1]), reads=[accs[h], rec, ob], writes=[ob])
            pt = kb.ps('misc')
            ptb = pt[:, :].bitcast(BF16)
            for c in range(2):
                self.tr(pt, ptb[:, c * 128:c * 128 + r], ob, ob[:r, c * 128:(c + 1) * 128], r)
            kb.op('act', lambda e: e.copy(out=brT[:, chunk0:chunk0 + 2, q0:q0 + r],
                                          in_=ptb[:, 0:256].rearrange("p (a b) -> p a b", a=2)[:, :, 0:r]),
                  reads=[pt, brT], writes=[brT])

    def dsa_open(self, l, pes, qiT, wqs):
        cfg, kb = self.cfg, self.kb
        Tn, qts, P, kind = self.Tn, self.qts, self.P, self.kind
        NK = sum(t[2] for t in self.ktiles)
        sups = self.key_supers()
        st = {}
        kidxT = kb.sb("kidxT", [32, NK], BF16, es=pes)
        kiin16 = [kb.sb("kiin16_%d" % i, [128, 4, 32], BF16, es=pes) for i in range(2)]
        k0 = 0
        koffs = []
        for si, sup in enumerate(sups):
            b = si % 2
            nt = len(sup)
            a1, d1, kr = self.ksrc("kidx", l, sup)
            nk = sum(t[2] for t in sup)
            kb.dma('pool', kiin16[b][:kr, 0:nt, :], a1, reads=d1, writes=[kiin16[b]])
            pt = kb.ps('misc')
            ptb = pt[:, :].bitcast(BF16)
            for t in range(nt):
                self.tr(pt, ptb[0:32, t * 128:t * 128 + kr], kiin16[b], kiin16[b][:kr, t, :], kr)
            kb.op('act', lambda e: e.copy(out=kidxT[:, k0:k0 + nk], in_=ptb[0:32, 0:nk]), reads=[pt, kidxT], writes=[kidxT])
            koffs.append(k0)
            k0 += nk
        GQ = cfg.GQ
        iscs = [kb.sb("isc%d" % i, [128, NK], F32, es=pes) for i in range(2)]
        mask = kb.sb("mask", [128, GQ, NK], BF16, es=pes)
        mask1 = T(mask.t)
        mtok = [mask, mask1] + [T(mask.t) for _ in range(max(0, GQ - 2))]
        rl = [kb.sb("rl%d" % i, [128, 512], F32, es=pes) for i in range(2)]
        bss = [kb.sb("bs%d" % i, [128, 8 + NIT], F32, es=pes) for i in range(2)]
        st.update(dict(NK=NK, sups=sups, koffs=koffs, kidxT=kidxT, iscs=iscs, mask=mask, mask1=mask1, rl=rl, bss=bss,
                       qiT=qiT, wqs=wqs, GQ=GQ, rli=0, mtok=mtok))
        return st

    def dsa_pass1(self, l, st, g0):
        cfg, kb = self.cfg, self.kb
        Tn, qts, P, kind = self.Tn, self.qts, self.P, self.kind
        NK, sups, koffs, kidxT, iscs, mask, mask1, rl, bss, qiT, wqs, GQ = (st[k] for k in (
            "NK", "sups", "koffs", "kidxT", "iscs", "mask", "mask1", "rl", "bss", "qiT", "wqs", "GQ"))
        rli = st["rli"]
        grp = qts[g0:g0 + 2]
        mtok = st["mtok"]
        ms0 = g0
        qc0 = grp[0][0]
        qc1 = grp[-1][0] + grp[-1][1]
        nvs = []
        for gi, (q0, r) in enumerate(grp):
            qi = g0 + gi
            isc = iscs[gi % 2]
            nvis = (P + q0 + 128) if kind == 'p' else NK
            nvs.append(nvis)
            for si, sup in enumerate(sups):
                kk0 = koffs[si]
                if kk0 >= nvis:
                    break
                nk = min(sum(t[2] for t in sup), nvis - kk0)
                for h in range(4):
                    psi = kb.ps('qk')
                    self.mm(psi, psi[:r, 0:nk], qiT, qiT[:, h, q0:q0 + r], kidxT, kidxT[:, kk0:kk0 + nk], True, True)
                    if h == 0:
                        kb.op('dve', lambda e: e.tensor_scalar(out=isc[:r, kk0:kk0 + nk], in0=psi[:r, 0:nk], scalar1=0.0,
                                                               scalar2=wqs[:r, qi, 0:1], op0=ALU.max, op1=ALU.mult),
                              reads=[psi, wqs, isc], writes=[isc])
                    else:
                        rr = rl[rli % 2]
                        rli += 1
                        kb.op('act', lambda e: e.activation(out=rr[:r, 0:nk], in_=psi[:r, 0:nk], func=AF.Relu),
                              reads=[psi], writes=[rr])
                        kb.op('dve', lambda e: e.scalar_tensor_tensor(out=isc[:r, kk0:kk0 + nk], in0=rr[:r, 0:nk],
                                                                      scalar=wqs[:r, qi, h:h + 1], in1=isc[:r, kk0:kk0 + nk],
                                                                      op0=ALU.mult, op1=ALU.add),
                              reads=[rr, wqs, isc], writes=[isc])
                yield
            bs = bss[gi % 2]
            kb.op('dve', lambda e: e.memset(bs[:], 0.0), writes=[bs])
            kb.op('dve', lambda e: e.tensor_reduce(out=bs[:r, 0:1], in_=isc[:r, 0:nvis], axis=AX.X, op=ALU.max), reads=[isc, bs], writes=[bs])
            kb.op('dve', lambda e: e.tensor_reduce(out=bs[:r, 1:2], in_=isc[:r, 0:nvis], axis=AX.X, op=ALU.min), reads=[isc, bs], writes=[bs])
            if kind == 'p':
                kb.op('dve', lambda e: e.memset(isc[0:64, nvis - 64:nvis], NEG_BIG), reads=[isc], writes=[isc])
            kb.op('dve', lambda e: e.tensor_tensor(out=bs[:r, 2:3], in0=bs[:r, 0:1], in1=bs[:r, 1:2], op=ALU.subtract), reads=[bs], writes=[bs])
            if gi % 2 == 1:
                kb.op('dve', lambda e: e.tensor_scalar(out=bs[:r, 1:2], in0=bs[:r, 1:2], scalar1=-1.0, scalar2=0.0, op0=ALU.mult, op1=ALU.add),
                      reads=[bs], writes=[bs])
        for it in range(1, NIT + 1):
            f = 2.0 ** -it
            yield
            if len(grp) > 1:
                (q0, r), isc, bs, nvis = grp[1], iscs[1], bss[1], nvs[1]
                kb.op('dve', lambda e: e.scalar_tensor_tensor(out=bs[:r, 3:4], in0=bs[:r, 2:3], scalar=-f, in1=bs[:r, 1:2],
                                                              op0=ALU.mult, op1=ALU.add), reads=[bs], writes=[bs])
                kb.op('act', lambda e: e.activation(out=mask[:r, ms0 + 1, 0:nvis], in_=isc[:r, 0:nvis], func=AF.Sign, bias=bs[:r, 3:4], scale=1.0,
                                                    accum_out=bs[:r, 7 + it:8 + it]), reads=[isc, bs, mtok[ms0 + 1]], writes=[mtok[ms0 + 1], bs])
            (q0, r), isc, bs, nvis = grp[0], iscs[0], bss[0], nvs[0]
            kb.op('dve', lambda e: e.scalar_tensor_tensor(out=bs[:r, 3:4], in0=bs[:r, 2:3], scalar=f, in1=bs[:r, 1:2],
                                                          op0=ALU.mult, op1=ALU.add), reads=[bs], writes=[bs])
            kb.op('dve', lambda e: e.tensor_scalar(out=mask[:r, ms0, 0:nvis], in0=isc[:r, 0:nvis], scalar1=bs[:r, 3:4], scalar2=0.0,
                                                   op0=ALU.is_ge, op1=ALU.add, accum_out=bs[:r, 7 + it:8 + it]),
                  reads=[isc, bs, mtok[ms0]], writes=[mtok[ms0], bs])
            kb.op('dve', lambda e: e.tensor_scalar(out=bs[:r, 4:5], in0=bs[:r, 7 + it:8 + it], scalar1=float(self.ktop) - 0.5,
                                                   scalar2=f, op0=ALU.is_ge, op1=ALU.mult), reads=[bs], writes=[bs])
            kb.op('dve', lambda e: e.scalar_tensor_tensor(out=bs[:r, 1:2], in0=bs[:r, 4:5], scalar=bs[:r, 2:3], in1=bs[:r, 1:2],
                                                          op0=ALU.mult, op1=ALU.add), reads=[bs], writes=[bs])
            if len(grp) > 1:
                (q0, r), isc, bs, nvis = grp[1], iscs[1], bss[1], nvs[1]
                kb.op('dve', lambda e: e.tensor_scalar(out=bs[:r, 4:5], in0=bs[:r, 7 + it:8 + it], scalar1=2.0 * self.ktop - nvis - 0.5,
                                                       scalar2=-f, op0=ALU.is_ge, op1=ALU.mult), reads=[bs], writes=[bs])
                kb.op('dve', lambda e: e.scalar_tensor_tensor(out=bs[:r, 1:2], in0=bs[:r, 4:5], scalar=bs[:r, 2:3], in1=bs[:r, 1:2],
                                                              op0=ALU.mult, op1=ALU.add), reads=[bs], writes=[bs])
        for gi, (q0, r) in enumerate(grp):
            isc, bs, nvis = iscs[gi % 2], bss[gi % 2], nvs[gi]
            mk = mtok[ms0 + gi]
            if gi % 2 == 1:
                kb.op('dve', lambda e: e.tensor_scalar(out=bs[:r, 5:6], in0=bs[:r, 1:2], scalar1=-1.0, scalar2=0.0, op0=ALU.mult, op1=ALU.add),
                      reads=[bs], writes=[bs])
                tcol = bs[:r, 5:6]
            else:
                tcol = bs[:r, 1:2]
            kb.op('dve', lambda e: e.tensor_scalar(out=mask[:r, ms0 + gi, 0:nvis], in0=isc[:r, 0:nvis], scalar1=tcol, scalar2=1.0,
                                                   op0=ALU.is_ge, op1=ALU.mult), reads=[isc, bs, mk], writes=[mk])
        st["rli"] = rli
        yield

    def dsa_rest(self, l, pes, st, brT, bqT, first_done):
        cfg, kb = self.cfg, self.kb
        Tn, qts, P, kind = self.Tn, self.qts, self.P, self.kind
        NK, sups, koffs, mask, mask1, GQ, mtok = (st[k] for k in ("NK", "sups", "koffs", "mask", "mask1", "GQ", "mtok"))
        bk16 = [kb.sb("bk16_%d" % i, [128, 4, 256], BF16, es=pes) for i in range(2)]
        KBt = [kb.sb("KBt%d" % i, [128, 2, 512], BF16, es=pes) for i in range(2)]
        VB = [kb.sb("VB%d" % i, [128, 4, 4, 65], BF16, es=pes) for i in range(2)]
        Eb = [kb.sb("Eb%d" % i, [128, 512], BF16, es=pes) for i in range(2)]
        tb = [kb.sb("tbias%d" % i, [128, 260], F32, es=pes) for i in range(2)]
        PT = [kb.sb("PTb%d" % i, [128, 512], BF16, es=pes) for i in range(4)]
        self.fin_bufs = [(kb.sb("recb%d" % i, [128, 4], F32, es=pes), kb.sb("obb%d" % i, [128, 256], BF16, es=pes)) for i in range(2)]
        for v in VB:
            kb.op('dve', lambda e, v=v: e.memset(v[:, :, :, 64:65], 1.0), writes=[v])
        VBd = [[T(v.t) for _ in range(4)] for v in VB]
        for g0 in range(0, len(qts), GQ):
            grp = qts[g0:g0 + GQ]
            qc0 = grp[0][0]
            qc1 = grp[-1][0] + grp[-1][1]
            accs = [kb.ps('acc') for _ in range(4)]
            first_acc = [True] * 4
            pti = 0
            pend = []
            nvis_g = (P + qc1) if kind == 'p' else NK
            for si, sup in enumerate(sups):
                kk0 = koffs[si]
                if kk0 >= nvis_g:
                    break
                b = si % 2
                nt = len(sup)
                a1, d1, kr = self.ksrc("bk", l, sup)
                a2, d2, _ = self.ksrc("bv", l, sup)
                kb.dma('pool', bk16[b][:kr, 0:nt, :], a1, reads=d1, writes=[bk16[b]])
                for t in range(nt):
                    kb.dma('pool', VB[b][:kr, t, :, 0:64], a2[:, t, :].rearrange("p (h c) -> p h c", h=4), reads=d2, writes=[VBd[b][t]])
                pt = kb.ps('misc')
                ptb = pt[:, :].bitcast(BF16)
                for t in range(nt):
                    for pr in range(2):
                        self.tr(pt, ptb[:, pr * 512 + t * 128:pr * 512 + t * 128 + kr], bk16[b], bk16[b][:kr, t, pr * 128:(pr + 1) * 128], kr)
                nk = sum(t[2] for t in sup)
                kb.op('act', lambda e: e.copy(out=KBt[b][:, :, 0:nk], in_=ptb[:, :].rearrange("p (a b) -> p a b", a=2)[:, :, 0:nk]),
                      reads=[pt], writes=[KBt[b]])
                for t, (_, _, kr_t, apos) in enumerate(sup):
                    qlo, diag = self.vis(apos, kr_t)
                    qlo = max(qlo, qc0)
                    if qlo >= qc1:
                        continue
                    nq = qc1 - qlo
                    kcol = kk0 + t * 128
                    pm = kb.ps('misc')
                    pmb = pm[:, :].bitcast(BF16)
                    for gi, (q0, r) in enumerate(grp):
                        if q0 < qlo:
                            continue
                        self.tr(pm, pmb[:kr_t, q0 - qc0:q0 - qc0 + r], mtok[gi], mask[:r, gi, kcol:kcol + kr_t], r)
                    a = (apos - P) // 128
                    blo = max(qlo, 128 * a) if a >= -1 else None
                    bhi = min(qc1, 128 * a + 257) if a >= -1 else None
                    for h in range(4):
                        hp, ho = h // 2, (h % 2) * 64
                        psq = kb.ps('qk')
                        self.mm(psq, psq[:kr_t, 0:nq], KBt[b], KBt[b][ho:ho + 64, hp, t * 128:t * 128 + kr_t],
                                bqT, bqT[ho:ho + 64, hp, qlo:qc1], True, True)
                        E = Eb[pti % len(Eb)]
                        p = PT[pti % len(PT)]
                        tbt = tb[pti % len(tb)]
                        pti += 1
                        if blo is not None and blo < bhi:
                            wdt = bhi - blo
                            kb.op('dve', lambda e: e.scalar_tensor_tensor(
                                out=tbt[:kr_t, 0:wdt], in0=psq[:kr_t, blo - qlo:bhi - qlo], scalar=B_SCALE,
                                in1=self.gt[:kr_t, h, blo - 128 * a:bhi - 128 * a], op0=ALU.mult, op1=ALU.add),
                                reads=[psq, self.gt, tbt], writes=[tbt])
                            kb.op('act', lambda e: e.activation(out=E[:kr_t, blo - qlo:bhi - qlo], in_=tbt[:kr_t, 0:wdt],
                                                                func=AF.Exp), reads=[tbt, E], writes=[E])
                            for (c0, c1) in ((qlo, blo), (bhi, qc1)):
                                if c1 > c0:
                                    kb.op('act', lambda e: e.activation(
                                        out=E[:kr_t, c0 - qlo:c1 - qlo], in_=psq[:kr_t, c0 - qlo:c1 - qlo], func=AF.Exp, scale=B_SCALE),
                                        reads=[psq, E], writes=[E])
                        else:
                            kb.op('act', lambda e: e.activation(out=E[:kr_t, 0:nq], in_=psq[:kr_t, 0:nq], func=AF.Exp, scale=B_SCALE),
                                  reads=[psq, E], writes=[E])
                        kb.op('dve', lambda e: e.tensor_tensor(out=p[:kr_t, 0:nq], in0=E[:kr_t, 0:nq],
                                                               in1=pmb[:kr_t, qlo - qc0:qc1 - qc0], op=ALU.mult),
                              reads=[E, pm, p], writes=[p])

                        def pv(p=p, h=h, hp=hp, t=t, b=b, kr_t=kr_t, qlo=qlo):
                            vq = [(gi, q0, r) for gi, (q0, r) in enumerate(grp) if q0 >= qlo]
                            for j, (gi, q0, r) in enumerate(vq):
                                col = gi * 65
                                self.mm(accs[h], accs[h][:r, col:col + 65], p, p[:kr_t, q0 - qlo:q0 - qlo + r],
                                        VBd[b][t], VB[b][:kr_t, t, h, :], first_acc[h], True, inc=(j == len(vq) - 1), skip_group_check=True)
                                first_acc[h] = False
                        pend.append(pv)
                        if len(pend) > 1:
                            pend.pop(0)()
            while pend:
                pend.pop(0)()
            self.dsa_finish(pes, accs, brT, grp, GQ)

    def dsa_finish(self, pes, accs, brT, grp, GQ):
        kb = self.kb
        if not hasattr(self, "_dsa_fin"):
            self._dsa_fin = 0
        for gi, (q0, r) in enumerate(grp):
            rec, ob = self.fin_bufs[gi % 2]
            for h in range(4):
                a = accs[h]
                col = gi * 65
                kb.op('dve', lambda e, h=h, a=a, col=col: e.reciprocal(out=rec[:r, h:h + 1], in_=a[:r, col + 64:col + 65]),
                      reads=[a, rec], writes=[rec])
                kb.op('act', lambda e, h=h, a=a, col=col: e.activation(out=ob[:r, h * 64:(h + 1) * 64], in_=a[:r, col:col + 64],
                                                                       func=AF.Copy, scale=rec[:r, h:h + 1]), reads=[a, rec, ob], writes=[ob])
            pt = kb.ps('misc')
            ptb = pt[:, :].bitcast(BF16)
            for c in range(2):
                self.tr(pt, ptb[:, c * 128:c * 128 + r], ob, ob[:r, c * 128:(c + 1) * 128], r)
            kb.op('act', lambda e: e.copy(out=brT[:, 2:4, q0:q0 + r],
                                          in_=ptb[:, 0:256].rearrange("p (a b) -> p a b", a=2)[:, :, 0:r]),
                  reads=[pt, brT], writes=[brT])
        self._dsa_fin += 1

    def phase_merge(self, l, brT):
        cfg, kb = self.cfg, self.kb
        Tn = self.Tn
        x, xT = self.x, self.xT
        with ExitStack() as pes:
            mixed = kb.sb("mixed", [128, 8, Tn], F32, es=pes)
            mixed16 = kb.sb("mixed16", [128, 8, Tn], BF16, es=pes)
            sgs = [kb.sb("sgm%d" % i, [128, Tn], F32, es=pes) for i in range(2)]
            tmp = [kb.sb("tmpm%d" % i, [128, Tn], F32, es=pes) for i in range(2)]
            i = 0
            for hf in range(2):
                for n in range(4):
                    wbr = self.getw("wbr%d%d" % (n, hf), l)
                    wbr3 = wbr[:, 0:1024].rearrange("p (k n) -> p k n", k=2)
                    wg = self.getw("wg%d%d" % (n, hf), l, hold=1)
                    wg3 = wg[:, :].rearrange("p (k n) -> p k n", k=8)
                    for dl in range(4):
                        dc = hf * 4 + dl
                        pg = kb.ps()
                        for k in range(8):
                            self.mm(pg, pg[:, 0:Tn], wg, wg3[:, k, dl * 128:(dl + 1) * 128], xT, xT[:, k, 0:Tn], k == 0, k == 7, inc=(k == 7))
                        pb = kb.ps()
                        for k in range(2):
                            self.mm(pb, pb[:, 0:Tn], wbr, wbr3[:, k, dl * 128:(dl + 1) * 128], brT, brT[:, n * 2 + k, 0:Tn], k == 0, k == 1, inc=(k == 1))
                        sg = sgs[i % 2]
                        tm = tmp[i % 2]
                        i += 1
                        kb.op('act', lambda e: e.activation(out=sg[:, :], in_=pg[:, 0:Tn], func=AF.Sigmoid), reads=[pg], writes=[sg])
                        if n == 0:
                            kb.op('dve', lambda e: e.tensor_tensor(out=mixed[:, dc, :], in0=pb[:, 0:Tn], in1=sg[:, :], op=ALU.mult),
                                  reads=[pb, sg, mixed], writes=[mixed])
                        else:
                            kb.op('dve', lambda e: e.tensor_tensor(out=tm[:, :], in0=pb[:, 0:Tn], in1=sg[:, :], op=ALU.mult),
                                  reads=[pb, sg], writes=[tm])
                            if n < 3:
                                kb.op('dve', lambda e: e.tensor_tensor(out=mixed[:, dc, :], in0=mixed[:, dc, :], in1=tm[:, :], op=ALU.add),
                                      reads=[tm, mixed], writes=[mixed])
                            else:
                                kb.op('dve', lambda e: e.tensor_tensor(out=mixed16[:, dc, :], in0=mixed[:, dc, :], in1=tm[:, :], op=ALU.add),
                                      reads=[tm, mixed, mixed16], writes=[mixed16])
            s = kb.sb("s_ln1", [128, 8, Tn], F32, es=pes)
            for hf in range(2):
                wo = self.getw("wout%d" % hf, l)
                wo3 = wo[:, :].rearrange("p (k n) -> p k n", k=8)
                for dl in range(4):
                    dc = hf * 4 + dl
                    py = kb.ps()
                    for k in range(8):
                        self.mm(py, py[:, 0:Tn], wo, wo3[:, k, dl * 128:(dl + 1) * 128], mixed16, mixed16[:, k, :], k == 0, k == 7, inc=(k == 7))
                    kb.op('dve', lambda e: e.scalar_tensor_tensor(out=s[:, dc, :], in0=x[:, dc, 0:Tn], scalar=cfg.ALPHA, in1=py[:, 0:Tn],
                                                                  op0=ALU.mult, op1=ALU.add), reads=[x, py, s], writes=[s])
            pv = self.pv_ln
            self.lnfm(pes, s, 8, Tn, self.ones1024, lambda c: pv[:, l, 0, c:c + 1], lambda c: pv[:, l, 1, c:c + 1], pv,
                      [(x, lambda c: x[:, c, 0:Tn]), (xT, lambda c: xT[:, c, 0:Tn])])
            kb.barrier()

    def phase_ffn(self, l):
        cfg, kb, I, O = self.cfg, self.kb, self.I, self.O
        Tn, kind, pre, row0, qts = self.Tn, self.kind, self.pre, self.row0, self.qts
        x, xT = self.x, self.xT
        fcar = self.fcar[kind]
        fcv = fcar[:, l, :, :].rearrange("p (g c) t -> p g c t", g=2)
        pvf = self.pv_f
        with ExitStack() as pes:
            hT = kb.sb("hT", [128, NFC, Tn], BF16, es=pes)
            Es = [kb.sb("Effn%d" % i, [128, 2, 2 + Tn], F32, es=pes) for i in range(3)]
            av = [kb.sb("avf%d" % i, [128, 2, Tn], F32, es=pes) for i in range(3)]
            sgl = [kb.sb("sgf%d" % i, [128, Tn], F32, es=pes) for i in range(3)]
            Ecs = [T(E_.t) for E_ in Es]
            ci = 0
            tails = []
            for g in range(6):
                nch = 4 if g < 5 else 2
                wvv = self.getw("wupv%d" % g, l)
                wgg = self.getw("wupg%d" % g, l, hold=1)
                wts = (wvv, wgg)
                w3s = [w_[:, 0:8 * nch * 128].rearrange("p (k n) -> p k n", k=8) for w_ in wts]
                for cl in range(nch):
                    c = g * 4 + cl
                    E = Es[ci % 3]
                    a = av[ci % 3]
                    sg = sgl[ci % 3]
                    ci += 1
                    Ec = Ecs[(ci - 1) % 3]
                    kb.op('dve', lambda e, E=E, c=c: e.tensor_copy(out=E[:, :, 0:2], in_=fcv[:, :, c, :]), reads=[fcar], writes=[Ec])
                    for gg in range(2):
                        cc = gg * NFC + c
                        ps = kb.ps()
                        for k in range(8):
                            self.mm(ps, ps[:, 0:Tn], wts[gg], w3s[gg][:, k, cl * 128:(cl + 1) * 128], xT, xT[:, k, 0:Tn], k == 0, k == 7, inc=(k == 7))
                        kb.op('act', lambda e: e.copy(out=E[:, gg, 2:2 + Tn], in_=ps[:, 0:Tn]), reads=[ps, E], writes=[E])
                        kb.op('act', lambda e: e.activation(out=a[:, gg, :], in_=ps[:, 0:Tn], func=AF.Identity, scale=pvf[:, l, cc, 2:3],
                                                            bias=pvf[:, l, cc, 3:4]), reads=[ps, pvf, a], writes=[a])
                    kb.op('dve', lambda e: e.tensor_copy(out=fcv[:, :, c, :], in_=E[:, :, Tn:Tn + 2]), reads=[E, fcar], writes=[fcar])
                    for gg in range(2):
                        cc = gg * NFC + c
                        for k in (1, 0):
                            kb.op('dve', lambda e: e.scalar_tensor_tensor(
                                out=a[:, gg, :], in0=E[:, gg, k:k + Tn], scalar=pvf[:, l, cc, k:k + 1], in1=a[:, gg, :], op0=ALU.mult, op1=ALU.add),
                                reads=[E, Ec, pvf, a], writes=[a])
                    def tail(a=a, sg=sg, c=c):
                        kb.op('act', lambda e: e.activation(out=sg[:, :], in_=a[:, 1, :], func=AF.Silu), reads=[a], writes=[sg])
                        kb.op('dve', lambda e: e.tensor_tensor(out=hT[:, c, :], in0=a[:, 0, :], in1=sg[:, :], op=ALU.mult),
                              reads=[a, sg, hT], writes=[hT])
                    tails.append(tail)
                    if len(tails) > 1:
                        tails.pop(0)()
            while tails:
                tails.pop(0)()
            if self.last_block:
                sts = [kb.sb("stf%d" % i, [2, 512], F32, es=pes) for i in range(2)]
                for c0 in range(0, 44, 4):
                    st = sts[(c0 // 4) % 2]
                    ps = kb.ps()
                    for i in range(4):
                        self.tr(ps, ps[0:2, i * 128:(i + 1) * 128], fcar, fcar[:, l, c0 + i, :], 128, bf=False)
                    kb.op('act', lambda e: e.copy(out=st[0:2, :], in_=ps[0:2, :]), reads=[ps, st], writes=[st])
                    kb.dma('sp', O[pre + "_ffn"][l, :, c0 * 128:(c0 + 4) * 128], st[0:2, :], reads=[st])
            s = kb.sb("s_ln2", [128, 8, Tn], F32, es=pes)
            for dc in range(8):
                w = self.getw("wdn%d" % dc, l)
                w3 = w[:, 0:NFC * 128].rearrange("p (k n) -> p k n", k=NFC)
                py = kb.ps()
                for k in range(NFC):
                    self.mm(py, py[:, 0:Tn], w, w3[:, k, :], hT, hT[:, k, :], k == 0, k == NFC - 1, inc=(k == NFC - 1))
                kb.op('dve', lambda e: e.scalar_tensor_tensor(out=s[:, dc, :], in0=x[:, dc, 0:Tn], scalar=cfg.ALPHA, in1=py[:, 0:Tn],
                                                              op0=ALU.mult, op1=ALU.add), reads=[x, py, s], writes=[s])
            pv = self.pv_ln
            self.lnfm(pes, s, 8, Tn, self.ones1024, lambda c: pv[:, l, 2, c:c + 1], lambda c: pv[:, l, 3, c:c + 1], pv,
                      [(x, lambda c: x[:, c, 0:Tn]), (xT, lambda c: xT[:, c, 0:Tn])])
            if l == cfg.L - 1:
                yts = [kb.sb("ytok%d" % i, [128, D], F32, es=pes) for i in range(2)]
                for qi, (q0, r) in enumerate(qts):
                    yt = yts[qi % 2]
                    for half in range(2):
                        ps = kb.ps()
                        for c in range(4):
                            cc = half * 4 + c
                            self.tr(ps, ps[:r, c * 128:(c + 1) * 128], x, x[:, cc, q0:q0 + r], 128, bf=False)
                        kb.op('act', lambda e, ps=ps, half=half, yt=yt, r=r: e.copy(out=yt[:r, half * 512:(half + 1) * 512], in_=ps[:r, :]),
                              reads=[ps, yt], writes=[yt])
                    kb.dma('sp', O["y_" + pre][row0 + q0:row0 + q0 + r, :], yt[:r, :], reads=[yt])
            kb.barrier()


def _rel_bucket(rel):
    nb, max_exact = 16, 8
    ret = np.where(rel > 0, nb, 0)
    n = np.abs(rel)
    nf = np.maximum(n, 1).astype(np.float32)
    large = max_exact + (np.log(nf / np.float32(max_exact)) / np.float32(math.log(128 / max_exact))
                         * np.float32(nb - max_exact)).astype(np.int32)
    large = np.minimum(large, nb - 1)
    return ret + np.where(n < max_exact, n, large)


def _consts(cfg):
    half = 16
    freqs = (10000.0 ** (-np.arange(half, dtype=np.float32) / half)).astype(np.float32)

    def rope_tab(pos):
        ang = pos.astype(np.float32)[:, None] * freqs[None, :]
        c, s = np.cos(ang).astype(np.float32), np.sin(ang).astype(np.float32)
        return np.concatenate([c, c, -s, s], axis=1).astype(np.float32)

    rope_p = rope_tab(np.arange(cfg.SEQ))
    rope_s = rope_tab(cfg.PAST + np.arange(cfg.TS))
    p = np.arange(128)[:, None]
    jj = np.arange(258)[None, :]
    bidx = _rel_bucket(p - jj).astype(np.float32)
    icnt = np.zeros((128, 2, cfg.TB), np.float32)
    t = np.arange(cfg.TB)
    for g, w in enumerate((2, 4, 8, 16)):
        c, lo = g // 2, (g % 2) * 64
        icnt[lo:lo + 64, c, :] = 1.0 / np.minimum(w, t + 1).astype(np.float32)[None, :]
    return rope_p, rope_s, bidx, icnt.reshape(128, 2 * cfg.TB)


_CACHE = {}


def _get_nc(cfg_key):
    if cfg_key not in _CACHE:
        cfg = Cfg(*cfg_key)
        gen = Gen(cfg)
        nc = bass.Bass("TRN2", target_bir_lowering=False)
        gen.build(nc)
        _CACHE[cfg_key] = (cfg, gen, nc)
    return _CACHE[cfg_key]


def run(inputs, cfg_key):
    cfg, gen, nc = _get_nc(cfg_key)
    L = cfg.L
    f = lambda a: np.ascontiguousarray(np.asarray(a, dtype=np.float32))
    rope_p, rope_s, bidx, icnt0 = _consts(cfg)
    B = inputs["x_prompt"].shape[0]
    NS = inputs["x_sample"].shape[0]
    ncores = 8
    shared = {
        "rel_bias": f(inputs["rel_bias"]).reshape(1, 128), "ln_in_g": f(inputs["ln_in_g"]).reshape(1, D),
        "ln_in_b": f(inputs["ln_in_b"]).reshape(1, D), "rope_p": rope_p, "rope_s": rope_s, "bidx": bidx, "icnt0": icnt0,
    }
    for nm in ("w_in", "a_q_norm", "a_kv_norm", "a_w_qup", "a_w_kvup", "pool_w", "pool_scale", "conv_w", "conv_b",
               "conv_ln_g", "conv_ln_b", "w_branch", "w_out", "ln1_g", "ln1_b", "w_up", "ffn_conv_w", "ffn_conv_b",
               "w_down", "ln2_g", "ln2_b"):
        shared[nm] = f(inputs[nm])
    in_maps = []
    for c in range(ncores):
        m = dict(shared)
        m["xp"] = f(inputs["x_prompt"][c % B])
        s = c % NS
        m["xs"] = f(inputs["x_sample"][s])
        m["c_ckv"] = f(inputs["cache_a_ckv"][:, s]); m["c_krope"] = f(inputs["cache_a_krope"][:, s])
        m["c_bk"] = f(inputs["cache_b_k"][:, s]).reshape(L, cfg.PAST, 256)
        m["c_bv"] = f(inputs["cache_b_v"][:, s]).reshape(L, cfg.PAST, 256)
        m["c_kidx"] = f(inputs["cache_b_kidx"][:, s])
        m["st_pool"] = f(inputs["state_pool"][:, s]); m["st_conv"] = f(inputs["state_conv"][:, s])
        m["st_ffn"] = f(inputs["state_ffn"][:, s])
        in_maps.append(m)
    res = run_bass_kernel_spmd(nc, in_maps, core_ids=list(range(ncores)))
    R = res.results

    def stack_p(name, shape_tail):
        return np.stack([R[b]["p_" + name] for b in range(B)], axis=1).reshape((L, B) + shape_tail)

    def stack_s(name, shape_tail):
        return np.stack([R[s]["s_" + name] for s in range(NS)], axis=1).reshape((L, NS) + shape_tail)

    outs = [np.stack([R[b]["y_p"] for b in range(B)], 0), np.stack([R[s]["y_s"] for s in range(NS)], 0)]
    for st, n in ((stack_p, cfg.SEQ), (stack_s, cfg.TS)):
        outs += [st("ckv", (n, 128)), st("krope", (n, 32)), st("bk", (n, 4, 64)), st("bv", (n, 4, 64)), st("kidx", (n, 32)),
                 st("pool", (15, 256)), st("conv", (30, 256)), st("ffn", (2, 2 * DFF))]
    return tuple(np.ascontiguousarray(o.astype(np.float32)) for o in outs)


def kernel(**inputs):
    return run(inputs, (4096, 4, 4096, 16, 512, 4))
```

```python
import math
from contextlib import ExitStack

import numpy as np
import concourse.bass as bass
import concourse.mybir as mybir
from concourse.bass_utils import run_bass_kernel_spmd

F32 = mybir.dt.float32
BF16 = mybir.dt.bfloat16
ALU = mybir.AluOpType
AF = mybir.ActivationFunctionType
AX = mybir.AxisListType

D = 1024
DIN = 6148
DFF = 2816
NFC = DFF // 128
O_CQ, O_CKV, O_KR, O_BQ, O_BK, O_BV, O_QI, O_KI, O_WI, O_UP, O_UC, O_G = (
    0, 192, 320, 352, 608, 864, 1120, 1248, 1280, 1284, 1540, 2052)
A_SCALE = 96 ** -0.5
B_SCALE = 64 ** -0.5
IDX_SCALE = (4 ** -0.5) * (32 ** -0.5)
LN_EPS = 1e-5
NEG_BIG = -1.0e30
NIT = 12


class Cfg:
    def __init__(self, SEQ=4096, DEPTH=4, PAST=4096, DEC_SEQ=16, TB=512, GQ=2):
        self.SEQ, self.L, self.PAST, self.TS, self.TB, self.GQ = SEQ, DEPTH, PAST, DEC_SEQ, TB, GQ
        self.ALPHA = (2 * DEPTH) ** 0.25
        self.NB = SEQ // TB
        self.KP = min(256, SEQ // 4)
        self.KS = min(256, (PAST + DEC_SEQ) // 4)
        self.NKMAX = max(SEQ, PAST + 128)


class Dep:
    __slots__ = ("w", "r")

    def __init__(self):
        self.w = None
        self.r = []


class T:
    __slots__ = ("t", "d", "ex")

    def __init__(self, t, d=None, ex=False):
        self.t = t
        self.d = d if d is not None else Dep()
        self.ex = ex

    def __getitem__(self, k):
        return self.t[k]


class KB:
    NR = 8

    def __init__(self, nc, es):
        self.nc = nc
        self.es = es
        self.E = {'pe': nc.tensor, 'act': nc.scalar, 'dve': nc.vector, 'pool': nc.gpsimd, 'sp': nc.sync}
        self.sem = {e: es.enter_context(nc.semaphore("s_" + e)) for e in ['pe', 'act', 'dve', 'pool']}
        self.cnt = {e: 0 for e in self.sem}
        self.waited = {e: {} for e in self.E}
        self.dring = {q: [es.enter_context(nc.semaphore("d_%s%d" % (q, i))) for i in range(self.NR)]
                      for q in ['sp', 'pool']}
        self.dcnt = {q: 0 for q in self.dring}
        self.n_ins = 0
        self.n_wait = 0
        self._pend = []
        self.pending_dma = []
        self.ps_banks = [T(es.enter_context(nc.psum_tensor("psb%d" % i, [128, 512], F32)), ex=True) for i in range(8)]
        self.ps_groups = {'any': list(range(8))}
        self.ps_ctr = {}

    def sb(self, name, shape, dt=F32, es=None):
        self._nm = getattr(self, "_nm", 0) + 1
        return T((es or self.es).enter_context(self.nc.sbuf_tensor("%s_%d" % (name, self._nm), list(shape), dt)))

    def ps(self, group='any'):
        banks = self.ps_groups[group]
        i = self.ps_ctr.get(group, 0)
        self.ps_ctr[group] = i + 1
        return self.ps_banks[banks[i % len(banks)]]

    def _wait(self, e, evt):
        if evt is None:
            return
        key, sem, val = evt
        if key == e and e == 'pe':
            return
        w = self.waited[e]
        if w.get(key, 0) >= val:
            return
        self.E[e].wait_ge(sem, val)
        w[key] = val
        self.n_wait += 1

    def _deps(self, e, reads, writes):
        for t in reads:
            self._wait(e, t.d.w)
        for t in writes:
            self._wait(e, t.d.w)
            for ev in t.d.r:
                self._wait(e, ev)

    @staticmethod
    def _mark(evt, reads, writes):
        for t in reads:
            t.d.r.append(evt)
        for t in writes:
            t.d.w = evt
            t.d.r = []

    def op(self, e, fn, reads=(), writes=(), inc=True):
        ex = [t for t in reads if t.ex]
        if ex:
            reads = [t for t in reads if not t.ex]
            writes = list(writes) + ex
        self._deps(e, reads, writes)
        ins = fn(self.E[e])
        self.n_ins += 1
        if not inc:
            self._pend.append((list(reads), list(writes)))
            return None
        ins.then_inc(self.sem[e], 1)
        self.cnt[e] += 1
        evt = (e, self.sem[e], self.cnt[e])
        if e == 'pe' and self._pend:
            for (r_, w_) in self._pend:
                self._mark(evt, r_, w_)
            self._pend = []
        self._mark(evt, reads, writes)
        return evt

    def dma(self, q, out, in_, reads=(), writes=(), **kw):
        self._deps(q, reads, writes)
        i = self.dcnt[q] % self.NR
        gen = self.dcnt[q] // self.NR + 1
        sem = self.dring[q][i]
        key = "d_%s%d" % (q, i)
        if gen > 1:
            self._wait(q, (key, sem, 16 * (gen - 1)))
        self.E[q].dma_start(out=out, in_=in_, **kw).then_inc(sem, 16)
        self.dcnt[q] += 1
        evt = (key, sem, 16 * gen)
        self._mark(evt, reads, writes)
        self.pending_dma.append(evt)
        self.n_ins += 1
        return evt

    def barrier(self, engines=('pe', 'act', 'dve', 'sp', 'pool')):
        evs = [(e, self.sem[e], self.cnt[e]) for e in self.sem if self.cnt[e] > 0]
        evs += self.pending_dma
        self.pending_dma = []
        for e in engines:
            for ev in evs:
                self._wait(e, ev)

    def finish(self):
        self.barrier(engines=('sp',))


class Gen:
    def __init__(self, cfg):
        self.cfg = cfg

    def declare(self, nc):
        c = self.cfg
        L = c.L
        I = {}
        O = {}

        def inp(name, shape):
            I[name] = nc.dram_tensor(name, list(shape), F32, kind="ExternalInput").ap()

        def outp(name, shape):
            O[name] = nc.dram_tensor(name, list(shape), F32, kind="ExternalOutput").ap()

        inp("xp", [c.SEQ, D]); inp("xs", [c.TS, D])
        inp("c_ckv", [L, c.PAST, 128]); inp("c_krope", [L, c.PAST, 32]); inp("c_bk", [L, c.PAST, 256])
        inp("c_bv", [L, c.PAST, 256]); inp("c_kidx", [L, c.PAST, 32])
        inp("st_pool", [L, 15, 256]); inp("st_conv", [L, 30, 256]); inp("st_ffn", [L, 2, 2 * DFF])
        inp("rel_bias", [1, 128]); inp("ln_in_g", [1, D]); inp("ln_in_b", [1, D])
        inp("w_in", [L, D, DIN]); inp("a_q_norm", [L, 192]); inp("a_kv_norm", [L, 128])
        inp("a_w_qup", [L, 192, 384]); inp("a_w_kvup", [L, 128, 512]); inp("pool_w", [L, 4, 64, 64])
        inp("pool_scale", [L, 256]); inp("conv_w", [L, 31, 256]); inp("conv_b", [L, 256])
        inp("conv_ln_g", [L, 256]); inp("conv_ln_b", [L, 256]); inp("w_branch", [L, 4, 256, D])
        inp("w_out", [L, D, D]); inp("ln1_g", [L, D]); inp("ln1_b", [L, D]); inp("w_up", [L, D, 2 * DFF])
        inp("ffn_conv_w", [L, 3, 2 * DFF]); inp("ffn_conv_b", [L, 2 * DFF]); inp("w_down", [L, DFF, D])
        inp("ln2_g", [L, D]); inp("ln2_b", [L, D])
        inp("rope_p", [c.SEQ, 64]); inp("rope_s", [c.TS, 64]); inp("bidx", [128, 258]); inp("icnt0", [128, 2 * c.TB])
        outp("y_p", [c.SEQ, D]); outp("y_s", [c.TS, D])
        for pre, n in (("p", c.SEQ), ("s", c.TS)):
            outp(pre + "_ckv", [L, n, 128]); outp(pre + "_krope", [L, n, 32]); outp(pre + "_bk", [L, n, 256])
            outp(pre + "_bv", [L, n, 256]); outp(pre + "_kidx", [L, n, 32])
            outp(pre + "_pool", [L, 15, 256]); outp(pre + "_conv", [L, 30, 256]); outp(pre + "_ffn", [L, 2, 2 * DFF])
        self.I, self.O = I, O

    def mm(self, ps, out_ap, lt, lhsT, rt, rhs, start, stop, extra_r=(), inc=True, **kw):
        self.kb.op('pe', lambda e: e.matmul(out_ap, lhsT=lhsT, rhs=rhs, start=start, stop=stop, **kw),
                   reads=[lt, rt] + list(extra_r), writes=[ps], inc=inc)

    def tr(self, ps, out_ap, it, in_ap, rows, bf=True):
        idt = self.ident16 if bf else self.ident
        self.kb.op('pe', lambda e: e.transpose(out=out_ap, in_=in_ap, identity=idt[:rows, :rows]),
                   reads=[it, idt], writes=[ps])

    def build(self, nc):
        cfg = self.cfg
        self.declare(nc)
        es = ExitStack()
        with es:
            kb = KB(nc, es)
            self.kb = kb
            self.nc = nc
            self.setup_persistent()
            self.wplan = []
            self.wpos = 0
            self.plan_weights()
            self.wissue = 0
            for j in range(cfg.NB):
                self.block('p', j)
            self.block('s', 0)
            assert self.wpos == len(self.wplan), (self.wpos, len(self.wplan))
            kb.finish()
            self.stats = (kb.n_ins, kb.n_wait)
        return nc

    def setup_persistent(self):
        kb, cfg, I = self.kb, self.cfg, self.I
        L = cfg.L
        sb = kb.sb
        self.ident = sb("ident", [128, 128])
        self.ident16 = sb("ident16", [128, 128], BF16)
        kb.op('dve', lambda e: e.memset(self.ident[:], 1.0), writes=[self.ident])
        kb.op('pool', lambda e: e.affine_select(out=self.ident[:], in_=self.ident[:], pattern=[[-1, 128]],
                                                compare_op=ALU.is_equal, fill=0.0, base=0, channel_multiplier=1),
              reads=[self.ident], writes=[self.ident])
        kb.op('dve', lambda e: e.tensor_copy(out=self.ident16[:], in_=self.ident[:]), reads=[self.ident],
              writes=[self.ident16])
        self.ones1024 = sb("ones1024", [128, 128])
        self.ones256 = sb("ones256", [128, 128])
        kb.op('dve', lambda e: e.memset(self.ones1024[:], 1.0 / 1024), writes=[self.ones1024])
        kb.op('dve', lambda e: e.memset(self.ones256[:], 1.0 / 256), writes=[self.ones256])
        o16a, o16b = sb("ones1024b", [128, 128], BF16), sb("ones256b", [128, 128], BF16)
        kb.op('dve', lambda e: e.memset(o16a[:], 1.0 / 1024), writes=[o16a])
        kb.op('dve', lambda e: e.memset(o16b[:], 1.0 / 256), writes=[o16b])
        self.ones16 = {self.ones1024: o16a, self.ones256: o16b}
        self.esel = sb("esel", [32, 96], BF16)
        kb.op('dve', lambda e: e.memset(self.esel[:], 0.0), writes=[self.esel])
        kb.op('dve', lambda e: e.tensor_copy(out=self.esel[:, 64:96], in_=self.ident[0:32, 0:32]),
              reads=[self.ident, self.esel], writes=[self.esel])
        TB = cfg.TB
        self.ropet = sb("ropet", [128, 4, 64])
        self.x = sb("x", [128, 8, TB])
        self.xT = sb("xT", [128, 8, TB], BF16)
        self.qn = sb("qn", [128, L, 320])
        for l in range(L):
            kb.dma('sp', self.qn[:, l, 0:192], I["a_q_norm"][l:l + 1, :].partition_broadcast(128), writes=[self.qn])
            kb.dma('sp', self.qn[:, l, 192:320], I["a_kv_norm"][l:l + 1, :].partition_broadcast(128), writes=[self.qn])
        self.rb = sb("rb", [128, 128])
        kb.dma('sp', self.rb[:], I["rel_bias"].partition_broadcast(128), writes=[self.rb])
        self.pv_ln = sb("pv_ln", [128, L, 4, 8])
        self.pv_lnin = sb("pv_lnin", [128, 2, 8])
        self.pv_c = sb("pv_c", [128, L, 2, 35])
        self.pv_f = sb("pv_f", [128, L, 44, 4])
        self.pcar = {k: sb("pcar_" + k, [128, L, 2, 15]) for k in 'ps'}
        self.ccar = {k: sb("ccar_" + k, [128, L, 2, 30]) for k in 'ps'}
        self.fcar = {k: sb("fcar_" + k, [128, L, 44, 2]) for k in 'ps'}
        for t in (self.pcar['p'], self.ccar['p'], self.fcar['p']):
            kb.op('dve', lambda e, t=t: e.memset(t[:], 0.0), writes=[t])
        self.gt = sb("gt", [128, 4, 258])
        self.NW = 4
        self.wring = [sb("wring%d" % i, [128, 4096], BF16) for i in range(self.NW)]
        self._wsm_ctr = 0
        self._wr_ctr = 0
        self._wbr_ctr = 0
        self.w_released = 0
        self.w_ringreq = 0
        self.wsm = [sb("wsm%d" % i, [128, 1536], BF16) for i in range(2)]
        for t in self.wsm:
            kb.op('dve', lambda e, t=t: e.memset(t[:], 0.0), writes=[t])
        with ExitStack() as pes:
            stg = kb.sb("pstage", [32, 2 * DFF], F32, es=pes)

            def colvec(dst_fn, src_ap, R, C, dst_t):
                kb.dma('sp', stg[:R, 0:C * 128], src_ap, writes=[stg])
                c0 = 0
                while c0 < C:
                    nch = min(512 // max(R, 1), C - c0, 16)
                    nch = max(1, min(nch, 512 // R))
                    ps = kb.ps()
                    for i in range(nch):
                        self.tr(ps, ps[:, i * R:(i + 1) * R], stg, stg[:R, (c0 + i) * 128:(c0 + i + 1) * 128], R, bf=False)
                    for i in range(nch):
                        kb.op('act', lambda e, i=i: e.copy(out=dst_fn(c0 + i), in_=ps[:, i * R:(i + 1) * R]),
                              reads=[ps], writes=[dst_t])
                    c0 += nch

            colvec(lambda c: self.pv_lnin[:, 0, c:c + 1], I["ln_in_g"], 1, 8, self.pv_lnin)
            colvec(lambda c: self.pv_lnin[:, 1, c:c + 1], I["ln_in_b"], 1, 8, self.pv_lnin)
            for l in range(L):
                for i, nm in enumerate(("ln1_g", "ln1_b", "ln2_g", "ln2_b")):
                    colvec(lambda c, i=i: self.pv_ln[:, l, i, c:c + 1], I[nm][l:l + 1, :], 1, 8, self.pv_ln)
                colvec(lambda c: self.pv_c[:, l, c, 0:31], I["conv_w"][l], 31, 2, self.pv_c)
                for i, nm in enumerate(("conv_b", "conv_ln_g", "conv_ln_b", "pool_scale")):
                    colvec(lambda c, i=i: self.pv_c[:, l, c, 31 + i:32 + i], I[nm][l:l + 1, :], 1, 2, self.pv_c)
                colvec(lambda c: self.pv_f[:, l, c, 0:3], I["ffn_conv_w"][l], 3, 44, self.pv_f)
                colvec(lambda c: self.pv_f[:, l, c, 3:4], I["ffn_conv_b"][l:l + 1, :], 1, 44, self.pv_f)
                colvec(lambda c: self.pcar['s'][:, l, c, :], I["st_pool"][l], 15, 2, self.pcar['s'])
                colvec(lambda c: self.ccar['s'][:, l, c, :], I["st_conv"][l], 30, 2, self.ccar['s'])
                colvec(lambda c: self.fcar['s'][:, l, c, :], I["st_ffn"][l], 2, 44, self.fcar['s'])
            bidx = kb.sb("bidx_sb", [128, 258], F32, es=pes)
            tmpg = kb.sb("tmpg", [128, 258], F32, es=pes)
            kb.dma('sp', bidx[:], I["bidx"], writes=[bidx])
            for h in range(4):
                kb.op('dve', lambda e, h=h: e.tensor_scalar(out=self.gt[:, h, :], in0=bidx[:], scalar1=0.0,
                                                            scalar2=self.rb[:, 15 * 4 + h:15 * 4 + h + 1],
                                                            op0=ALU.mult, op1=ALU.subtract),
                      reads=[bidx, self.rb], writes=[self.gt])
                for b in range(32):
                    kb.op('dve', lambda e, h=h, b=b: e.tensor_scalar(out=tmpg[:], in0=bidx[:], scalar1=float(b),
                                                                    scalar2=self.rb[:, b * 4 + h:b * 4 + h + 1],
                                                                    op0=ALU.is_equal, op1=ALU.mult),
                          reads=[bidx, self.rb], writes=[tmpg])
                    kb.op('dve', lambda e, h=h: e.tensor_tensor(out=self.gt[:, h, :], in0=self.gt[:, h, :], in1=tmpg[:],
                                                                op=ALU.add),
                          reads=[tmpg, self.gt], writes=[self.gt])
            kb.barrier()
    def plan_weights(self):
        cfg, I = self.cfg, self.I
        plan = []
        nblocks = cfg.NB + 1
        for b in range(nblocks):
            for l in range(cfg.L):
                w_in = I["w_in"][l].rearrange("(k p) n -> p k n", p=128)
                plan.append(("wsm", l, None))
                plan.append(("wtok_a0", l, [(w_in[:, :, 0:352], (8, 352), 0)]))
                plan.append(("wtok_a1", l, [(w_in[:, :, 352:864], (8, 512), 0)]))
                plan.append(("wtok_b", l, [(w_in[:, :, 864:1284], (8, 420), 0)]))
                plan.append(("wpc0", l, [(w_in[:, :, 1284:1668], (8, 384), 0)]))
                plan.append(("wpc1", l, [(w_in[:, :, 1668:2052], (8, 384), 0)]))
                wbr = I["w_branch"][l].rearrange("n (k p) d -> p (n k) d", p=128)
                for hf in range(2):
                    for n in range(4):
                        c0 = O_G + n * 1024 + hf * 512
                        plan.append(("wbr%d%d" % (n, hf), l, [(wbr[:, 2 * n:2 * n + 2, hf * 512:(hf + 1) * 512], (2, 512), 0)]))
                        plan.append(("wg%d%d" % (n, hf), l, [(w_in[:, :, c0:c0 + 512], (8, 512), 0)]))
                w_o = I["w_out"][l].rearrange("(k p) n -> p k n", p=128)
                for hf in range(2):
                    plan.append(("wout%d" % hf, l, [(w_o[:, :, hf * 512:(hf + 1) * 512], (8, 512), 0)]))
                w_up = I["w_up"][l].rearrange("(k p) n -> p k n", p=128)
                for g in range(6):
                    nch = 4 if g < 5 else 2
                    plan.append(("wupv%d" % g, l, [(w_up[:, :, g * 512:g * 512 + nch * 128], (8, nch * 128), 0)]))
                    plan.append(("wupg%d" % g, l, [(w_up[:, :, DFF + g * 512:DFF + g * 512 + nch * 128], (8, nch * 128), 0)]))
                w_dn = I["w_down"][l].rearrange("(k p) n -> p k n", p=128)
                for g in range(8):
                    plan.append(("wdn%d" % g, l, [(w_dn[:, :, g * 128:(g + 1) * 128], (NFC, 128), 0)]))
        self.wplan = plan
        self.wslot = [None] * len(plan)

    def issue_next_weight(self):
        i = self.wissue
        if i >= len(self.wplan):
            return False
        kb, I = self.kb, self.I
        tag, l, parts = self.wplan[i]
        if tag == "wsm":
            t = self.wsm[self._wsm_ctr % 2]
            self._wsm_ctr += 1
            kb.dma('pool', t[:, 0:384], I["a_w_qup"][l, 0:128, :], writes=[t])
            kb.dma('pool', t[0:64, 384:768], I["a_w_qup"][l, 128:192, :], writes=[t])
            kb.dma('pool', t[:, 768:1280], I["a_w_kvup"][l], writes=[t])
            for g in range(4):
                cc, hh = g // 2, g % 2
                kb.dma('pool', t[hh * 64:hh * 64 + 64, 1280 + cc * 128 + hh * 64:1280 + cc * 128 + hh * 64 + 64],
                       I["pool_w"][l, g], writes=[t])
            self.wslot[i] = t
        else:
            if self._wr_ctr - self.NW >= self.w_released:
                return False
            t = self.wring[self._wr_ctr % self.NW]
            self._wr_ctr += 1
            for (src, (a, b), off) in parts:
                dst = t[:, off:off + a * b].rearrange("p (a b) -> p a b", a=a)
                kb.dma('pool', dst, src, writes=[t])
            self.wslot[i] = t
        self.wissue += 1
        return True

    def getw(self, tag, l, hold=0):
        ptag, pl, _ = self.wplan[self.wpos]
        assert ptag == tag and pl == l, (ptag, pl, tag, l)
        is_ring = tag != "wsm"
        if is_ring:
            self.w_ringreq += 1
        self.w_released = max(self.w_released, self.w_ringreq - hold - (1 if is_ring else 0))
        while self.wissue < len(self.wplan) and self.wissue <= self.wpos + 6:
            if self.issue_next_weight() is False:
                break
        assert self.wissue > self.wpos, (tag, l)
        t = self.wslot[self.wpos]
        self.wpos += 1
        return t

    def lnfm(self, es, s, C, T, ones, gfn, bfn, gb_t, outs, func=AF.Identity):
        kb = self.kb
        sqs = [kb.sb("ln_sq%d" % i, [128, T], BF16, es=es) for i in range(2)]
        s16s = [kb.sb("ln_s16%d" % i, [128, T], BF16, es=es) for i in range(2)]
        ones16 = self.ones16[ones]
        pm = kb.ps()
        pq = kb.ps()
        for c in range(C):
            sq, s16 = sqs[c % 2], s16s[c % 2]
            kb.op('dve', lambda e: e.tensor_copy(out=s16[:, :], in_=s[:, c, 0:T]), reads=[s], writes=[s16])
            kb.op('act', lambda e: e.activation(out=sq[:, :], in_=s[:, c, 0:T], func=AF.Square), reads=[s], writes=[sq])
            self.mm(pm, pm[:, 0:T], ones16, ones16[:, :], s16, s16[:, :], c == 0, c == C - 1)
            self.mm(pq, pq[:, 0:T], ones16, ones16[:, :], sq, sq[:, :], c == 0, c == C - 1)
        mean = kb.sb("ln_mean", [128, T], F32, es=es)
        rstd = kb.sb("ln_rstd", [128, T], F32, es=es)
        kb.op('act', lambda e: e.copy(out=mean[:], in_=pm[:, 0:T]), reads=[pm], writes=[mean])
        kb.op('dve', lambda e: e.tensor_tensor(out=rstd[:], in0=mean[:], in1=mean[:], op=ALU.mult), reads=[mean], writes=[rstd])
        kb.op('dve', lambda e: e.tensor_tensor(out=rstd[:], in0=pq[:, 0:T], in1=rstd[:], op=ALU.subtract), reads=[pq, rstd], writes=[rstd])
        kb.op('dve', lambda e: e.tensor_scalar(out=rstd[:], in0=rstd[:], scalar1=0.0, scalar2=LN_EPS, op0=ALU.max, op1=ALU.add),
              reads=[rstd], writes=[rstd])
        kb.op('act', lambda e: e.activation(out=rstd[:], in_=rstd[:], func=AF.Sqrt), reads=[rstd], writes=[rstd])
        kb.op('dve', lambda e: e.reciprocal(out=rstd[:], in_=rstd[:]), reads=[rstd], writes=[rstd])
        for c in range(C):
            sc = type(s)(s.t)
            kb.op('dve', lambda e: e.tensor_tensor(out=s[:, c, 0:T], in0=s[:, c, 0:T], in1=mean[:], op=ALU.subtract),
                  reads=[s, mean], writes=[sc])
            kb.op('dve', lambda e: e.tensor_tensor(out=s[:, c, 0:T], in0=s[:, c, 0:T], in1=rstd[:], op=ALU.mult),
                  reads=[sc, rstd], writes=[sc])
            for (ot, ofn) in outs:
                kb.op('act', lambda e: e.activation(out=ofn(c), in_=s[:, c, 0:T], func=func, bias=bfn(c), scale=gfn(c)),
                      reads=[sc, gb_t], writes=[ot])

    def block(self, kind, j):
        cfg, kb, I, O = self.cfg, self.kb, self.I, self.O
        L = cfg.L
        if kind == 'p':
            Tn = cfg.TB
            P = j * cfg.TB
            xin = I["xp"][P:P + Tn, :]
            rope = I["rope_p"][P:P + Tn, :]
            pre = "p"
            row0 = P
            ktop = cfg.KP
        else:
            Tn = cfg.TS
            P = cfg.PAST
            xin = I["xs"]
            rope = I["rope_s"]
            pre = "s"
            row0 = 0
            ktop = cfg.KS
        self.kind, self.Tn, self.P, self.pre, self.row0, self.ktop, self.bj = kind, Tn, P, pre, row0, ktop, j
        qts = [(q0, min(128, Tn - q0)) for q0 in range(0, Tn, 128)]
        self.qts = qts
        if kind == 'p':
            self.ktiles = [('o', t * 128, 128, t * 128) for t in range((P + Tn) // 128)]
        else:
            self.ktiles = [('c', t * 128, 128, t * 128) for t in range(P // 128)] + [('o', 0, Tn, P)]
        self.last_block = (kind == 's') or (j == cfg.NB - 1)
        x, xT = self.x, self.xT
        with ExitStack() as pes:
            for qi, (q0, r) in enumerate(qts):
                kb.dma('sp', self.ropet[:r, qi, :], rope[q0:q0 + r, :], writes=[self.ropet])
            xraw = kb.sb("xraw", [128, 8, Tn], F32, es=pes)
            xtoks = [kb.sb("xtok%d" % i, [128, D], F32, es=pes) for i in range(2)]
            for qi, (q0, r) in enumerate(qts):
                xtok = xtoks[qi % 2]
                kb.dma('sp', xtok[:r, :], xin[q0:q0 + r, :], writes=[xtok])
                for half in range(2):
                    ps = kb.ps()
                    for c in range(4):
                        cc = half * 4 + c
                        self.tr(ps, ps[:, c * 128:c * 128 + r], xtok, xtok[:r, cc * 128:(cc + 1) * 128], r, bf=False)
                    kb.op('act', lambda e, half=half, ps=ps, q0=q0, r=r: e.copy(
                        out=xraw[:, half * 4:half * 4 + 4, q0:q0 + r],
                        in_=ps[:, :].rearrange("p (c t) -> p c t", c=4)[:, :, 0:r]), reads=[ps], writes=[xraw])
            self.lnfm(pes, xraw, 8, Tn, self.ones1024,
                      lambda c: self.pv_lnin[:, 0, c:c + 1], lambda c: self.pv_lnin[:, 1, c:c + 1], self.pv_lnin,
                      [(x, lambda c: x[:, c, 0:Tn]), (xT, lambda c: xT[:, c, 0:Tn])])
            kb.barrier()
        for l in range(L):
            self.layer(l)

    def layer(self, l):
        cfg, kb, I, O = self.cfg, self.kb, self.I, self.O
        kind, Tn, P, pre, row0, qts = self.kind, self.Tn, self.P, self.pre, self.row0, self.qts
        x, xT = self.x, self.xT
        NQ = len(qts)
        wsm = self.getw("wsm", l)
        wqup = lambda kc, ksz: wsm[0:ksz, kc * 384:(kc + 1) * 384]
        with ExitStack() as mes:
            brT = kb.sb("brT", [128, 8, Tn], BF16, es=mes)
            qaT = kb.sb("qaT", [96, 4, Tn], BF16, es=mes)
            bqT = kb.sb("bqT", [128, 2, Tn], BF16, es=mes)
            qiT = kb.sb("qiT", [32, 4, Tn], BF16, es=mes)
            wqs = kb.sb("wqs", [128, NQ, 4], F32, es=mes)
            wn = kb.sb("wn", [128, 4, 96], BF16, es=mes)
            wv = kb.sb("wv", [128, 256], BF16, es=mes)
            kvv = wsm[:, 768:1280].rearrange("p (h c) -> p h c", h=4)
            kb.op('dve', lambda e: e.memset(wn[:], 0.0), writes=[wn])
            kb.op('dve', lambda e: e.tensor_copy(out=wn[:, :, 0:64], in_=kvv[:, :, 0:64]), reads=[wsm, wn], writes=[wn])
            kb.op('dve', lambda e: e.tensor_copy(out=wv[:, :].rearrange("p (h c) -> p h c", h=4), in_=kvv[:, :, 64:128]),
                  reads=[wsm], writes=[wv])
            self.wn, self.wv = wn, wv
            odeps = {nm: T(None) for nm in ("ckv", "krope", "bk", "bv", "kidx")}
            wa0 = self.getw("wtok_a0", l)
            wa1 = self.getw("wtok_a1", l, hold=1)
            wb = self.getw("wtok_b", l, hold=2)
            wa03 = wa0[:, 0:8 * 352].rearrange("p (k n) -> p k n", k=8)
            wa13 = wa1[:, 0:8 * 512].rearrange("p (k n) -> p k n", k=8)
            wb3 = wb[:, 0:8 * 420].rearrange("p (k n) -> p k n", k=8)
            with ExitStack() as pes:
                NQ_ = len(qts)
                rp = self.ropet
                Bf = []
                for i in range(NQ_):
                    sfx = str(i)
                    Bf.append(dict(
                        toka=kb.sb("toka" + sfx, [128, 352], F32, es=pes), tokb=kb.sb("tokb" + sfx, [128, 512], F32, es=pes),
                        tokc=kb.sb("tokc" + sfx, [128, 420], F32, es=pes), rows=kb.sb("rowsA" + sfx, [128, 160], F32, es=pes),
                        ss=kb.sb("ss" + sfx, [128, 4], F32, es=pes), junk=kb.sb("junk" + sfx, [128, 192], F32, es=pes),
                        cqn=kb.sb("cqn" + sfx, [128, 192], BF16, es=pes), bq16=kb.sb("bq16" + sfx, [128, 384], BF16, es=pes),
                        cqT=kb.sb("cqT" + sfx, [128, 2, 128], BF16, es=pes), qa16=kb.sb("qa16" + sfx, [128, 4, 96], BF16, es=pes),
                        rt1=kb.sb("rt1" + sfx, [128, 4, 32], F32, es=pes), rt2=kb.sb("rt2" + sfx, [128, 4, 32], F32, es=pes)))
                for qi, (q0, r) in enumerate(qts):
                    b_ = Bf[qi]
                    tpa, tpb, tpc = kb.ps(), kb.ps(), kb.ps()
                    for (ps, wt, w3, c0, c1) in ((tpa, wa0, wa03, 0, 352), (tpb, wa1, wa13, 0, 512), (tpc, wb, wb3, 0, 420)):
                        for k in range(8):
                            self.mm(ps, ps[:r, 0:c1 - c0], xT, xT[:, k, q0:q0 + r], wt, w3[:, k, c0:c1], k == 0, k == 7, inc=(k == 7))
                    kb.op('act', lambda e: e.copy(out=b_['toka'][:r, :], in_=tpa[:r, 0:352]), reads=[tpa], writes=[b_['toka']])
                    kb.op('dve', lambda e: e.tensor_copy(out=b_['tokb'][:r, :], in_=tpb[:r, :]), reads=[tpb], writes=[b_['tokb']])
                    kb.op('act', lambda e: e.copy(out=b_['tokc'][:r, :], in_=tpc[:r, 0:420]), reads=[tpc], writes=[b_['tokc']])
                pc = self.phase_poolconv(l, brT)
                next(pc, None)
                for qi, (q0, r) in enumerate(qts):
                    b_ = Bf[qi]
                    ss, junk, toka = b_['ss'], b_['junk'], b_['toka']
                    kb.op('dve', lambda e: e.memset(ss[:], 0.0), writes=[ss])
                    kb.op('act', lambda e: e.activation(out=junk[:r, 0:192], in_=toka[:r, 0:192], func=AF.Square,
                                                        scale=192 ** -0.5, accum_out=ss[:r, 0:1]), reads=[toka, ss], writes=[junk, ss])
                    kb.op('act', lambda e: e.activation(out=junk[:r, 0:128], in_=toka[:r, 192:320], func=AF.Square,
                                                        scale=128 ** -0.5, accum_out=ss[:r, 1:2]), reads=[toka, ss], writes=[junk, ss])
                for qi, (q0, r) in enumerate(qts):
                    ss = Bf[qi]['ss']
                    kb.op('dve', lambda e: e.tensor_scalar(out=ss[:r, 2:4], in0=ss[:r, 0:2], scalar1=LN_EPS, scalar2=1.0, op0=ALU.add, op1=ALU.mult),
                          reads=[ss], writes=[ss])
                for qi, (q0, r) in enumerate(qts):
                    ss = Bf[qi]['ss']
                    kb.op('act', lambda e: e.activation(out=ss[:r, 2:4], in_=ss[:r, 2:4], func=AF.Sqrt), reads=[ss], writes=[ss])
                for qi, (q0, r) in enumerate(qts):
                    ss = Bf[qi]['ss']
                    kb.op('dve', lambda e: e.reciprocal(out=ss[:r, 2:4], in_=ss[:r, 2:4]), reads=[ss], writes=[ss])
                next(pc, None)
                for qi, (q0, r) in enumerate(qts):
                    b_ = Bf[qi]
                    ss, toka, tokb, tokc, rows, cqn, rt1, rt2, bq16 = (b_[k] for k in ('ss', 'toka', 'tokb', 'tokc', 'rows', 'cqn', 'rt1', 'rt2', 'bq16'))
                    kb.op('dve', lambda e: e.scalar_tensor_tensor(out=cqn[:r, :], in0=toka[:r, 0:192], scalar=ss[:r, 2:3],
                                                                  in1=self.qn[:r, l, 0:192], op0=ALU.mult, op1=ALU.mult),
                          reads=[toka, ss, self.qn], writes=[cqn])
                    kb.op('dve', lambda e: e.scalar_tensor_tensor(out=rows[:r, 0:128], in0=toka[:r, 192:320], scalar=ss[:r, 3:4],
                                                                  in1=self.qn[:r, l, 192:320], op0=ALU.mult, op1=ALU.mult),
                          reads=[toka, ss, self.qn], writes=[rows])
                    kb.op('dve', lambda e: e.tensor_tensor(out=rt1[:r, 0, :], in0=toka[:r, 320:352], in1=rp[:r, qi, 0:32], op=ALU.mult),
                          reads=[toka, rp], writes=[rt1])
                    kb.op('dve', lambda e: e.tensor_tensor(out=rt2[:r, 0, 0:16], in0=toka[:r, 336:352], in1=rp[:r, qi, 32:48], op=ALU.mult),
                          reads=[toka, rp], writes=[rt2])
                    kb.op('dve', lambda e: e.tensor_tensor(out=rt2[:r, 0, 16:32], in0=toka[:r, 320:336], in1=rp[:r, qi, 48:64], op=ALU.mult),
                          reads=[toka, rp, rt2], writes=[rt2])
                    kb.op('dve', lambda e: e.tensor_tensor(out=rows[:r, 128:160], in0=rt1[:r, 0, :], in1=rt2[:r, 0, :], op=ALU.add),
                          reads=[rt1, rt2, rows], writes=[rows])
                    rr = slice(row0 + q0, row0 + q0 + r)
                    kb.dma('sp', O[pre + "_ckv"][l, rr, :], rows[:r, 0:128], reads=[rows], writes=[odeps["ckv"]])
                    kb.dma('sp', O[pre + "_krope"][l, rr, :], rows[:r, 128:160], reads=[rows], writes=[odeps["krope"]])
                    kb.dma('sp', O[pre + "_bk"][l, rr, :], tokb[:r, 256:512], reads=[tokb], writes=[odeps["bk"]])
                    kb.dma('sp', O[pre + "_bv"][l, rr, :], tokc[:r, 0:256], reads=[tokc], writes=[odeps["bv"]])
                    kb.dma('sp', O[pre + "_kidx"][l, rr, :], tokc[:r, 384:416], reads=[tokc], writes=[odeps["kidx"]])
                    kb.op('dve', lambda e: e.tensor_copy(out=bq16[:r, 0:256], in_=tokb[:r, 0:256]), reads=[tokb], writes=[bq16])
                    kb.op('dve', lambda e: e.tensor_copy(out=bq16[:r, 256:384], in_=tokc[:r, 256:384]), reads=[tokc, bq16], writes=[bq16])
                    kb.op('dve', lambda e: e.tensor_scalar(out=wqs[:r, qi, :], in0=tokc[:r, 416:420], scalar1=IDX_SCALE, scalar2=0.0,
                                                           op0=ALU.mult, op1=ALU.add), reads=[tokc], writes=[wqs])
                next(pc, None)
                pts = []
                for qi, (q0, r) in enumerate(qts):
                    b_ = Bf[qi]
                    bq16, cqn = b_['bq16'], b_['cqn']
                    pt = kb.ps()
                    ptb = pt[:, :].bitcast(BF16)
                    for pr in range(2):
                        self.tr(pt, ptb[:, pr * 128:pr * 128 + r], bq16, bq16[:r, pr * 128:(pr + 1) * 128], r)
                    for h in range(4):
                        self.tr(pt, ptb[0:32, 256 + h * 128:256 + h * 128 + r], bq16, bq16[:r, 256 + h * 32:256 + (h + 1) * 32], r)
                    self.tr(pt, ptb[:, 768:768 + r], cqn, cqn[:r, 0:128], r)
                    self.tr(pt, ptb[0:64, 896:896 + r], cqn, cqn[:r, 128:192], r)
                    pts.append((pt, ptb))
                for qi, (q0, r) in enumerate(qts):
                    pt, ptb = pts[qi]
                    cqT = Bf[qi]['cqT']
                    kb.op('act', lambda e: e.copy(out=bqT[:, :, q0:q0 + r],
                                                  in_=ptb[:, 0:256].rearrange("p (a b) -> p a b", a=2)[:, :, 0:r]),
                          reads=[pt], writes=[bqT])
                    kb.op('dve', lambda e: e.tensor_copy(out=qiT[:, :, q0:q0 + r],
                                                         in_=ptb[0:32, 256:768].rearrange("p (a b) -> p a b", a=4)[:, :, 0:r]),
                          reads=[pt], writes=[qiT])
                    kb.op('act', lambda e: e.copy(out=cqT[:, 0, 0:r], in_=ptb[:, 768:768 + r]), reads=[pt], writes=[cqT])
                    kb.op('dve', lambda e: e.tensor_copy(out=cqT[0:64, 1, 0:r], in_=ptb[0:64, 896:896 + r]), reads=[pt, cqT], writes=[cqT])
                next(pc, None)
                pqs = []
                for qi, (q0, r) in enumerate(qts):
                    cqT = Bf[qi]['cqT']
                    pq = kb.ps()
                    self.mm(pq, pq[:r, 0:384], cqT, cqT[:, 0, 0:r], wsm, wqup(0, 128), True, False)
                    self.mm(pq, pq[:r, 0:384], cqT, cqT[0:64, 1, 0:r], wsm, wqup(1, 64), False, True)
                    pqs.append(pq)
                for qi, (q0, r) in enumerate(qts):
                    b_ = Bf[qi]
                    qa16, rt1, rt2 = b_['qa16'], b_['rt1'], b_['rt2']
                    pq = pqs[qi]
                    pq3 = pq[:, 0:384].rearrange("p (h c) -> p h c", h=4)
                    kb.op('act', lambda e: e.copy(out=qa16[:r, :, 0:64], in_=pq3[:r, :, 0:64]), reads=[pq], writes=[qa16])
                    for h in range(4):
                        kb.op('dve', lambda e: e.tensor_tensor(out=rt1[:r, h, :], in0=pq3[:r, h, 64:96], in1=rp[:r, qi, 0:32], op=ALU.mult),
                              reads=[pq, rp, rt1], writes=[rt1])
                        kb.op('dve', lambda e: e.tensor_tensor(out=rt2[:r, h, 0:16], in0=pq3[:r, h, 80:96], in1=rp[:r, qi, 32:48], op=ALU.mult),
                              reads=[pq, rp, rt2], writes=[rt2])
                        kb.op('dve', lambda e: e.tensor_tensor(out=rt2[:r, h, 16:32], in0=pq3[:r, h, 64:80], in1=rp[:r, qi, 48:64], op=ALU.mult),
                              reads=[pq, rp, rt2], writes=[rt2])
                    kb.op('dve', lambda e: e.tensor_tensor(out=qa16[:r, :, 64:96], in0=rt1[:r, :, :], in1=rt2[:r, :, :], op=ALU.add),
                          reads=[rt1, rt2, qa16], writes=[qa16])
                next(pc, None)
                pts = []
                for qi, (q0, r) in enumerate(qts):
                    qa16 = Bf[qi]['qa16']
                    pt3 = kb.ps()
                    pt3b = pt3[:, :].bitcast(BF16)
                    for h in range(4):
                        self.tr(pt3, pt3b[0:96, h * 128:h * 128 + r], qa16, qa16[:r, h, :], r)
                    pts.append((pt3, pt3b))
                for qi, (q0, r) in enumerate(qts):
                    pt3, pt3b = pts[qi]
                    kb.op('act', lambda e: e.copy(out=qaT[:, :, q0:q0 + r],
                                                  in_=pt3b[0:96, 0:512].rearrange("p (a b) -> p a b", a=4)[:, :, 0:r]),
                          reads=[pt3], writes=[qaT])
                for _ in pc:
                    pass
                kb.barrier()
            self.odeps = odeps
            kb.ps_groups.update({'acc': [0, 1, 2, 3], 'qk': [4, 5], 'misc': [6, 7]})
            with ExitStack() as des:
                st = self.dsa_open(l, des, qiT, wqs)
                def chain():
                    for t0 in range(0, len(qts), 2):
                        yield from self.dsa_pass1(l, st, t0)
                self.phase_mla(l, brT, qaT, bg=chain())
                self.dsa_rest(l, des, st, brT, bqT, True)
                kb.barrier()
            self.phase_merge(l, brT)
        self.phase_ffn(l)

    def phase_poolconv(self, l, brT):
        cfg, kb, I, O = self.cfg, self.kb, self.I, self.O
        kind, Tn, pre = self.kind, self.Tn, self.pre
        xT = self.xT
        with ExitStack() as pes:
            wpcs = [self.getw("wpc0", l), self.getw("wpc1", l, hold=1)]
            wpc3 = [w[:, 0:8 * 384].rearrange("p (k n) -> p k n", k=8) for w in wpcs]
            wsm = self.cur_wsm(l)
            pext = kb.sb("pext", [128, 2, 15 + Tn], F32, es=pes)
            cext = kb.sb("cext", [128, 2, 30 + Tn], F32, es=pes)
            sg = kb.sb("sgl", [128, 2, Tn], F32, es=pes)
            pcar, ccar = self.pcar[kind], self.ccar[kind]
            kb.op('dve', lambda e: e.tensor_copy(out=pext[:, :, 0:15], in_=pcar[:, l, :, :]), reads=[pcar], writes=[pext])
            kb.op('dve', lambda e: e.tensor_copy(out=cext[:, :, 0:30], in_=ccar[:, l, :, :]), reads=[ccar], writes=[cext])
            pss = []
            for c in range(6):
                ps = kb.ps()
                for k in range(8):
                    self.mm(ps, ps[:, 0:Tn], wpcs[c // 3], wpc3[c // 3][:, k, (c % 3) * 128:(c % 3 + 1) * 128], xT, xT[:, k, 0:Tn], k == 0, k == 7, inc=(k == 7))
                pss.append(ps)
                if c < 2:
                    kb.op('act', lambda e, c=c, ps=ps: e.copy(out=pext[:, c, 15:15 + Tn], in_=ps[:, 0:Tn]), reads=[ps, pext], writes=[pext])
                elif c >= 4:
                    kb.op('act', lambda e, c=c, ps=ps: e.activation(out=sg[:, c - 4, :], in_=ps[:, 0:Tn], func=AF.Sigmoid),
                          reads=[ps, sg], writes=[sg])
                    kb.op('dve', lambda e, c=c: e.tensor_tensor(out=cext[:, c - 4, 30:30 + Tn], in0=pss[c - 2][:, 0:Tn], in1=sg[:, c - 4, :],
                                                                op=ALU.mult), reads=[pss[c - 2], sg, cext], writes=[cext])
            yield
            kb.op('dve', lambda e: e.tensor_copy(out=pcar[:, l, :, :], in_=pext[:, :, Tn:Tn + 15]), reads=[pext, pcar], writes=[pcar])
            kb.op('dve', lambda e: e.tensor_copy(out=ccar[:, l, :, :], in_=cext[:, :, Tn:Tn + 30]), reads=[cext, ccar], writes=[ccar])
            if self.last_block:
                st = kb.sb("st_out", [32, 512], F32, es=pes)
                ps = kb.ps()
                for c in range(2):
                    self.tr(ps, ps[0:15, c * 128:(c + 1) * 128], pcar, pcar[:, l, c, :], 128, bf=False)
                    self.tr(ps, ps[0:30, 256 + c * 128:256 + (c + 1) * 128], ccar, ccar[:, l, c, :], 128, bf=False)
                kb.op('act', lambda e: e.copy(out=st[0:30, :], in_=ps[0:30, :]), reads=[ps], writes=[st])
                kb.dma('sp', O[pre + "_pool"][l], st[0:15, 0:256], reads=[st])
                kb.dma('sp', O[pre + "_conv"][l], st[0:30, 256:512], reads=[st])
            yield
            bA = kb.sb("paA", [128, 2, 15 + Tn], F32, es=pes)
            bB = kb.sb("paB", [128, 2, 15 + Tn], F32, es=pes)
            n = 15 + Tn
            pl = kb.sb("pl", [128, 2, Tn], F32, es=pes)
            pl16 = kb.sb("pl16", [128, 2, Tn], BF16, es=pes)
            first = (kind == 'p' and self.bj == 0)
            if first:
                self.icnt0 = kb.sb("icnt0", [128, 2, cfg.TB], F32, es=pes)
                kb.dma('sp', self.icnt0[:], I["icnt0"].rearrange("p (c t) -> p c t", c=2), writes=[self.icnt0])

            def pooled(g, src, wdt):
                c, lo = g // 2, (g % 2) * 64
                if first:
                    kb.op('dve', lambda e: e.tensor_tensor(out=pl[lo:lo + 64, c, :], in0=src[lo:lo + 64, c, 15:15 + Tn],
                                                           in1=self.icnt0[lo:lo + 64, c, 0:Tn], op=ALU.mult),
                          reads=[src, self.icnt0, pl], writes=[pl])
                    kb.op('dve', lambda e: e.tensor_tensor(out=pl[lo:lo + 64, c, :], in0=pl[lo:lo + 64, c, :],
                                                           in1=pext[lo:lo + 64, c, 15:15 + Tn], op=ALU.subtract),
                          reads=[pl, pext], writes=[pl])
                else:
                    kb.op('dve', lambda e: e.scalar_tensor_tensor(
                        out=pl[lo:lo + 64, c, :], in0=src[lo:lo + 64, c, 15:15 + Tn], scalar=1.0 / wdt,
                        in1=pext[lo:lo + 64, c, 15:15 + Tn], op0=ALU.mult, op1=ALU.subtract), reads=[src, pext, pl], writes=[pl])

            kb.op('dve', lambda e: e.tensor_tensor(out=bA[:, :, 1:n], in0=pext[:, :, 1:n], in1=pext[:, :, 0:n - 1], op=ALU.add),
                  reads=[pext], writes=[bA])
            kb.op('dve', lambda e: e.tensor_tensor(out=bB[:, :, 3:n], in0=bA[:, :, 3:n], in1=bA[:, :, 1:n - 2], op=ALU.add),
                  reads=[bA], writes=[bB])
            pooled(0, bA, 2)
            pooled(1, bB, 4)
            yield
            kb.op('dve', lambda e: e.tensor_tensor(out=bA[:, :, 7:n], in0=bB[:, :, 7:n], in1=bB[:, :, 3:n - 4], op=ALU.add),
                  reads=[bB, bA], writes=[bA])
            kb.op('dve', lambda e: e.tensor_tensor(out=bB[:, :, 15:n], in0=bA[:, :, 15:n], in1=bA[:, :, 7:n - 8], op=ALU.add),
                  reads=[bA, bB], writes=[bB])
            pooled(2, bA, 8)
            pooled(3, bB, 16)
            yield
            kb.op('dve', lambda e: e.tensor_copy(out=pl16[:], in_=pl[:]), reads=[pl], writes=[pl16])
            pvc = self.pv_c
            for c in range(2):
                ps = kb.ps()
                self.mm(ps, ps[:, 0:Tn], wsm, wsm[:, 1280 + c * 128:1280 + (c + 1) * 128], pl16, pl16[:, c, :], True, True)
                kb.op('act', lambda e, c=c, ps=ps: e.activation(out=brT[:, 4 + c, 0:Tn], in_=ps[:, 0:Tn], func=AF.Copy,
                                                                scale=pvc[:, l, c, 34:35]), reads=[ps, pvc, brT], writes=[brT])
            yield
            cacc = kb.sb("cacc", [128, 2, Tn], F32, es=pes)
            cext16 = kb.sb("cext16", [128, 2, 30 + Tn], BF16, es=pes)
            kb.op('act', lambda e: e.copy(out=cext16[:], in_=cext[:]), reads=[cext], writes=[cext16])
            for c in range(2):
                dg = kb.sb("dg%d" % c, [128, 31, 128], BF16, es=pes)
                for k in range(31):
                    kb.op('dve', lambda e: e.tensor_scalar(out=dg[:, k, :], in0=self.ident16[:, :], scalar1=pvc[:, l, c, k:k + 1],
                                                            scalar2=0.0, op0=ALU.mult, op1=ALU.add),
                          reads=[self.ident16, pvc, dg], writes=[dg])
                yield
                ps = kb.ps()
                for k in range(31):
                    self.mm(ps, ps[:, 0:Tn], dg, dg[:, k, :], cext16, cext16[:, c, k:k + Tn], k == 0, k == 30, inc=(k == 30))
                kb.op('act', lambda e: e.activation(out=cacc[:, c, :], in_=ps[:, 0:Tn], func=AF.Identity, bias=pvc[:, l, c, 31:32]),
                      reads=[ps, pvc, cacc], writes=[cacc])
            yield
            self.lnfm(pes, cacc, 2, Tn, self.ones256, lambda c: pvc[:, l, c, 32:33], lambda c: pvc[:, l, c, 33:34], pvc,
                      [(brT, lambda c: brT[:, 6 + c, 0:Tn])], func=AF.Silu)
            kb.barrier()

    def cur_wsm(self, l):
        for i in range(self.wpos - 1, -1, -1):
            if self.wplan[i][0] == "wsm":
                return self.wslot[i]
        raise AssertionError

    def key_supers(self):
        sups, cur = [], []
        for kt in self.ktiles:
            if kt[2] < 128 or (cur and cur[0][0] != kt[0]):
                if cur:
                    sups.append(cur)
                    cur = []
            cur.append(kt)
            if len(cur) == 4 or kt[2] < 128:
                sups.append(cur)
                cur = []
        if cur:
            sups.append(cur)
        return sups

    def ksrc(self, name, l, sup):
        I, O = self.I, self.O
        kindsrc, r0, kr, _ = sup[0]
        nrows = sum(t[2] for t in sup)
        if kindsrc == 'c':
            ap = I["c_" + name][l, r0:r0 + nrows, :]
            dep = []
        else:
            ap = O[self.pre + "_" + name][l, r0:r0 + nrows, :]
            dep = [self.odeps[name]]
        if kr == 128:
            return ap.rearrange("(t p) f -> p t f", p=128), dep, 128
        return ap.rearrange("(t p) f -> p t f", t=1), dep, kr

    def vis(self, abs_pos, kr):
        if self.kind == 's':
            return 0, None
        a = (abs_pos - self.P) // 128
        if a < 0:
            return 0, None
        return a * 128, a

    def phase_mla(self, l, brT, qaT, bg=None):
        cfg, kb = self.cfg, self.kb
        Tn, qts = self.Tn, self.qts
        NQ = len(qts)
        wn, wv = self.wn, self.wv
        with ExitStack() as pes:
            accs = [kb.ps('acc') for _ in range(4)]
            first_acc = [True] * 4
            kin16 = [kb.sb("kinA16_%d" % i, [128, 4, 160], BF16, es=pes) for i in range(2)]
            kin2 = [T(k.t) for k in kin16]
            ckvT = [kb.sb("ckvT%d" % i, [128, 512], BF16, es=pes) for i in range(2)]
            krT = [kb.sb("krT%d" % i, [32, 512], BF16, es=pes) for i in range(2)]
            KA = [kb.sb("KA%d" % i, [96, 4, 512], BF16, es=pes) for i in range(2)]
            VA = [kb.sb("VA%d" % i, [128, 4, 4, 65], BF16, es=pes) for i in range(2)]
            PT = [kb.sb("PTa%d" % i, [128, 512], BF16, es=pes) for i in range(4)]
            for v in VA:
                kb.op('dve', lambda e, v=v: e.memset(v[:, :, :, 64:65], 1.0), writes=[v])
            pti = 0
            pend = []
            for si, sup in enumerate(self.key_supers()):
                b = si % 2
                nt = len(sup)
                a1, d1, kr = self.ksrc("ckv", l, sup)
                a2, d2, _ = self.ksrc("krope", l, sup)
                nk = sum(t[2] for t in sup)
                kb.dma('pool', kin16[b][:kr, 0:nt, 0:128], a1, reads=d1, writes=[kin16[b]])
                kb.dma('pool', kin16[b][:kr, 0:nt, 128:160], a2, reads=d2, writes=[kin2[b]])
                pt = kb.ps('misc')
                ptb = pt[:, :].bitcast(BF16)
                for t in range(nt):
                    self.tr(pt, ptb[:, t * 128:t * 128 + kr], kin16[b], kin16[b][:kr, t, 0:128], kr)
                    self.tr(pt, ptb[0:32, 512 + t * 128:512 + t * 128 + kr], kin2[b], kin16[b][:kr, t, 128:160], kr)
                kb.op('act', lambda e: e.copy(out=ckvT[b][:, 0:nk], in_=ptb[:, 0:nk]), reads=[pt], writes=[ckvT[b]])
                kb.op('act', lambda e: e.copy(out=krT[b][:, 0:nk], in_=ptb[0:32, 512:512 + nk]), reads=[pt], writes=[krT[b]])
                for h in range(4):
                    pk = kb.ps('misc')
                    self.mm(pk, pk[0:96, 0:nk], wn, wn[:, h, :], ckvT[b], ckvT[b][:, 0:nk], True, False)
                    self.mm(pk, pk[0:96, 0:nk], self.esel, self.esel[:, :], krT[b], krT[b][:, 0:nk], False, True)
                    kb.op('act', lambda e, h=h, pk=pk: e.copy(out=KA[b][:, h, 0:nk], in_=pk[0:96, 0:nk]), reads=[pk, KA[b]], writes=[KA[b]])
                for t in range(nt):
                    pvp = kb.ps('misc')
                    self.mm(pvp, pvp[:kr, 0:256], ckvT[b], ckvT[b][:, t * 128:t * 128 + kr], wv, wv[:, :], True, True)
                    kb.op('act', lambda e, t=t, pvp=pvp: e.copy(out=VA[b][:kr, t, :, 0:64],
                                                                in_=pvp[:kr, 0:256].rearrange("p (h c) -> p h c", h=4)),
                          reads=[pvp, VA[b]], writes=[VA[b]])
                for t, (_, _, kr_t, apos) in enumerate(sup):
                    qlo, diag = self.vis(apos, kr_t)
                    if qlo >= Tn:
                        continue
                    nq = Tn - qlo
                    for h in range(4):
                        psq = kb.ps('qk')
                        self.mm(psq, psq[:kr_t, 0:nq], KA[b], KA[b][:, h, t * 128:t * 128 + kr_t], qaT, qaT[:, h, qlo:Tn], True, True)
                        p = PT[pti % len(PT)]
                        pti += 1
                        kb.op('act', lambda e: e.activation(out=p[:kr_t, 0:nq], in_=psq[:kr_t, 0:nq], func=AF.Exp, scale=A_SCALE),
                              reads=[psq], writes=[p])
                        if diag is not None:
                            kb.op('pool', lambda e: e.memset(p[64:128, 0:64], 0.0), reads=[p], writes=[p])

                        def pv(p=p, h=h, t=t, b=b, kr_t=kr_t, qlo=qlo):
                            vq = [(qi, q0, r) for qi, (q0, r) in enumerate(qts) if q0 >= qlo]
                            for j, (qi, q0, r) in enumerate(vq):
                                self.mm(accs[h], accs[h][:r, qi * 65:qi * 65 + 65], p, p[:kr_t, q0 - qlo:q0 - qlo + r],
                                        VA[b], VA[b][:kr_t, t, h, :], first_acc[h], True, inc=(j == len(vq) - 1), skip_group_check=True)
                                first_acc[h] = False
                        pend.append(pv)
                        if len(pend) > 1:
                            pend.pop(0)()
                        if bg is not None:
                            next(bg, None)
            while pend:
                pend.pop(0)()
            if bg is not None:
                for _ in bg:
                    pass
            self.attn_finish(pes, accs, brT, 0)
            kb.barrier()

    def attn_finish(self, pes, accs, brT, chunk0, qsel=None):
        kb = self.kb
        qts = self.qts if qsel is None else qsel
        recs = [kb.sb("rec%d" % i, [128, 4], F32, es=pes) for i in range(2)]
        obs = [kb.sb("ob%d" % i, [128, 256], BF16, es=pes) for i in range(2)]
        for gi, (q0, r) in enumerate(qts):
            rec, ob = recs[gi % 2], obs[gi % 2]
            for h in range(4):
                kb.op('dve', lambda e, h=h: e.reciprocal(out=rec[:r, h:h + 1], in_=accs[h][:r, gi * 65 + 64:gi * 65 + 65]),
                      reads=[accs[h], rec], writes=[rec])
                kb.op('act', lambda e, h=h: e.activation(out=ob[:r, h * 64:(h + 1) * 64], in_=accs[h][:r, gi * 65:gi * 65 + 64],
                                                         func=AF.Copy, scale=rec[:r, h:h + 1]), reads=[accs[h], rec, ob], writes=[ob])
            pt = kb.ps('misc')
            ptb = pt[:, :].bitcast(BF16)
            for c in range(2):
                self.tr(pt, ptb[:, c * 128:c * 128 + r], ob, ob[:r, c * 128:(c + 1) * 128], r)
            kb.op('act', lambda e: e.copy(out=brT[:, chunk0:chunk0 + 2, q0:q0 + r],
                                          in_=ptb[:, 0:256].rearrange("p (a b) -> p a b", a=2)[:, :, 0:r]),
                  reads=[pt, brT], writes=[brT])

    def dsa_open(self, l, pes, qiT, wqs):
        cfg, kb = self.cfg, self.kb
        Tn, qts, P, kind = self.Tn, self.qts, self.P, self.kind
        NK = sum(t[2] for t in self.ktiles)
        sups = self.key_supers()
        st = {}
        kidxT = kb.sb("kidxT", [32, NK], BF16, es=pes)
        kiin16 = [kb.sb("kiin16_%d" % i, [128, 4, 32], BF16, es=pes) for i in range(2)]
        k0 = 0
        koffs = []
        for si, sup in enumerate(sups):
            b = si % 2
            nt = len(sup)
            a1, d1, kr = self.ksrc("kidx", l, sup)
            nk = sum(t[2] for t in sup)
            kb.dma('pool', kiin16[b][:kr, 0:nt, :], a1, reads=d1, writes=[kiin16[b]])
            pt = kb.ps('misc')
            ptb = pt[:, :].bitcast(BF16)
            for t in range(nt):
                self.tr(pt, ptb[0:32, t * 128:t * 128 + kr], kiin16[b], kiin16[b][:kr, t, :], kr)
            kb.op('act', lambda e: e.copy(out=kidxT[:, k0:k0 + nk], in_=ptb[0:32, 0:nk]), reads=[pt, kidxT], writes=[kidxT])
            koffs.append(k0)
            k0 += nk
        GQ = cfg.GQ
        iscs = [kb.sb("isc%d" % i, [128, NK], F32, es=pes) for i in range(2)]
        mask = kb.sb("mask", [128, GQ, NK], BF16, es=pes)
        mask1 = T(mask.t)
        mtok = [mask, mask1] + [T(mask.t) for _ in range(max(0, GQ - 2))]
        rl = [kb.sb("rl%d" % i, [128, 512], F32, es=pes) for i in range(2)]
        bss = [kb.sb("bs%d" % i, [128, 8 + NIT], F32, es=pes) for i in range(2)]
        st.update(dict(NK=NK, sups=sups, koffs=koffs, kidxT=kidxT, iscs=iscs, mask=mask, mask1=mask1, rl=rl, bss=bss,
                       qiT=qiT, wqs=wqs, GQ=GQ, rli=0, mtok=mtok))
        return st

    def dsa_pass1(self, l, st, g0):
        cfg, kb = self.cfg, self.kb
        Tn, qts, P, kind = self.Tn, self.qts, self.P, self.kind
        NK, sups, koffs, kidxT, iscs, mask, mask1, rl, bss, qiT, wqs, GQ = (st[k] for k in (
            "NK", "sups", "koffs", "kidxT", "iscs", "mask", "mask1", "rl", "bss", "qiT", "wqs", "GQ"))
        rli = st["rli"]
        grp = qts[g0:g0 + 2]
        mtok = st["mtok"]
        ms0 = g0
        on_act = lambda gi: (gi == 1) or (g0 >= 2)
        qc0 = grp[0][0]
        qc1 = grp[-1][0] + grp[-1][1]
        nvs = []
        for gi, (q0, r) in enumerate(grp):
            qi = g0 + gi
            isc = iscs[gi % 2]
            nvis = (P + q0 + 128) if kind == 'p' else NK
            nvs.append(nvis)
            for si, sup in enumerate(sups):
                kk0 = koffs[si]
                if kk0 >= nvis:
                    break
                nk = min(sum(t[2] for t in sup), nvis - kk0)
                for h in range(4):
                    psi = kb.ps('qk')
                    self.mm(psi, psi[:r, 0:nk], qiT, qiT[:, h, q0:q0 + r], kidxT, kidxT[:, kk0:kk0 + nk], True, True)
                    if h == 0:
                        kb.op('dve', lambda e: e.tensor_scalar(out=isc[:r, kk0:kk0 + nk], in0=psi[:r, 0:nk], scalar1=0.0,
                                                               scalar2=wqs[:r, qi, 0:1], op0=ALU.max, op1=ALU.mult),
                              reads=[psi, wqs, isc], writes=[isc])
                    else:
                        rr = rl[rli % 2]
                        rli += 1
                        kb.op('act', lambda e: e.activation(out=rr[:r, 0:nk], in_=psi[:r, 0:nk], func=AF.Relu),
                              reads=[psi], writes=[rr])
                        kb.op('dve', lambda e: e.scalar_tensor_tensor(out=isc[:r, kk0:kk0 + nk], in0=rr[:r, 0:nk],
                                                                      scalar=wqs[:r, qi, h:h + 1], in1=isc[:r, kk0:kk0 + nk],
                                                                      op0=ALU.mult, op1=ALU.add),
                              reads=[rr, wqs, isc], writes=[isc])
                yield
            bs = bss[gi % 2]
            kb.op('dve', lambda e: e.memset(bs[:], 0.0), writes=[bs])
            kb.op('dve', lambda e: e.tensor_reduce(out=bs[:r, 0:1], in_=isc[:r, 0:nvis], axis=AX.X, op=ALU.max), reads=[isc, bs], writes=[bs])
            kb.op('dve', lambda e: e.tensor_reduce(out=bs[:r, 1:2], in_=isc[:r, 0:nvis], axis=AX.X, op=ALU.min), reads=[isc, bs], writes=[bs])
            if kind == 'p':
                kb.op('dve', lambda e: e.memset(isc[0:64, nvis - 64:nvis], NEG_BIG), reads=[isc], writes=[isc])
            kb.op('dve', lambda e: e.tensor_tensor(out=bs[:r, 2:3], in0=bs[:r, 0:1], in1=bs[:r, 1:2], op=ALU.subtract), reads=[bs], writes=[bs])
            if on_act(gi):
                kb.op('dve', lambda e: e.tensor_scalar(out=bs[:r, 1:2], in0=bs[:r, 1:2], scalar1=-1.0, scalar2=0.0, op0=ALU.mult, op1=ALU.add),
                      reads=[bs], writes=[bs])
        for it in range(1, NIT + 1):
            f = 2.0 ** -it
            yield
            act_t = [gi for gi in range(len(grp)) if on_act(gi)]
            dve_t = [gi for gi in range(len(grp)) if not on_act(gi)]
            for gi in act_t:
                (q0, r), isc, bs, nvis = grp[gi], iscs[gi % 2], bss[gi % 2], nvs[gi]
                kb.op('dve', lambda e: e.scalar_tensor_tensor(out=bs[:r, 3:4], in0=bs[:r, 2:3], scalar=-f, in1=bs[:r, 1:2],
                                                              op0=ALU.mult, op1=ALU.add), reads=[bs], writes=[bs])
                kb.op('act', lambda e: e.activation(out=mask[:r, ms0 + gi, 0:nvis], in_=isc[:r, 0:nvis], func=AF.Sign, bias=bs[:r, 3:4], scale=1.0,
                                                    accum_out=bs[:r, 7 + it:8 + it]), reads=[isc, bs, mtok[ms0 + gi]], writes=[mtok[ms0 + gi], bs])
            for gi in dve_t:
                (q0, r), isc, bs, nvis = grp[gi], iscs[gi % 2], bss[gi % 2], nvs[gi]
                kb.op('dve', lambda e: e.scalar_tensor_tensor(out=bs[:r, 3:4], in0=bs[:r, 2:3], scalar=f, in1=bs[:r, 1:2],
                                                              op0=ALU.mult, op1=ALU.add), reads=[bs], writes=[bs])
                kb.op('dve', lambda e: e.tensor_scalar(out=mask[:r, ms0 + gi, 0:nvis], in0=isc[:r, 0:nvis], scalar1=bs[:r, 3:4], scalar2=0.0,
                                                       op0=ALU.is_ge, op1=ALU.add, accum_out=bs[:r, 7 + it:8 + it]),
                      reads=[isc, bs, mtok[ms0 + gi]], writes=[mtok[ms0 + gi], bs])
                kb.op('dve', lambda e: e.tensor_scalar(out=bs[:r, 4:5], in0=bs[:r, 7 + it:8 + it], scalar1=float(self.ktop) - 0.5,
                                                       scalar2=f, op0=ALU.is_ge, op1=ALU.mult), reads=[bs], writes=[bs])
                kb.op('dve', lambda e: e.scalar_tensor_tensor(out=bs[:r, 1:2], in0=bs[:r, 4:5], scalar=bs[:r, 2:3], in1=bs[:r, 1:2],
                                                              op0=ALU.mult, op1=ALU.add), reads=[bs], writes=[bs])
            for gi in act_t:
                (q0, r), isc, bs, nvis = grp[gi], iscs[gi % 2], bss[gi % 2], nvs[gi]
                kb.op('dve', lambda e: e.tensor_scalar(out=bs[:r, 4:5], in0=bs[:r, 7 + it:8 + it], scalar1=2.0 * self.ktop - nvis - 0.5,
                                                       scalar2=-f, op0=ALU.is_ge, op1=ALU.mult), reads=[bs], writes=[bs])
                kb.op('dve', lambda e: e.scalar_tensor_tensor(out=bs[:r, 1:2], in0=bs[:r, 4:5], scalar=bs[:r, 2:3], in1=bs[:r, 1:2],
                                                              op0=ALU.mult, op1=ALU.add), reads=[bs], writes=[bs])
        for gi, (q0, r) in enumerate(grp):
            isc, bs, nvis = iscs[gi % 2], bss[gi % 2], nvs[gi]
            mk = mtok[ms0 + gi]
            if on_act(gi):
                kb.op('dve', lambda e: e.tensor_scalar(out=bs[:r, 5:6], in0=bs[:r, 1:2], scalar1=-1.0, scalar2=0.0, op0=ALU.mult, op1=ALU.add),
                      reads=[bs], writes=[bs])
                tcol = bs[:r, 5:6]
            else:
                tcol = bs[:r, 1:2]
            kb.op('dve', lambda e: e.tensor_scalar(out=mask[:r, ms0 + gi, 0:nvis], in0=isc[:r, 0:nvis], scalar1=tcol, scalar2=1.0,
                                                   op0=ALU.is_ge, op1=ALU.mult), reads=[isc, bs, mk], writes=[mk])
        st["rli"] = rli
        yield

    def dsa_rest(self, l, pes, st, brT, bqT, first_done):
        cfg, kb = self.cfg, self.kb
        Tn, qts, P, kind = self.Tn, self.qts, self.P, self.kind
        NK, sups, koffs, mask, mask1, GQ, mtok = (st[k] for k in ("NK", "sups", "koffs", "mask", "mask1", "GQ", "mtok"))
        bk16 = [kb.sb("bk16_%d" % i, [128, 4, 256], BF16, es=pes) for i in range(2)]
        KBt = [kb.sb("KBt%d" % i, [128, 2, 512], BF16, es=pes) for i in range(2)]
        VB = [kb.sb("VB%d" % i, [128, 4, 4, 65], BF16, es=pes) for i in range(2)]
        Eb = [kb.sb("Eb%d" % i, [128, 512], BF16, es=pes) for i in range(2)]
        tb = [kb.sb("tbias%d" % i, [128, 260], F32, es=pes) for i in range(2)]
        PT = [kb.sb("PTb%d" % i, [128, 512], BF16, es=pes) for i in range(4)]
        self.fin_bufs = [(kb.sb("recb%d" % i, [128, 4], F32, es=pes), kb.sb("obb%d" % i, [128, 256], BF16, es=pes)) for i in range(2)]
        for v in VB:
            kb.op('dve', lambda e, v=v: e.memset(v[:, :, :, 64:65], 1.0), writes=[v])
        VBd = [[T(v.t) for _ in range(4)] for v in VB]
        for g0 in range(0, len(qts), GQ):
            grp = qts[g0:g0 + GQ]
            qc0 = grp[0][0]
            qc1 = grp[-1][0] + grp[-1][1]
            accs = [kb.ps('acc') for _ in range(4)]
            first_acc = [True] * 4
            pti = 0
            pend = []
            nvis_g = (P + qc1) if kind == 'p' else NK
            for si, sup in enumerate(sups):
                kk0 = koffs[si]
                if kk0 >= nvis_g:
                    break
                b = si % 2
                nt = len(sup)
                a1, d1, kr = self.ksrc("bk", l, sup)
                a2, d2, _ = self.ksrc("bv", l, sup)
                kb.dma('pool', bk16[b][:kr, 0:nt, :], a1, reads=d1, writes=[bk16[b]])
                for t in range(nt):
                    kb.dma('pool', VB[b][:kr, t, :, 0:64], a2[:, t, :].rearrange("p (h c) -> p h c", h=4), reads=d2, writes=[VBd[b][t]])
                pt = kb.ps('misc')
                ptb = pt[:, :].bitcast(BF16)
                for t in range(nt):
                    for pr in range(2):
                        self.tr(pt, ptb[:, pr * 512 + t * 128:pr * 512 + t * 128 + kr], bk16[b], bk16[b][:kr, t, pr * 128:(pr + 1) * 128], kr)
                nk = sum(t[2] for t in sup)
                kb.op('act', lambda e: e.copy(out=KBt[b][:, :, 0:nk], in_=ptb[:, :].rearrange("p (a b) -> p a b", a=2)[:, :, 0:nk]),
                      reads=[pt], writes=[KBt[b]])
                for t, (_, _, kr_t, apos) in enumerate(sup):
                    qlo, diag = self.vis(apos, kr_t)
                    qlo = max(qlo, qc0)
                    if qlo >= qc1:
                        continue
                    nq = qc1 - qlo
                    kcol = kk0 + t * 128
                    pm = kb.ps('misc')
                    pmb = pm[:, :].bitcast(BF16)
                    for gi, (q0, r) in enumerate(grp):
                        if q0 < qlo:
                            continue
                        self.tr(pm, pmb[:kr_t, q0 - qc0:q0 - qc0 + r], mtok[gi], mask[:r, gi, kcol:kcol + kr_t], r)
                    a = (apos - P) // 128
                    blo = max(qlo, 128 * a) if a >= -1 else None
                    bhi = min(qc1, 128 * a + 257) if a >= -1 else None
                    for h in range(4):
                        hp, ho = h // 2, (h % 2) * 64
                        psq = kb.ps('qk')
                        self.mm(psq, psq[:kr_t, 0:nq], KBt[b], KBt[b][ho:ho + 64, hp, t * 128:t * 128 + kr_t],
                                bqT, bqT[ho:ho + 64, hp, qlo:qc1], True, True)
                        E = Eb[pti % len(Eb)]
                        p = PT[pti % len(PT)]
                        tbt = tb[pti % len(tb)]
                        pti += 1
                        if blo is not None and blo < bhi:
                            wdt = bhi - blo
                            kb.op('dve', lambda e: e.scalar_tensor_tensor(
                                out=tbt[:kr_t, 0:wdt], in0=psq[:kr_t, blo - qlo:bhi - qlo], scalar=B_SCALE,
                                in1=self.gt[:kr_t, h, blo - 128 * a:bhi - 128 * a], op0=ALU.mult, op1=ALU.add),
                                reads=[psq, self.gt, tbt], writes=[tbt])
                            kb.op('act', lambda e: e.activation(out=E[:kr_t, blo - qlo:bhi - qlo], in_=tbt[:kr_t, 0:wdt],
                                                                func=AF.Exp), reads=[tbt, E], writes=[E])
                            for (c0, c1) in ((qlo, blo), (bhi, qc1)):
                                if c1 > c0:
                                    kb.op('act', lambda e: e.activation(
                                        out=E[:kr_t, c0 - qlo:c1 - qlo], in_=psq[:kr_t, c0 - qlo:c1 - qlo], func=AF.Exp, scale=B_SCALE),
                                        reads=[psq, E], writes=[E])
                        else:
                            kb.op('act', lambda e: e.activation(out=E[:kr_t, 0:nq], in_=psq[:kr_t, 0:nq], func=AF.Exp, scale=B_SCALE),
                                  reads=[psq, E], writes=[E])
                        kb.op('dve', lambda e: e.tensor_tensor(out=p[:kr_t, 0:nq], in0=E[:kr_t, 0:nq],
                                                               in1=pmb[:kr_t, qlo - qc0:qc1 - qc0], op=ALU.mult),
                              reads=[E, pm, p], writes=[p])

                        def pv(p=p, h=h, hp=hp, t=t, b=b, kr_t=kr_t, qlo=qlo):
                            vq = [(gi, q0, r) for gi, (q0, r) in enumerate(grp) if q0 >= qlo]
                            for j, (gi, q0, r) in enumerate(vq):
                                col = gi * 65
                                self.mm(accs[h], accs[h][:r, col:col + 65], p, p[:kr_t, q0 - qlo:q0 - qlo + r],
                                        VBd[b][t], VB[b][:kr_t, t, h, :], first_acc[h], True, inc=(j == len(vq) - 1), skip_group_check=True)
                                first_acc[h] = False
                        pend.append(pv)
                        if len(pend) > 1:
                            pend.pop(0)()
            while pend:
                pend.pop(0)()
            self.dsa_finish(pes, accs, brT, grp, GQ)

    def dsa_finish(self, pes, accs, brT, grp, GQ):
        kb = self.kb
        if not hasattr(self, "_dsa_fin"):
            self._dsa_fin = 0
        for gi, (q0, r) in enumerate(grp):
            rec, ob = self.fin_bufs[gi % 2]
            for h in range(4):
                a = accs[h]
                col = gi * 65
                kb.op('dve', lambda e, h=h, a=a, col=col: e.reciprocal(out=rec[:r, h:h + 1], in_=a[:r, col + 64:col + 65]),
                      reads=[a, rec], writes=[rec])
                kb.op('act', lambda e, h=h, a=a, col=col: e.activation(out=ob[:r, h * 64:(h + 1) * 64], in_=a[:r, col:col + 64],
                                                                       func=AF.Copy, scale=rec[:r, h:h + 1]), reads=[a, rec, ob], writes=[ob])
            pt = kb.ps('misc')
            ptb = pt[:, :].bitcast(BF16)
            for c in range(2):
                self.tr(pt, ptb[:, c * 128:c * 128 + r], ob, ob[:r, c * 128:(c + 1) * 128], r)
            kb.op('act', lambda e: e.copy(out=brT[:, 2:4, q0:q0 + r],
                                          in_=ptb[:, 0:256].rearrange("p (a b) -> p a b", a=2)[:, :, 0:r]),
                  reads=[pt, brT], writes=[brT])
        self._dsa_fin += 1

    def phase_merge(self, l, brT):
        cfg, kb = self.cfg, self.kb
        Tn = self.Tn
        x, xT = self.x, self.xT
        with ExitStack() as pes:
            mixed = kb.sb("mixed", [128, 8, Tn], F32, es=pes)
            mixed16 = kb.sb("mixed16", [128, 8, Tn], BF16, es=pes)
            sgs = [kb.sb("sgm%d" % i, [128, Tn], F32, es=pes) for i in range(2)]
            tmp = [kb.sb("tmpm%d" % i, [128, Tn], F32, es=pes) for i in range(2)]
            i = 0
            for hf in range(2):
                for n in range(4):
                    wbr = self.getw("wbr%d%d" % (n, hf), l)
                    wbr3 = wbr[:, 0:1024].rearrange("p (k n) -> p k n", k=2)
                    wg = self.getw("wg%d%d" % (n, hf), l, hold=1)
                    wg3 = wg[:, :].rearrange("p (k n) -> p k n", k=8)
                    for dl in range(4):
                        dc = hf * 4 + dl
                        pg = kb.ps()
                        for k in range(8):
                            self.mm(pg, pg[:, 0:Tn], wg, wg3[:, k, dl * 128:(dl + 1) * 128], xT, xT[:, k, 0:Tn], k == 0, k == 7, inc=(k == 7))
                        pb = kb.ps()
                        for k in range(2):
                            self.mm(pb, pb[:, 0:Tn], wbr, wbr3[:, k, dl * 128:(dl + 1) * 128], brT, brT[:, n * 2 + k, 0:Tn], k == 0, k == 1, inc=(k == 1))
                        sg = sgs[i % 2]
                        tm = tmp[i % 2]
                        i += 1
                        kb.op('act', lambda e: e.activation(out=sg[:, :], in_=pg[:, 0:Tn], func=AF.Sigmoid), reads=[pg], writes=[sg])
                        if n == 0:
                            kb.op('dve', lambda e: e.tensor_tensor(out=mixed[:, dc, :], in0=pb[:, 0:Tn], in1=sg[:, :], op=ALU.mult),
                                  reads=[pb, sg, mixed], writes=[mixed])
                        else:
                            kb.op('dve', lambda e: e.tensor_tensor(out=tm[:, :], in0=pb[:, 0:Tn], in1=sg[:, :], op=ALU.mult),
                                  reads=[pb, sg], writes=[tm])
                            if n < 3:
                                kb.op('dve', lambda e: e.tensor_tensor(out=mixed[:, dc, :], in0=mixed[:, dc, :], in1=tm[:, :], op=ALU.add),
                                      reads=[tm, mixed], writes=[mixed])
                            else:
                                kb.op('dve', lambda e: e.tensor_tensor(out=mixed16[:, dc, :], in0=mixed[:, dc, :], in1=tm[:, :], op=ALU.add),
                                      reads=[tm, mixed, mixed16], writes=[mixed16])
            s = kb.sb("s_ln1", [128, 8, Tn], F32, es=pes)
            for hf in range(2):
                wo = self.getw("wout%d" % hf, l)
                wo3 = wo[:, :].rearrange("p (k n) -> p k n", k=8)
                for dl in range(4):
                    dc = hf * 4 + dl
                    py = kb.ps()
                    for k in range(8):
                        self.mm(py, py[:, 0:Tn], wo, wo3[:, k, dl * 128:(dl + 1) * 128], mixed16, mixed16[:, k, :], k == 0, k == 7, inc=(k == 7))
                    kb.op('dve', lambda e: e.scalar_tensor_tensor(out=s[:, dc, :], in0=x[:, dc, 0:Tn], scalar=cfg.ALPHA, in1=py[:, 0:Tn],
                                                                  op0=ALU.mult, op1=ALU.add), reads=[x, py, s], writes=[s])
            pv = self.pv_ln
            self.lnfm(pes, s, 8, Tn, self.ones1024, lambda c: pv[:, l, 0, c:c + 1], lambda c: pv[:, l, 1, c:c + 1], pv,
                      [(x, lambda c: x[:, c, 0:Tn]), (xT, lambda c: xT[:, c, 0:Tn])])
            kb.barrier()

    def phase_ffn(self, l):
        cfg, kb, I, O = self.cfg, self.kb, self.I, self.O
        Tn, kind, pre, row0, qts = self.Tn, self.kind, self.pre, self.row0, self.qts
        x, xT = self.x, self.xT
        fcar = self.fcar[kind]
        fcv = fcar[:, l, :, :].rearrange("p (g c) t -> p g c t", g=2)
        pvf = self.pv_f
        with ExitStack() as pes:
            hT = kb.sb("hT", [128, NFC, Tn], BF16, es=pes)
            Es = [kb.sb("Effn%d" % i, [128, 2, 2 + Tn], F32, es=pes) for i in range(3)]
            av = [kb.sb("avf%d" % i, [128, 2, Tn], F32, es=pes) for i in range(3)]
            sgl = [kb.sb("sgf%d" % i, [128, Tn], F32, es=pes) for i in range(3)]
            Ecs = [T(E_.t) for E_ in Es]
            ci = 0
            tails = []
            for g in range(6):
                nch = 4 if g < 5 else 2
                wvv = self.getw("wupv%d" % g, l)
                wgg = self.getw("wupg%d" % g, l, hold=1)
                wts = (wvv, wgg)
                w3s = [w_[:, 0:8 * nch * 128].rearrange("p (k n) -> p k n", k=8) for w_ in wts]
                for cl in range(nch):
                    c = g * 4 + cl
                    E = Es[ci % 3]
                    a = av[ci % 3]
                    sg = sgl[ci % 3]
                    ci += 1
                    Ec = Ecs[(ci - 1) % 3]
                    kb.op('dve', lambda e, E=E, c=c: e.tensor_copy(out=E[:, :, 0:2], in_=fcv[:, :, c, :]), reads=[fcar], writes=[Ec])
                    for gg in range(2):
                        cc = gg * NFC + c
                        ps = kb.ps()
                        for k in range(8):
                            self.mm(ps, ps[:, 0:Tn], wts[gg], w3s[gg][:, k, cl * 128:(cl + 1) * 128], xT, xT[:, k, 0:Tn], k == 0, k == 7, inc=(k == 7))
                        kb.op('act', lambda e: e.copy(out=E[:, gg, 2:2 + Tn], in_=ps[:, 0:Tn]), reads=[ps, E], writes=[E])
                        kb.op('act', lambda e: e.activation(out=a[:, gg, :], in_=ps[:, 0:Tn], func=AF.Identity, scale=pvf[:, l, cc, 2:3],
                                                            bias=pvf[:, l, cc, 3:4]), reads=[ps, pvf, a], writes=[a])
                    kb.op('dve', lambda e: e.tensor_copy(out=fcv[:, :, c, :], in_=E[:, :, Tn:Tn + 2]), reads=[E, fcar], writes=[fcar])
                    for gg in range(2):
                        cc = gg * NFC + c
                        for k in (1, 0):
                            kb.op('dve', lambda e: e.scalar_tensor_tensor(
                                out=a[:, gg, :], in0=E[:, gg, k:k + Tn], scalar=pvf[:, l, cc, k:k + 1], in1=a[:, gg, :], op0=ALU.mult, op1=ALU.add),
                                reads=[E, Ec, pvf, a], writes=[a])
                    def tail(a=a, sg=sg, c=c):
                        kb.op('act', lambda e: e.activation(out=sg[:, :], in_=a[:, 1, :], func=AF.Silu), reads=[a], writes=[sg])
                        kb.op('dve', lambda e: e.tensor_tensor(out=hT[:, c, :], in0=a[:, 0, :], in1=sg[:, :], op=ALU.mult),
                              reads=[a, sg, hT], writes=[hT])
                    tails.append(tail)
                    if len(tails) > 1:
                        tails.pop(0)()
            while tails:
                tails.pop(0)()
            if self.last_block:
                sts = [kb.sb("stf%d" % i, [2, 512], F32, es=pes) for i in range(2)]
                for c0 in range(0, 44, 4):
                    st = sts[(c0 // 4) % 2]
                    ps = kb.ps()
                    for i in range(4):
                        self.tr(ps, ps[0:2, i * 128:(i + 1) * 128], fcar, fcar[:, l, c0 + i, :], 128, bf=False)
                    kb.op('act', lambda e: e.copy(out=st[0:2, :], in_=ps[0:2, :]), reads=[ps, st], writes=[st])
                    kb.dma('sp', O[pre + "_ffn"][l, :, c0 * 128:(c0 + 4) * 128], st[0:2, :], reads=[st])
            s = kb.sb("s_ln2", [128, 8, Tn], F32, es=pes)
            for dc in range(8):
                w = self.getw("wdn%d" % dc, l)
                w3 = w[:, 0:NFC * 128].rearrange("p (k n) -> p k n", k=NFC)
                py = kb.ps()
                for k in range(NFC):
                    self.mm(py, py[:, 0:Tn], w, w3[:, k, :], hT, hT[:, k, :], k == 0, k == NFC - 1, inc=(k == NFC - 1))
                kb.op('dve', lambda e: e.scalar_tensor_tensor(out=s[:, dc, :], in0=x[:, dc, 0:Tn], scalar=cfg.ALPHA, in1=py[:, 0:Tn],
                                                              op0=ALU.mult, op1=ALU.add), reads=[x, py, s], writes=[s])
            pv = self.pv_ln
            self.lnfm(pes, s, 8, Tn, self.ones1024, lambda c: pv[:, l, 2, c:c + 1], lambda c: pv[:, l, 3, c:c + 1], pv,
                      [(x, lambda c: x[:, c, 0:Tn]), (xT, lambda c: xT[:, c, 0:Tn])])
            if l == cfg.L - 1:
                yts = [kb.sb("ytok%d" % i, [128, D], F32, es=pes) for i in range(2)]
                for qi, (q0, r) in enumerate(qts):
                    yt = yts[qi % 2]
                    for half in range(2):
                        ps = kb.ps()
                        for c in range(4):
                            cc = half * 4 + c
                            self.tr(ps, ps[:r, c * 128:(c + 1) * 128], x, x[:, cc, q0:q0 + r], 128, bf=False)
                        kb.op('act', lambda e, ps=ps, half=half, yt=yt, r=r: e.copy(out=yt[:r, half * 512:(half + 1) * 512], in_=ps[:r, :]),
                              reads=[ps, yt], writes=[yt])
                    kb.dma('sp', O["y_" + pre][row0 + q0:row0 + q0 + r, :], yt[:r, :], reads=[yt])
            kb.barrier()


def _rel_bucket(rel):
    nb, max_exact = 16, 8
    ret = np.where(rel > 0, nb, 0)
    n = np.abs(rel)
    nf = np.maximum(n, 1).astype(np.float32)
    large = max_exact + (np.log(nf / np.float32(max_exact)) / np.float32(math.log(128 / max_exact))
                         * np.float32(nb - max_exact)).astype(np.int32)
    large = np.minimum(large, nb - 1)
    return ret + np.where(n < max_exact, n, large)


def _consts(cfg):
    half = 16
    freqs = (10000.0 ** (-np.arange(half, dtype=np.float32) / half)).astype(np.float32)

    def rope_tab(pos):
        ang = pos.astype(np.float32)[:, None] * freqs[None, :]
        c, s = np.cos(ang).astype(np.float32), np.sin(ang).astype(np.float32)
        return np.concatenate([c, c, -s, s], axis=1).astype(np.float32)

    rope_p = rope_tab(np.arange(cfg.SEQ))
    rope_s = rope_tab(cfg.PAST + np.arange(cfg.TS))
    p = np.arange(128)[:, None]
    jj = np.arange(258)[None, :]
    bidx = _rel_bucket(p - jj).astype(np.float32)
    icnt = np.zeros((128, 2, cfg.TB), np.float32)
    t = np.arange(cfg.TB)
    for g, w in enumerate((2, 4, 8, 16)):
        c, lo = g // 2, (g % 2) * 64
        icnt[lo:lo + 64, c, :] = 1.0 / np.minimum(w, t + 1).astype(np.float32)[None, :]
    return rope_p, rope_s, bidx, icnt.reshape(128, 2 * cfg.TB)


_CACHE = {}


def _get_nc(cfg_key):
    if cfg_key not in _CACHE:
        cfg = Cfg(*cfg_key)
        gen = Gen(cfg)
        nc = bass.Bass("TRN2", target_bir_lowering=False)
        gen.build(nc)
        _CACHE[cfg_key] = (cfg, gen, nc)
    return _CACHE[cfg_key]


def run(inputs, cfg_key):
    cfg, gen, nc = _get_nc(cfg_key)
    L = cfg.L
    f = lambda a: np.ascontiguousarray(np.asarray(a, dtype=np.float32))
    rope_p, rope_s, bidx, icnt0 = _consts(cfg)
    B = inputs["x_prompt"].shape[0]
    NS = inputs["x_sample"].shape[0]
    ncores = 8
    shared = {
        "rel_bias": f(inputs["rel_bias"]).reshape(1, 128), "ln_in_g": f(inputs["ln_in_g"]).reshape(1, D),
        "ln_in_b": f(inputs["ln_in_b"]).reshape(1, D), "rope_p": rope_p, "rope_s": rope_s, "bidx": bidx, "icnt0": icnt0,
    }
    for nm in ("w_in", "a_q_norm", "a_kv_norm", "a_w_qup", "a_w_kvup", "pool_w", "pool_scale", "conv_w", "conv_b",
               "conv_ln_g", "conv_ln_b", "w_branch", "w_out", "ln1_g", "ln1_b", "w_up", "ffn_conv_w", "ffn_conv_b",
               "w_down", "ln2_g", "ln2_b"):
        shared[nm] = f(inputs[nm])
    in_maps = []
    for c in range(ncores):
        m = dict(shared)
        m["xp"] = f(inputs["x_prompt"][c % B])
        s = c % NS
        m["xs"] = f(inputs["x_sample"][s])
        m["c_ckv"] = f(inputs["cache_a_ckv"][:, s]); m["c_krope"] = f(inputs["cache_a_krope"][:, s])
        m["c_bk"] = f(inputs["cache_b_k"][:, s]).reshape(L, cfg.PAST, 256)
        m["c_bv"] = f(inputs["cache_b_v"][:, s]).reshape(L, cfg.PAST, 256)
        m["c_kidx"] = f(inputs["cache_b_kidx"][:, s])
        m["st_pool"] = f(inputs["state_pool"][:, s]); m["st_conv"] = f(inputs["state_conv"][:, s])
        m["st_ffn"] = f(inputs["state_ffn"][:, s])
        in_maps.append(m)
    res = run_bass_kernel_spmd(nc, in_maps, core_ids=list(range(ncores)))
    R = res.results

    def stack_p(name, shape_tail):
        return np.stack([R[b]["p_" + name] for b in range(B)], axis=1).reshape((L, B) + shape_tail)

    def stack_s(name, shape_tail):
        return np.stack([R[s]["s_" + name] for s in range(NS)], axis=1).reshape((L, NS) + shape_tail)

    outs = [np.stack([R[b]["y_p"] for b in range(B)], 0), np.stack([R[s]["y_s"] for s in range(NS)], 0)]
    for st, n in ((stack_p, cfg.SEQ), (stack_s, cfg.TS)):
        outs += [st("ckv", (n, 128)), st("krope", (n, 32)), st("bk", (n, 4, 64)), st("bv", (n, 4, 64)), st("kidx", (n, 32)),
                 st("pool", (15, 256)), st("conv", (30, 256)), st("ffn", (2, 2 * DFF))]
    return tuple(np.ascontiguousarray(o.astype(np.float32)) for o in outs)


def kernel(**inputs):
    return run(inputs, (4096, 4, 4096, 16, 512, 4))
```

```python
import math
from contextlib import ExitStack

import numpy as np
import concourse.bass as bass
import concourse.mybir as mybir
from concourse.bass_utils import run_bass_kernel_spmd

F32 = mybir.dt.float32
BF16 = mybir.dt.bfloat16
ALU = mybir.AluOpType
AF = mybir.ActivationFunctionType
AX = mybir.AxisListType

D = 1024
DIN = 6148
DFF = 2816
NFC = DFF // 128
O_CQ, O_CKV, O_KR, O_BQ, O_BK, O_BV, O_QI, O_KI, O_WI, O_UP, O_UC, O_G = (
    0, 192, 320, 352, 608, 864, 1120, 1248, 1280, 1284, 1540, 2052)
A_SCALE = 96 ** -0.5
B_SCALE = 64 ** -0.5
IDX_SCALE = (4 ** -0.5) * (32 ** -0.5)
LN_EPS = 1e-5
NEG_BIG = -1.0e30
NIT = 12


class Cfg:
    def __init__(self, SEQ=4096, DEPTH=4, PAST=4096, DEC_SEQ=16, TB=512, GQ=2):
        self.SEQ, self.L, self.PAST, self.TS, self.TB, self.GQ = SEQ, DEPTH, PAST, DEC_SEQ, TB, GQ
        self.ALPHA = (2 * DEPTH) ** 0.25
        self.NB = SEQ // TB
        self.KP = min(256, SEQ // 4)
        self.KS = min(256, (PAST + DEC_SEQ) // 4)
        self.NKMAX = max(SEQ, PAST + 128)


class Dep:
    __slots__ = ("w", "r")

    def __init__(self):
        self.w = None
        self.r = []


class T:
    __slots__ = ("t", "d", "ex")

    def __init__(self, t, d=None, ex=False):
        self.t = t
        self.d = d if d is not None else Dep()
        self.ex = ex

    def __getitem__(self, k):
        return self.t[k]


class KB:
    NR = 8

    def __init__(self, nc, es):
        self.nc = nc
        self.es = es
        self.E = {'pe': nc.tensor, 'act': nc.scalar, 'dve': nc.vector, 'pool': nc.gpsimd, 'sp': nc.sync}
        self.sem = {e: es.enter_context(nc.semaphore("s_" + e)) for e in ['pe', 'act', 'dve', 'pool']}
        self.cnt = {e: 0 for e in self.sem}
        self.waited = {e: {} for e in self.E}
        self.dring = {q: [es.enter_context(nc.semaphore("d_%s%d" % (q, i))) for i in range(self.NR)]
                      for q in ['sp', 'pool']}
        self.dcnt = {q: 0 for q in self.dring}
        self.n_ins = 0
        self.n_wait = 0
        self._pend = []
        self.pending_dma = []
        self.ps_banks = [T(es.enter_context(nc.psum_tensor("psb%d" % i, [128, 512], F32)), ex=True) for i in range(8)]
        self.ps_groups = {'any': list(range(8))}
        self.ps_ctr = {}

    def sb(self, name, shape, dt=F32, es=None):
        self._nm = getattr(self, "_nm", 0) + 1
        return T((es or self.es).enter_context(self.nc.sbuf_tensor("%s_%d" % (name, self._nm), list(shape), dt)))

    def ps(self, group='any'):
        banks = self.ps_groups[group]
        i = self.ps_ctr.get(group, 0)
        self.ps_ctr[group] = i + 1
        return self.ps_banks[banks[i % len(banks)]]

    def _wait(self, e, evt):
        if evt is None:
            return
        key, sem, val = evt
        if key == e and e == 'pe':
            return
        w = self.waited[e]
        if w.get(key, 0) >= val:
            return
        self.E[e].wait_ge(sem, val)
        w[key] = val
        self.n_wait += 1

    def _deps(self, e, reads, writes):
        for t in reads:
            self._wait(e, t.d.w)
        for t in writes:
            self._wait(e, t.d.w)
            for ev in t.d.r:
                self._wait(e, ev)

    @staticmethod
    def _mark(evt, reads, writes):
        for t in reads:
            t.d.r.append(evt)
        for t in writes:
            t.d.w = evt
            t.d.r = []

    def op(self, e, fn, reads=(), writes=(), inc=True):
        ex = [t for t in reads if t.ex]
        if ex:
            reads = [t for t in reads if not t.ex]
            writes = list(writes) + ex
        self._deps(e, reads, writes)
        ins = fn(self.E[e])
        self.n_ins += 1
        if not inc:
            self._pend.append((list(reads), list(writes)))
            return None
        ins.then_inc(self.sem[e], 1)
        self.cnt[e] += 1
        evt = (e, self.sem[e], self.cnt[e])
        if e == 'pe' and self._pend:
            for (r_, w_) in self._pend:
                self._mark(evt, r_, w_)
            self._pend = []
        self._mark(evt, reads, writes)
        return evt

    def dma(self, q, out, in_, reads=(), writes=(), **kw):
        self._deps(q, reads, writes)
        i = self.dcnt[q] % self.NR
        gen = self.dcnt[q] // self.NR + 1
        sem = self.dring[q][i]
        key = "d_%s%d" % (q, i)
        if gen > 1:
            self._wait(q, (key, sem, 16 * (gen - 1)))
        self.E[q].dma_start(out=out, in_=in_, **kw).then_inc(sem, 16)
        self.dcnt[q] += 1
        evt = (key, sem, 16 * gen)
        self._mark(evt, reads, writes)
        self.pending_dma.append(evt)
        self.n_ins += 1
        return evt

    def barrier(self, engines=('pe', 'act', 'dve', 'sp', 'pool')):
        evs = [(e, self.sem[e], self.cnt[e]) for e in self.sem if self.cnt[e] > 0]
        evs += self.pending_dma
        self.pending_dma = []
        for e in engines:
            for ev in evs:
                self._wait(e, ev)

    def finish(self):
        self.barrier(engines=('sp',))


class Gen:
    def __init__(self, cfg):
        self.cfg = cfg

    def declare(self, nc):
        c = self.cfg
        L = c.L
        I = {}
        O = {}

        def inp(name, shape):
            I[name] = nc.dram_tensor(name, list(shape), F32, kind="ExternalInput").ap()

        def outp(name, shape):
            O[name] = nc.dram_tensor(name, list(shape), F32, kind="ExternalOutput").ap()

        inp("xp", [c.SEQ, D]); inp("xs", [c.TS, D])
        inp("c_ckv", [L, c.PAST, 128]); inp("c_krope", [L, c.PAST, 32]); inp("c_bk", [L, c.PAST, 256])
        inp("c_bv", [L, c.PAST, 256]); inp("c_kidx", [L, c.PAST, 32])
        inp("st_pool", [L, 15, 256]); inp("st_conv", [L, 30, 256]); inp("st_ffn", [L, 2, 2 * DFF])
        inp("rel_bias", [1, 128]); inp("ln_in_g", [1, D]); inp("ln_in_b", [1, D])
        inp("w_in", [L, D, DIN]); inp("a_q_norm", [L, 192]); inp("a_kv_norm", [L, 128])
        inp("a_w_qup", [L, 192, 384]); inp("a_w_kvup", [L, 128, 512]); inp("pool_w", [L, 4, 64, 64])
        inp("pool_scale", [L, 256]); inp("conv_w", [L, 31, 256]); inp("conv_b", [L, 256])
        inp("conv_ln_g", [L, 256]); inp("conv_ln_b", [L, 256]); inp("w_branch", [L, 4, 256, D])
        inp("w_out", [L, D, D]); inp("ln1_g", [L, D]); inp("ln1_b", [L, D]); inp("w_up", [L, D, 2 * DFF])
        inp("ffn_conv_w", [L, 3, 2 * DFF]); inp("ffn_conv_b", [L, 2 * DFF]); inp("w_down", [L, DFF, D])
        inp("ln2_g", [L, D]); inp("ln2_b", [L, D])
        inp("rope_p", [c.SEQ, 64]); inp("rope_s", [c.TS, 64]); inp("bidx", [128, 258]); inp("icnt0", [128, 2 * c.TB])
        outp("y_p", [c.SEQ, D]); outp("y_s", [c.TS, D])
        for pre, n in (("p", c.SEQ), ("s", c.TS)):
            outp(pre + "_ckv", [L, n, 128]); outp(pre + "_krope", [L, n, 32]); outp(pre + "_bk", [L, n, 256])
            outp(pre + "_bv", [L, n, 256]); outp(pre + "_kidx", [L, n, 32])
            outp(pre + "_pool", [L, 15, 256]); outp(pre + "_conv", [L, 30, 256]); outp(pre + "_ffn", [L, 2, 2 * DFF])
        self.I, self.O = I, O

    def mm(self, ps, out_ap, lt, lhsT, rt, rhs, start, stop, extra_r=(), inc=True, **kw):
        self.kb.op('pe', lambda e: e.matmul(out_ap, lhsT=lhsT, rhs=rhs, start=start, stop=stop, **kw),
                   reads=[lt, rt] + list(extra_r), writes=[ps], inc=inc)

    def tr(self, ps, out_ap, it, in_ap, rows, bf=True):
        idt = self.ident16 if bf else self.ident
        self.kb.op('pe', lambda e: e.transpose(out=out_ap, in_=in_ap, identity=idt[:rows, :rows]),
                   reads=[it, idt], writes=[ps])

    def build(self, nc):
        cfg = self.cfg
        self.declare(nc)
        es = ExitStack()
        with es:
            kb = KB(nc, es)
            self.kb = kb
            self.nc = nc
            self.setup_persistent()
            self.wplan = []
            self.wpos = 0
            self.plan_weights()
            self.wissue = 0
            for j in range(cfg.NB):
                self.block('p', j)
            self.block('s', 0)
            assert self.wpos == len(self.wplan), (self.wpos, len(self.wplan))
            kb.finish()
            self.stats = (kb.n_ins, kb.n_wait)
        return nc

    def setup_persistent(self):
        kb, cfg, I = self.kb, self.cfg, self.I
        L = cfg.L
        sb = kb.sb
        self.ident = sb("ident", [128, 128])
        self.ident16 = sb("ident16", [128, 128], BF16)
        kb.op('dve', lambda e: e.memset(self.ident[:], 1.0), writes=[self.ident])
        kb.op('pool', lambda e: e.affine_select(out=self.ident[:], in_=self.ident[:], pattern=[[-1, 128]],
                                                compare_op=ALU.is_equal, fill=0.0, base=0, channel_multiplier=1),
              reads=[self.ident], writes=[self.ident])
        kb.op('dve', lambda e: e.tensor_copy(out=self.ident16[:], in_=self.ident[:]), reads=[self.ident],
              writes=[self.ident16])
        self.ones1024 = sb("ones1024", [128, 128])
        self.ones256 = sb("ones256", [128, 128])
        kb.op('dve', lambda e: e.memset(self.ones1024[:], 1.0 / 1024), writes=[self.ones1024])
        kb.op('dve', lambda e: e.memset(self.ones256[:], 1.0 / 256), writes=[self.ones256])
        o16a, o16b = sb("ones1024b", [128, 128], BF16), sb("ones256b", [128, 128], BF16)
        kb.op('dve', lambda e: e.memset(o16a[:], 1.0 / 1024), writes=[o16a])
        kb.op('dve', lambda e: e.memset(o16b[:], 1.0 / 256), writes=[o16b])
        self.ones16 = {self.ones1024: o16a, self.ones256: o16b}
        self.esel = sb("esel", [32, 96], BF16)
        kb.op('dve', lambda e: e.memset(self.esel[:], 0.0), writes=[self.esel])
        kb.op('dve', lambda e: e.tensor_copy(out=self.esel[:, 64:96], in_=self.ident[0:32, 0:32]),
              reads=[self.ident, self.esel], writes=[self.esel])
        TB = cfg.TB
        self.ropet = sb("ropet", [128, 4, 64])
        self.x = sb("x", [128, 8, TB])
        self.xT = sb("xT", [128, 8, TB], BF16)
        self.qn = sb("qn", [128, L, 320])
        for l in range(L):
            kb.dma('sp', self.qn[:, l, 0:192], I["a_q_norm"][l:l + 1, :].partition_broadcast(128), writes=[self.qn])
            kb.dma('sp', self.qn[:, l, 192:320], I["a_kv_norm"][l:l + 1, :].partition_broadcast(128), writes=[self.qn])
        self.rb = sb("rb", [128, 128])
        kb.dma('sp', self.rb[:], I["rel_bias"].partition_broadcast(128), writes=[self.rb])
        self.pv_ln = sb("pv_ln", [128, L, 4, 8])
        self.pv_lnin = sb("pv_lnin", [128, 2, 8])
        self.pv_c = sb("pv_c", [128, L, 2, 35])
        self.pv_f = sb("pv_f", [128, L, 44, 4])
        self.pcar = {k: sb("pcar_" + k, [128, L, 2, 15]) for k in 'ps'}
        self.ccar = {k: sb("ccar_" + k, [128, L, 2, 30]) for k in 'ps'}
        self.fcar = {k: sb("fcar_" + k, [128, L, 44, 2]) for k in 'ps'}
        for t in (self.pcar['p'], self.ccar['p'], self.fcar['p']):
            kb.op('dve', lambda e, t=t: e.memset(t[:], 0.0), writes=[t])
        self.gt = sb("gt", [128, 4, 258])
        self.NW = 4
        self.wring = [sb("wring%d" % i, [128, 4096], BF16) for i in range(self.NW)]
        self._wsm_ctr = 0
        self._wr_ctr = 0
        self._wbr_ctr = 0
        self.w_released = 0
        self.w_ringreq = 0
        self.wsm = [sb("wsm%d" % i, [128, 1536], BF16) for i in range(2)]
        for t in self.wsm:
            kb.op('dve', lambda e, t=t: e.memset(t[:], 0.0), writes=[t])
        with ExitStack() as pes:
            stg = kb.sb("pstage", [32, 2 * DFF], F32, es=pes)

            def colvec(dst_fn, src_ap, R, C, dst_t):
                kb.dma('sp', stg[:R, 0:C * 128], src_ap, writes=[stg])
                c0 = 0
                while c0 < C:
                    nch = min(512 // max(R, 1), C - c0, 16)
                    nch = max(1, min(nch, 512 // R))
                    ps = kb.ps()
                    for i in range(nch):
                        self.tr(ps, ps[:, i * R:(i + 1) * R], stg, stg[:R, (c0 + i) * 128:(c0 + i + 1) * 128], R, bf=False)
                    for i in range(nch):
                        kb.op('act', lambda e, i=i: e.copy(out=dst_fn(c0 + i), in_=ps[:, i * R:(i + 1) * R]),
                              reads=[ps], writes=[dst_t])
                    c0 += nch

            colvec(lambda c: self.pv_lnin[:, 0, c:c + 1], I["ln_in_g"], 1, 8, self.pv_lnin)
            colvec(lambda c: self.pv_lnin[:, 1, c:c + 1], I["ln_in_b"], 1, 8, self.pv_lnin)
            for l in range(L):
                for i, nm in enumerate(("ln1_g", "ln1_b", "ln2_g", "ln2_b")):
                    colvec(lambda c, i=i: self.pv_ln[:, l, i, c:c + 1], I[nm][l:l + 1, :], 1, 8, self.pv_ln)
                colvec(lambda c: self.pv_c[:, l, c, 0:31], I["conv_w"][l], 31, 2, self.pv_c)
                for i, nm in enumerate(("conv_b", "conv_ln_g", "conv_ln_b", "pool_scale")):
                    colvec(lambda c, i=i: self.pv_c[:, l, c, 31 + i:32 + i], I[nm][l:l + 1, :], 1, 2, self.pv_c)
                colvec(lambda c: self.pv_f[:, l, c, 0:3], I["ffn_conv_w"][l], 3, 44, self.pv_f)
                colvec(lambda c: self.pv_f[:, l, c, 3:4], I["ffn_conv_b"][l:l + 1, :], 1, 44, self.pv_f)
                colvec(lambda c: self.pcar['s'][:, l, c, :], I["st_pool"][l], 15, 2, self.pcar['s'])
                colvec(lambda c: self.ccar['s'][:, l, c, :], I["st_conv"][l], 30, 2, self.ccar['s'])
                colvec(lambda c: self.fcar['s'][:, l, c, :], I["st_ffn"][l], 2, 44, self.fcar['s'])
            bidx = kb.sb("bidx_sb", [128, 258], F32, es=pes)
            tmpg = kb.sb("tmpg", [128, 258], F32, es=pes)
            kb.dma('sp', bidx[:], I["bidx"], writes=[bidx])
            for h in range(4):
                kb.op('dve', lambda e, h=h: e.tensor_scalar(out=self.gt[:, h, :], in0=bidx[:], scalar1=0.0,
                                                            scalar2=self.rb[:, 15 * 4 + h:15 * 4 + h + 1],
                                                            op0=ALU.mult, op1=ALU.subtract),
                      reads=[bidx, self.rb], writes=[self.gt])
                for b in range(32):
                    kb.op('dve', lambda e, h=h, b=b: e.tensor_scalar(out=tmpg[:], in0=bidx[:], scalar1=float(b),
                                                                    scalar2=self.rb[:, b * 4 + h:b * 4 + h + 1],
                                                                    op0=ALU.is_equal, op1=ALU.mult),
                          reads=[bidx, self.rb], writes=[tmpg])
                    kb.op('dve', lambda e, h=h: e.tensor_tensor(out=self.gt[:, h, :], in0=self.gt[:, h, :], in1=tmpg[:],
                                                                op=ALU.add),
                          reads=[tmpg, self.gt], writes=[self.gt])
            kb.barrier()
    def plan_weights(self):
        cfg, I = self.cfg, self.I
        plan = []
        nblocks = cfg.NB + 1
        for b in range(nblocks):
            for l in range(cfg.L):
                w_in = I["w_in"][l].rearrange("(k p) n -> p k n", p=128)
                plan.append(("wsm", l, None))
                plan.append(("wtok_a0", l, [(w_in[:, :, 0:352], (8, 352), 0)]))
                plan.append(("wtok_a1", l, [(w_in[:, :, 352:864], (8, 512), 0)]))
                plan.append(("wtok_b", l, [(w_in[:, :, 864:1284], (8, 420), 0)]))
                plan.append(("wpc0", l, [(w_in[:, :, 1284:1668], (8, 384), 0)]))
                plan.append(("wpc1", l, [(w_in[:, :, 1668:2052], (8, 384), 0)]))
                wbr = I["w_branch"][l].rearrange("n (k p) d -> p (n k) d", p=128)
                for hf in range(2):
                    for n in range(4):
                        c0 = O_G + n * 1024 + hf * 512
                        plan.append(("wbr%d%d" % (n, hf), l, [(wbr[:, 2 * n:2 * n + 2, hf * 512:(hf + 1) * 512], (2, 512), 0)]))
                        plan.append(("wg%d%d" % (n, hf), l, [(w_in[:, :, c0:c0 + 512], (8, 512), 0)]))
                w_o = I["w_out"][l].rearrange("(k p) n -> p k n", p=128)
                for hf in range(2):
                    plan.append(("wout%d" % hf, l, [(w_o[:, :, hf * 512:(hf + 1) * 512], (8, 512), 0)]))
                w_up = I["w_up"][l].rearrange("(k p) n -> p k n", p=128)
                for g in range(6):
                    nch = 4 if g < 5 else 2
                    plan.append(("wupv%d" % g, l, [(w_up[:, :, g * 512:g * 512 + nch * 128], (8, nch * 128), 0)]))
                    plan.append(("wupg%d" % g, l, [(w_up[:, :, DFF + g * 512:DFF + g * 512 + nch * 128], (8, nch * 128), 0)]))
                w_dn = I["w_down"][l].rearrange("(k p) n -> p k n", p=128)
                for g in range(8):
                    plan.append(("wdn%d" % g, l, [(w_dn[:, :, g * 128:(g + 1) * 128], (NFC, 128), 0)]))
        self.wplan = plan
        self.wslot = [None] * len(plan)

    def issue_next_weight(self):
        i = self.wissue
        if i >= len(self.wplan):
            return False
        kb, I = self.kb, self.I
        tag, l, parts = self.wplan[i]
        if tag == "wsm":
            t = self.wsm[self._wsm_ctr % 2]
            self._wsm_ctr += 1
            kb.dma('pool', t[:, 0:384], I["a_w_qup"][l, 0:128, :], writes=[t])
            kb.dma('pool', t[0:64, 384:768], I["a_w_qup"][l, 128:192, :], writes=[t])
            kb.dma('pool', t[:, 768:1280], I["a_w_kvup"][l], writes=[t])
            for g in range(4):
                cc, hh = g // 2, g % 2
                kb.dma('pool', t[hh * 64:hh * 64 + 64, 1280 + cc * 128 + hh * 64:1280 + cc * 128 + hh * 64 + 64],
                       I["pool_w"][l, g], writes=[t])
            self.wslot[i] = t
        else:
            if self._wr_ctr - self.NW >= self.w_released:
                return False
            t = self.wring[self._wr_ctr % self.NW]
            self._wr_ctr += 1
            for (src, (a, b), off) in parts:
                dst = t[:, off:off + a * b].rearrange("p (a b) -> p a b", a=a)
                kb.dma('pool', dst, src, writes=[t])
            self.wslot[i] = t
        self.wissue += 1
        return True

    def getw(self, tag, l, hold=0):
        ptag, pl, _ = self.wplan[self.wpos]
        assert ptag == tag and pl == l, (ptag, pl, tag, l)
        is_ring = tag != "wsm"
        if is_ring:
            self.w_ringreq += 1
        self.w_released = max(self.w_released, self.w_ringreq - hold - (1 if is_ring else 0))
        while self.wissue < len(self.wplan) and self.wissue <= self.wpos + 6:
            if self.issue_next_weight() is False:
                break
        assert self.wissue > self.wpos, (tag, l)
        t = self.wslot[self.wpos]
        self.wpos += 1
        return t

    def lnfm(self, es, s, C, T, ones, gfn, bfn, gb_t, outs, func=AF.Identity):
        kb = self.kb
        sqs = [kb.sb("ln_sq%d" % i, [128, T], BF16, es=es) for i in range(2)]
        s16s = [kb.sb("ln_s16%d" % i, [128, T], BF16, es=es) for i in range(2)]
        ones16 = self.ones16[ones]
        pm = kb.ps()
        pq = kb.ps()
        for c in range(C):
            sq, s16 = sqs[c % 2], s16s[c % 2]
            kb.op('dve', lambda e: e.tensor_copy(out=s16[:, :], in_=s[:, c, 0:T]), reads=[s], writes=[s16])
            kb.op('act', lambda e: e.activation(out=sq[:, :], in_=s[:, c, 0:T], func=AF.Square), reads=[s], writes=[sq])
            self.mm(pm, pm[:, 0:T], ones16, ones16[:, :], s16, s16[:, :], c == 0, c == C - 1)
            self.mm(pq, pq[:, 0:T], ones16, ones16[:, :], sq, sq[:, :], c == 0, c == C - 1)
        mean = kb.sb("ln_mean", [128, T], F32, es=es)
        rstd = kb.sb("ln_rstd", [128, T], F32, es=es)
        kb.op('act', lambda e: e.copy(out=mean[:], in_=pm[:, 0:T]), reads=[pm], writes=[mean])
        kb.op('dve', lambda e: e.tensor_tensor(out=rstd[:], in0=mean[:], in1=mean[:], op=ALU.mult), reads=[mean], writes=[rstd])
        kb.op('dve', lambda e: e.tensor_tensor(out=rstd[:], in0=pq[:, 0:T], in1=rstd[:], op=ALU.subtract), reads=[pq, rstd], writes=[rstd])
        kb.op('dve', lambda e: e.tensor_scalar(out=rstd[:], in0=rstd[:], scalar1=0.0, scalar2=LN_EPS, op0=ALU.max, op1=ALU.add),
              reads=[rstd], writes=[rstd])
        kb.op('act', lambda e: e.activation(out=rstd[:], in_=rstd[:], func=AF.Sqrt), reads=[rstd], writes=[rstd])
        kb.op('dve', lambda e: e.reciprocal(out=rstd[:], in_=rstd[:]), reads=[rstd], writes=[rstd])
        for c in range(C):
            sc = type(s)(s.t)
            kb.op('dve', lambda e: e.tensor_tensor(out=s[:, c, 0:T], in0=s[:, c, 0:T], in1=mean[:], op=ALU.subtract),
                  reads=[s, mean], writes=[sc])
            kb.op('dve', lambda e: e.tensor_tensor(out=s[:, c, 0:T], in0=s[:, c, 0:T], in1=rstd[:], op=ALU.mult),
                  reads=[sc, rstd], writes=[sc])
            for (ot, ofn) in outs:
                kb.op('act', lambda e: e.activation(out=ofn(c), in_=s[:, c, 0:T], func=func, bias=bfn(c), scale=gfn(c)),
                      reads=[sc, gb_t], writes=[ot])

    def block(self, kind, j):
        cfg, kb, I, O = self.cfg, self.kb, self.I, self.O
        L = cfg.L
        if kind == 'p':
            Tn = cfg.TB
            P = j * cfg.TB
            xin = I["xp"][P:P + Tn, :]
            rope = I["rope_p"][P:P + Tn, :]
            pre = "p"
            row0 = P
            ktop = cfg.KP
        else:
            Tn = cfg.TS
            P = cfg.PAST
            xin = I["xs"]
            rope = I["rope_s"]
            pre = "s"
            row0 = 0
            ktop = cfg.KS
        self.kind, self.Tn, self.P, self.pre, self.row0, self.ktop, self.bj = kind, Tn, P, pre, row0, ktop, j
        qts = [(q0, min(128, Tn - q0)) for q0 in range(0, Tn, 128)]
        self.qts = qts
        if kind == 'p':
            self.ktiles = [('o', t * 128, 128, t * 128) for t in range((P + Tn) // 128)]
        else:
            self.ktiles = [('c', t * 128, 128, t * 128) for t in range(P // 128)] + [('o', 0, Tn, P)]
        self.last_block = (kind == 's') or (j == cfg.NB - 1)
        x, xT = self.x, self.xT
        with ExitStack() as pes:
            for qi, (q0, r) in enumerate(qts):
                kb.dma('sp', self.ropet[:r, qi, :], rope[q0:q0 + r, :], writes=[self.ropet])
            xraw = kb.sb("xraw", [128, 8, Tn], F32, es=pes)
            xtoks = [kb.sb("xtok%d" % i, [128, D], F32, es=pes) for i in range(2)]
            for qi, (q0, r) in enumerate(qts):
                xtok = xtoks[qi % 2]
                kb.dma('sp', xtok[:r, :], xin[q0:q0 + r, :], writes=[xtok])
                for half in range(2):
                    ps = kb.ps()
                    for c in range(4):
                        cc = half * 4 + c
                        self.tr(ps, ps[:, c * 128:c * 128 + r], xtok, xtok[:r, cc * 128:(cc + 1) * 128], r, bf=False)
                    kb.op('act', lambda e, half=half, ps=ps, q0=q0, r=r: e.copy(
                        out=xraw[:, half * 4:half * 4 + 4, q0:q0 + r],
                        in_=ps[:, :].rearrange("p (c t) -> p c t", c=4)[:, :, 0:r]), reads=[ps], writes=[xraw])
            self.lnfm(pes, xraw, 8, Tn, self.ones1024,
                      lambda c: self.pv_lnin[:, 0, c:c + 1], lambda c: self.pv_lnin[:, 1, c:c + 1], self.pv_lnin,
                      [(x, lambda c: x[:, c, 0:Tn]), (xT, lambda c: xT[:, c, 0:Tn])])
            kb.barrier()
        for l in range(L):
            self.layer(l)

    def layer(self, l):
        cfg, kb, I, O = self.cfg, self.kb, self.I, self.O
        kind, Tn, P, pre, row0, qts = self.kind, self.Tn, self.P, self.pre, self.row0, self.qts
        x, xT = self.x, self.xT
        NQ = len(qts)
        wsm = self.getw("wsm", l)
        wqup = lambda kc, ksz: wsm[0:ksz, kc * 384:(kc + 1) * 384]
        with ExitStack() as mes:
            brT = kb.sb("brT", [128, 8, Tn], BF16, es=mes)
            qaT = kb.sb("qaT", [96, 4, Tn], BF16, es=mes)
            bqT = kb.sb("bqT", [128, 2, Tn], BF16, es=mes)
            qiT = kb.sb("qiT", [32, 4, Tn], BF16, es=mes)
            wqs = kb.sb("wqs", [128, NQ, 4], F32, es=mes)
            wn = kb.sb("wn", [128, 4, 96], BF16, es=mes)
            wv = kb.sb("wv", [128, 256], BF16, es=mes)
            kvv = wsm[:, 768:1280].rearrange("p (h c) -> p h c", h=4)
            kb.op('dve', lambda e: e.memset(wn[:], 0.0), writes=[wn])
            kb.op('dve', lambda e: e.tensor_copy(out=wn[:, :, 0:64], in_=kvv[:, :, 0:64]), reads=[wsm, wn], writes=[wn])
            kb.op('dve', lambda e: e.tensor_copy(out=wv[:, :].rearrange("p (h c) -> p h c", h=4), in_=kvv[:, :, 64:128]),
                  reads=[wsm], writes=[wv])
            self.wn, self.wv = wn, wv
            odeps = {nm: T(None) for nm in ("ckv", "krope", "bk", "bv", "kidx")}
            wa0 = self.getw("wtok_a0", l)
            wa1 = self.getw("wtok_a1", l, hold=1)
            wb = self.getw("wtok_b", l, hold=2)
            wa03 = wa0[:, 0:8 * 352].rearrange("p (k n) -> p k n", k=8)
            wa13 = wa1[:, 0:8 * 512].rearrange("p (k n) -> p k n", k=8)
            wb3 = wb[:, 0:8 * 420].rearrange("p (k n) -> p k n", k=8)
            with ExitStack() as pes:
                NQ_ = len(qts)
                rp = self.ropet
                Bf = []
                for i in range(NQ_):
                    sfx = str(i)
                    Bf.append(dict(
                        toka=kb.sb("toka" + sfx, [128, 352], F32, es=pes), tokb=kb.sb("tokb" + sfx, [128, 512], F32, es=pes),
                        tokc=kb.sb("tokc" + sfx, [128, 420], F32, es=pes), rows=kb.sb("rowsA" + sfx, [128, 160], F32, es=pes),
                        ss=kb.sb("ss" + sfx, [128, 4], F32, es=pes), junk=kb.sb("junk" + sfx, [128, 192], F32, es=pes),
                        cqn=kb.sb("cqn" + sfx, [128, 192], BF16, es=pes), bq16=kb.sb("bq16" + sfx, [128, 384], BF16, es=pes),
                        cqT=kb.sb("cqT" + sfx, [128, 2, 128], BF16, es=pes), qa16=kb.sb("qa16" + sfx, [128, 4, 96], BF16, es=pes),
                        rt1=kb.sb("rt1" + sfx, [128, 4, 32], F32, es=pes), rt2=kb.sb("rt2" + sfx, [128, 4, 32], F32, es=pes)))
                for qi, (q0, r) in enumerate(qts):
                    b_ = Bf[qi]
                    tpa, tpb, tpc = kb.ps(), kb.ps(), kb.ps()
                    for (ps, wt, w3, c0, c1) in ((tpa, wa0, wa03, 0, 352), (tpb, wa1, wa13, 0, 512), (tpc, wb, wb3, 0, 420)):
                        for k in range(8):
                            self.mm(ps, ps[:r, 0:c1 - c0], xT, xT[:, k, q0:q0 + r], wt, w3[:, k, c0:c1], k == 0, k == 7, inc=(k == 7))
                    kb.op('act', lambda e: e.copy(out=b_['toka'][:r, :], in_=tpa[:r, 0:352]), reads=[tpa], writes=[b_['toka']])
                    kb.op('dve', lambda e: e.tensor_copy(out=b_['tokb'][:r, :], in_=tpb[:r, :]), reads=[tpb], writes=[b_['tokb']])
                    kb.op('act', lambda e: e.copy(out=b_['tokc'][:r, :], in_=tpc[:r, 0:420]), reads=[tpc], writes=[b_['tokc']])
                pc = self.phase_poolconv(l, brT)
                next(pc, None)
                for qi, (q0, r) in enumerate(qts):
                    b_ = Bf[qi]
                    ss, junk, toka = b_['ss'], b_['junk'], b_['toka']
                    kb.op('dve', lambda e: e.memset(ss[:], 0.0), writes=[ss])
                    kb.op('act', lambda e: e.activation(out=junk[:r, 0:192], in_=toka[:r, 0:192], func=AF.Square,
                                                        scale=192 ** -0.5, accum_out=ss[:r, 0:1]), reads=[toka, ss], writes=[junk, ss])
                    kb.op('act', lambda e: e.activation(out=junk[:r, 0:128], in_=toka[:r, 192:320], func=AF.Square,
                                                        scale=128 ** -0.5, accum_out=ss[:r, 1:2]), reads=[toka, ss], writes=[junk, ss])
                for qi, (q0, r) in enumerate(qts):
                    ss = Bf[qi]['ss']
                    kb.op('dve', lambda e: e.tensor_scalar(out=ss[:r, 2:4], in0=ss[:r, 0:2], scalar1=LN_EPS, scalar2=1.0, op0=ALU.add, op1=ALU.mult),
                          reads=[ss], writes=[ss])
                for qi, (q0, r) in enumerate(qts):
                    ss = Bf[qi]['ss']
                    kb.op('act', lambda e: e.activation(out=ss[:r, 2:4], in_=ss[:r, 2:4], func=AF.Sqrt), reads=[ss], writes=[ss])
                for qi, (q0, r) in enumerate(qts):
                    ss = Bf[qi]['ss']
                    kb.op('dve', lambda e: e.reciprocal(out=ss[:r, 2:4], in_=ss[:r, 2:4]), reads=[ss], writes=[ss])
                next(pc, None)
                for qi, (q0, r) in enumerate(qts):
                    b_ = Bf[qi]
                    ss, toka, tokb, tokc, rows, cqn, rt1, rt2, bq16 = (b_[k] for k in ('ss', 'toka', 'tokb', 'tokc', 'rows', 'cqn', 'rt1', 'rt2', 'bq16'))
                    kb.op('dve', lambda e: e.scalar_tensor_tensor(out=cqn[:r, :], in0=toka[:r, 0:192], scalar=ss[:r, 2:3],
                                                                  in1=self.qn[:r, l, 0:192], op0=ALU.mult, op1=ALU.mult),
                          reads=[toka, ss, self.qn], writes=[cqn])
                    kb.op('dve', lambda e: e.scalar_tensor_tensor(out=rows[:r, 0:128], in0=toka[:r, 192:320], scalar=ss[:r, 3:4],
                                                                  in1=self.qn[:r, l, 192:320], op0=ALU.mult, op1=ALU.mult),
                          reads=[toka, ss, self.qn], writes=[rows])
                    kb.op('dve', lambda e: e.tensor_tensor(out=rt1[:r, 0, :], in0=toka[:r, 320:352], in1=rp[:r, qi, 0:32], op=ALU.mult),
                          reads=[toka, rp], writes=[rt1])
                    kb.op('dve', lambda e: e.tensor_tensor(out=rt2[:r, 0, 0:16], in0=toka[:r, 336:352], in1=rp[:r, qi, 32:48], op=ALU.mult),
                          reads=[toka, rp], writes=[rt2])
                    kb.op('dve', lambda e: e.tensor_tensor(out=rt2[:r, 0, 16:32], in0=toka[:r, 320:336], in1=rp[:r, qi, 48:64], op=ALU.mult),
                          reads=[toka, rp, rt2], writes=[rt2])
                    kb.op('dve', lambda e: e.tensor_tensor(out=rows[:r, 128:160], in0=rt1[:r, 0, :], in1=rt2[:r, 0, :], op=ALU.add),
                          reads=[rt1, rt2, rows], writes=[rows])
                    rr = slice(row0 + q0, row0 + q0 + r)
                    kb.dma('sp', O[pre + "_ckv"][l, rr, :], rows[:r, 0:128], reads=[rows], writes=[odeps["ckv"]])
                    kb.dma('sp', O[pre + "_krope"][l, rr, :], rows[:r, 128:160], reads=[rows], writes=[odeps["krope"]])
                    kb.dma('sp', O[pre + "_bk"][l, rr, :], tokb[:r, 256:512], reads=[tokb], writes=[odeps["bk"]])
                    kb.dma('sp', O[pre + "_bv"][l, rr, :], tokc[:r, 0:256], reads=[tokc], writes=[odeps["bv"]])
                    kb.dma('sp', O[pre + "_kidx"][l, rr, :], tokc[:r, 384:416], reads=[tokc], writes=[odeps["kidx"]])
                    kb.op('dve', lambda e: e.tensor_copy(out=bq16[:r, 0:256], in_=tokb[:r, 0:256]), reads=[tokb], writes=[bq16])
                    kb.op('dve', lambda e: e.tensor_copy(out=bq16[:r, 256:384], in_=tokc[:r, 256:384]), reads=[tokc, bq16], writes=[bq16])
                    kb.op('dve', lambda e: e.tensor_scalar(out=wqs[:r, qi, :], in0=tokc[:r, 416:420], scalar1=IDX_SCALE, scalar2=0.0,
                                                           op0=ALU.mult, op1=ALU.add), reads=[tokc], writes=[wqs])
                next(pc, None)
                pts = []
                for qi, (q0, r) in enumerate(qts):
                    b_ = Bf[qi]
                    bq16, cqn = b_['bq16'], b_['cqn']
                    pt = kb.ps()
                    ptb = pt[:, :].bitcast(BF16)
                    for pr in range(2):
                        self.tr(pt, ptb[:, pr * 128:pr * 128 + r], bq16, bq16[:r, pr * 128:(pr + 1) * 128], r)
                    for h in range(4):
                        self.tr(pt, ptb[0:32, 256 + h * 128:256 + h * 128 + r], bq16, bq16[:r, 256 + h * 32:256 + (h + 1) * 32], r)
                    self.tr(pt, ptb[:, 768:768 + r], cqn, cqn[:r, 0:128], r)
                    self.tr(pt, ptb[0:64, 896:896 + r], cqn, cqn[:r, 128:192], r)
                    pts.append((pt, ptb))
                for qi, (q0, r) in enumerate(qts):
                    pt, ptb = pts[qi]
                    cqT = Bf[qi]['cqT']
                    kb.op('act', lambda e: e.copy(out=bqT[:, :, q0:q0 + r],
                                                  in_=ptb[:, 0:256].rearrange("p (a b) -> p a b", a=2)[:, :, 0:r]),
                          reads=[pt], writes=[bqT])
                    kb.op('dve', lambda e: e.tensor_copy(out=qiT[:, :, q0:q0 + r],
                                                         in_=ptb[0:32, 256:768].rearrange("p (a b) -> p a b", a=4)[:, :, 0:r]),
                          reads=[pt], writes=[qiT])
                    kb.op('act', lambda e: e.copy(out=cqT[:, 0, 0:r], in_=ptb[:, 768:768 + r]), reads=[pt], writes=[cqT])
                    kb.op('dve', lambda e: e.tensor_copy(out=cqT[0:64, 1, 0:r], in_=ptb[0:64, 896:896 + r]), reads=[pt, cqT], writes=[cqT])
                next(pc, None)
                pqs = []
                for qi, (q0, r) in enumerate(qts):
                    cqT = Bf[qi]['cqT']
                    pq = kb.ps()
                    self.mm(pq, pq[:r, 0:384], cqT, cqT[:, 0, 0:r], wsm, wqup(0, 128), True, False)
                    self.mm(pq, pq[:r, 0:384], cqT, cqT[0:64, 1, 0:r], wsm, wqup(1, 64), False, True)
                    pqs.append(pq)
                for qi, (q0, r) in enumerate(qts):
                    b_ = Bf[qi]
                    qa16, rt1, rt2 = b_['qa16'], b_['rt1'], b_['rt2']
                    pq = pqs[qi]
                    pq3 = pq[:, 0:384].rearrange("p (h c) -> p h c", h=4)
                    kb.op('act', lambda e: e.copy(out=qa16[:r, :, 0:64], in_=pq3[:r, :, 0:64]), reads=[pq], writes=[qa16])
                    for h in range(4):
                        kb.op('dve', lambda e: e.tensor_tensor(out=rt1[:r, h, :], in0=pq3[:r, h, 64:96], in1=rp[:r, qi, 0:32], op=ALU.mult),
                              reads=[pq, rp, rt1], writes=[rt1])
                        kb.op('dve', lambda e: e.tensor_tensor(out=rt2[:r, h, 0:16], in0=pq3[:r, h, 80:96], in1=rp[:r, qi, 32:48], op=ALU.mult),
                              reads=[pq, rp, rt2], writes=[rt2])
                        kb.op('dve', lambda e: e.tensor_tensor(out=rt2[:r, h, 16:32], in0=pq3[:r, h, 64:80], in1=rp[:r, qi, 48:64], op=ALU.mult),
                              reads=[pq, rp, rt2], writes=[rt2])
                    kb.op('dve', lambda e: e.tensor_tensor(out=qa16[:r, :, 64:96], in0=rt1[:r, :, :], in1=rt2[:r, :, :], op=ALU.add),
                          reads=[rt1, rt2, qa16], writes=[qa16])
                next(pc, None)
                pts = []
                for qi, (q0, r) in enumerate(qts):
                    qa16 = Bf[qi]['qa16']
                    pt3 = kb.ps()
                    pt3b = pt3[:, :].bitcast(BF16)
                    for h in range(4):
                        self.tr(pt3, pt3b[0:96, h * 128:h * 128 + r], qa16, qa16[:r, h, :], r)
                    pts.append((pt3, pt3b))
                for qi, (q0, r) in enumerate(qts):
                    pt3, pt3b = pts[qi]
                    kb.op('act', lambda e: e.copy(out=qaT[:, :, q0:q0 + r],
                                                  in_=pt3b[0:96, 0:512].rearrange("p (a b) -> p a b", a=4)[:, :, 0:r]),
                          reads=[pt3], writes=[qaT])
                for _ in pc:
                    pass
                kb.barrier()
            self.odeps = odeps
            kb.ps_groups.update({'acc': [0, 1, 2, 3], 'qk': [4, 5], 'misc': [6, 7]})
            with ExitStack() as des:
                st = self.dsa_open(l, des, qiT, wqs)
                def chain():
                    for t0 in range(0, len(qts), 2):
                        yield from self.dsa_pass1(l, st, t0)
                self.phase_mla(l, brT, qaT, bg=chain())
                self.dsa_rest(l, des, st, brT, bqT, True)
                kb.barrier()
            self.phase_merge(l, brT)
        self.phase_ffn(l)

    def phase_poolconv(self, l, brT):
        cfg, kb, I, O = self.cfg, self.kb, self.I, self.O
        kind, Tn, pre = self.kind, self.Tn, self.pre
        xT = self.xT
        with ExitStack() as pes:
            wpcs = [self.getw("wpc0", l), self.getw("wpc1", l, hold=1)]
            wpc3 = [w[:, 0:8 * 384].rearrange("p (k n) -> p k n", k=8) for w in wpcs]
            wsm = self.cur_wsm(l)
            pext = kb.sb("pext", [128, 2, 15 + Tn], F32, es=pes)
            cext = kb.sb("cext", [128, 2, 30 + Tn], F32, es=pes)
            sg = kb.sb("sgl", [128, 2, Tn], F32, es=pes)
            pcar, ccar = self.pcar[kind], self.ccar[kind]
            kb.op('dve', lambda e: e.tensor_copy(out=pext[:, :, 0:15], in_=pcar[:, l, :, :]), reads=[pcar], writes=[pext])
            kb.op('dve', lambda e: e.tensor_copy(out=cext[:, :, 0:30], in_=ccar[:, l, :, :]), reads=[ccar], writes=[cext])
            pss = []
            for c in range(6):
                ps = kb.ps()
                for k in range(8):
                    self.mm(ps, ps[:, 0:Tn], wpcs[c // 3], wpc3[c // 3][:, k, (c % 3) * 128:(c % 3 + 1) * 128], xT, xT[:, k, 0:Tn], k == 0, k == 7, inc=(k == 7))
                pss.append(ps)
                if c < 2:
                    kb.op('act', lambda e, c=c, ps=ps: e.copy(out=pext[:, c, 15:15 + Tn], in_=ps[:, 0:Tn]), reads=[ps, pext], writes=[pext])
                elif c >= 4:
                    kb.op('act', lambda e, c=c, ps=ps: e.activation(out=sg[:, c - 4, :], in_=ps[:, 0:Tn], func=AF.Sigmoid),
                          reads=[ps, sg], writes=[sg])
                    kb.op('dve', lambda e, c=c: e.tensor_tensor(out=cext[:, c - 4, 30:30 + Tn], in0=pss[c - 2][:, 0:Tn], in1=sg[:, c - 4, :],
                                                                op=ALU.mult), reads=[pss[c - 2], sg, cext], writes=[cext])
            yield
            kb.op('dve', lambda e: e.tensor_copy(out=pcar[:, l, :, :], in_=pext[:, :, Tn:Tn + 15]), reads=[pext, pcar], writes=[pcar])
            kb.op('dve', lambda e: e.tensor_copy(out=ccar[:, l, :, :], in_=cext[:, :, Tn:Tn + 30]), reads=[cext, ccar], writes=[ccar])
            if self.last_block:
                st = kb.sb("st_out", [32, 512], F32, es=pes)
                ps = kb.ps()
                for c in range(2):
                    self.tr(ps, ps[0:15, c * 128:(c + 1) * 128], pcar, pcar[:, l, c, :], 128, bf=False)
                    self.tr(ps, ps[0:30, 256 + c * 128:256 + (c + 1) * 128], ccar, ccar[:, l, c, :], 128, bf=False)
                kb.op('act', lambda e: e.copy(out=st[0:30, :], in_=ps[0:30, :]), reads=[ps], writes=[st])
                kb.dma('sp', O[pre + "_pool"][l], st[0:15, 0:256], reads=[st])
                kb.dma('sp', O[pre + "_conv"][l], st[0:30, 256:512], reads=[st])
            yield
            bA = kb.sb("paA", [128, 2, 15 + Tn], F32, es=pes)
            bB = kb.sb("paB", [128, 2, 15 + Tn], F32, es=pes)
            n = 15 + Tn
            pl = kb.sb("pl", [128, 2, Tn], F32, es=pes)
            pl16 = kb.sb("pl16", [128, 2, Tn], BF16, es=pes)
            first = (kind == 'p' and self.bj == 0)
            if first:
                self.icnt0 = kb.sb("icnt0", [128, 2, cfg.TB], F32, es=pes)
                kb.dma('sp', self.icnt0[:], I["icnt0"].rearrange("p (c t) -> p c t", c=2), writes=[self.icnt0])

            def pooled(g, src, wdt):
                c, lo = g // 2, (g % 2) * 64
                if first:
                    kb.op('dve', lambda e: e.tensor_tensor(out=pl[lo:lo + 64, c, :], in0=src[lo:lo + 64, c, 15:15 + Tn],
                                                           in1=self.icnt0[lo:lo + 64, c, 0:Tn], op=ALU.mult),
                          reads=[src, self.icnt0, pl], writes=[pl])
                    kb.op('dve', lambda e: e.tensor_tensor(out=pl[lo:lo + 64, c, :], in0=pl[lo:lo + 64, c, :],
                                                           in1=pext[lo:lo + 64, c, 15:15 + Tn], op=ALU.subtract),
                          reads=[pl, pext], writes=[pl])
                else:
                    kb.op('dve', lambda e: e.scalar_tensor_tensor(
                        out=pl[lo:lo + 64, c, :], in0=src[lo:lo + 64, c, 15:15 + Tn], scalar=1.0 / wdt,
                        in1=pext[lo:lo + 64, c, 15:15 + Tn], op0=ALU.mult, op1=ALU.subtract), reads=[src, pext, pl], writes=[pl])

            kb.op('dve', lambda e: e.tensor_tensor(out=bA[:, :, 1:n], in0=pext[:, :, 1:n], in1=pext[:, :, 0:n - 1], op=ALU.add),
                  reads=[pext], writes=[bA])
            kb.op('dve', lambda e: e.tensor_tensor(out=bB[:, :, 3:n], in0=bA[:, :, 3:n], in1=bA[:, :, 1:n - 2], op=ALU.add),
                  reads=[bA], writes=[bB])
            pooled(0, bA, 2)
            pooled(1, bB, 4)
            yield
            kb.op('dve', lambda e: e.tensor_tensor(out=bA[:, :, 7:n], in0=bB[:, :, 7:n], in1=bB[:, :, 3:n - 4], op=ALU.add),
                  reads=[bB, bA], writes=[bA])
            kb.op('dve', lambda e: e.tensor_tensor(out=bB[:, :, 15:n], in0=bA[:, :, 15:n], in1=bA[:, :, 7:n - 8], op=ALU.add),
                  reads=[bA, bB], writes=[bB])
            pooled(2, bA, 8)
            pooled(3, bB, 16)
            yield
            kb.op('dve', lambda e: e.tensor_copy(out=pl16[:], in_=pl[:]), reads=[pl], writes=[pl16])
            pvc = self.pv_c
            for c in range(2):
                ps = kb.ps()
                self.mm(ps, ps[:, 0:Tn], wsm, wsm[:, 1280 + c * 128:1280 + (c + 1) * 128], pl16, pl16[:, c, :], True, True)
                kb.op('act', lambda e, c=c, ps=ps: e.activation(out=brT[:, 4 + c, 0:Tn], in_=ps[:, 0:Tn], func=AF.Copy,
                                                                scale=pvc[:, l, c, 34:35]), reads=[ps, pvc, brT], writes=[brT])
            yield
            cacc = kb.sb("cacc", [128, 2, Tn], F32, es=pes)
            cext16 = kb.sb("cext16", [128, 2, 30 + Tn], BF16, es=pes)
            kb.op('act', lambda e: e.copy(out=cext16[:], in_=cext[:]), reads=[cext], writes=[cext16])
            for c in range(2):
                dg = kb.sb("dg%d" % c, [128, 31, 128], BF16, es=pes)
                for k in range(31):
                    kb.op('dve', lambda e: e.tensor_scalar(out=dg[:, k, :], in0=self.ident16[:, :], scalar1=pvc[:, l, c, k:k + 1],
                                                            scalar2=0.0, op0=ALU.mult, op1=ALU.add),
                          reads=[self.ident16, pvc, dg], writes=[dg])
                yield
                ps = kb.ps()
                for k in range(31):
                    self.mm(ps, ps[:, 0:Tn], dg, dg[:, k, :], cext16, cext16[:, c, k:k + Tn], k == 0, k == 30, inc=(k == 30))
                kb.op('act', lambda e: e.activation(out=cacc[:, c, :], in_=ps[:, 0:Tn], func=AF.Identity, bias=pvc[:, l, c, 31:32]),
                      reads=[ps, pvc, cacc], writes=[cacc])
            yield
            self.lnfm(pes, cacc, 2, Tn, self.ones256, lambda c: pvc[:, l, c, 32:33], lambda c: pvc[:, l, c, 33:34], pvc,
                      [(brT, lambda c: brT[:, 6 + c, 0:Tn])], func=AF.Silu)
            kb.barrier()

    def cur_wsm(self, l):
        for i in range(self.wpos - 1, -1, -1):
            if self.wplan[i][0] == "wsm":
                return self.wslot[i]
        raise AssertionError

    def key_supers(self):
        sups, cur = [], []
        for kt in self.ktiles:
            if kt[2] < 128 or (cur and cur[0][0] != kt[0]):
                if cur:
                    sups.append(cur)
                    cur = []
            cur.append(kt)
            if len(cur) == 4 or kt[2] < 128:
                sups.append(cur)
                cur = []
        if cur:
            sups.append(cur)
        return sups

    def ksrc(self, name, l, sup):
        I, O = self.I, self.O
        kindsrc, r0, kr, _ = sup[0]
        nrows = sum(t[2] for t in sup)
        if kindsrc == 'c':
            ap = I["c_" + name][l, r0:r0 + nrows, :]
            dep = []
        else:
            ap = O[self.pre + "_" + name][l, r0:r0 + nrows, :]
            dep = [self.odeps[name]]
        if kr == 128:
            return ap.rearrange("(t p) f -> p t f", p=128), dep, 128
        return ap.rearrange("(t p) f -> p t f", t=1), dep, kr

    def vis(self, abs_pos, kr):
        if self.kind == 's':
            return 0, None
        a = (abs_pos - self.P) // 128
        if a < 0:
            return 0, None
        return a * 128, a

    def phase_mla(self, l, brT, qaT, bg=None):
        cfg, kb = self.cfg, self.kb
        Tn, qts = self.Tn, self.qts
        NQ = len(qts)
        wn, wv = self.wn, self.wv
        with ExitStack() as pes:
            accs = [kb.ps('acc') for _ in range(4)]
            first_acc = [True] * 4
            kin16 = [kb.sb("kinA16_%d" % i, [128, 4, 160], BF16, es=pes) for i in range(2)]
            kin2 = [T(k.t) for k in kin16]
            ckvT = [kb.sb("ckvT%d" % i, [128, 512], BF16, es=pes) for i in range(2)]
            krT = [kb.sb("krT%d" % i, [32, 512], BF16, es=pes) for i in range(2)]
            KA = [kb.sb("KA%d" % i, [96, 4, 512], BF16, es=pes) for i in range(2)]
            VA = [kb.sb("VA%d" % i, [128, 4, 4, 65], BF16, es=pes) for i in range(2)]
            PT = [kb.sb("PTa%d" % i, [128, 512], BF16, es=pes) for i in range(4)]
            for v in VA:
                kb.op('dve', lambda e, v=v: e.memset(v[:, :, :, 64:65], 1.0), writes=[v])
            pti = 0
            pend = []
            for si, sup in enumerate(self.key_supers()):
                b = si % 2
                nt = len(sup)
                a1, d1, kr = self.ksrc("ckv", l, sup)
                a2, d2, _ = self.ksrc("krope", l, sup)
                nk = sum(t[2] for t in sup)
                kb.dma('pool', kin16[b][:kr, 0:nt, 0:128], a1, reads=d1, writes=[kin16[b]])
                kb.dma('pool', kin16[b][:kr, 0:nt, 128:160], a2, reads=d2, writes=[kin2[b]])
                pt = kb.ps('misc')
                ptb = pt[:, :].bitcast(BF16)
                for t in range(nt):
                    self.tr(pt, ptb[:, t * 128:t * 128 + kr], kin16[b], kin16[b][:kr, t, 0:128], kr)
                    self.tr(pt, ptb[0:32, 512 + t * 128:512 + t * 128 + kr], kin2[b], kin16[b][:kr, t, 128:160], kr)
                kb.op('act', lambda e: e.copy(out=ckvT[b][:, 0:nk], in_=ptb[:, 0:nk]), reads=[pt], writes=[ckvT[b]])
                kb.op('act', lambda e: e.copy(out=krT[b][:, 0:nk], in_=ptb[0:32, 512:512 + nk]), reads=[pt], writes=[krT[b]])
                for h in range(4):
                    pk = kb.ps('misc')
                    self.mm(pk, pk[0:96, 0:nk], wn, wn[:, h, :], ckvT[b], ckvT[b][:, 0:nk], True, False)
                    self.mm(pk, pk[0:96, 0:nk], self.esel, self.esel[:, :], krT[b], krT[b][:, 0:nk], False, True)
                    kb.op('act', lambda e, h=h, pk=pk: e.copy(out=KA[b][:, h, 0:nk], in_=pk[0:96, 0:nk]), reads=[pk, KA[b]], writes=[KA[b]])
                for t in range(nt):
                    pvp = kb.ps('misc')
                    self.mm(pvp, pvp[:kr, 0:256], ckvT[b], ckvT[b][:, t * 128:t * 128 + kr], wv, wv[:, :], True, True)
                    kb.op('act', lambda e, t=t, pvp=pvp: e.copy(out=VA[b][:kr, t, :, 0:64],
                                                                in_=pvp[:kr, 0:256].rearrange("p (h c) -> p h c", h=4)),
                          reads=[pvp, VA[b]], writes=[VA[b]])
                for t, (_, _, kr_t, apos) in enumerate(sup):
                    qlo, diag = self.vis(apos, kr_t)
                    if qlo >= Tn:
                        continue
                    nq = Tn - qlo
                    for h in range(4):
                        psq = kb.ps('qk')
                        self.mm(psq, psq[:kr_t, 0:nq], KA[b], KA[b][:, h, t * 128:t * 128 + kr_t], qaT, qaT[:, h, qlo:Tn], True, True)
                        p = PT[pti % len(PT)]
                        pti += 1
                        kb.op('act', lambda e: e.activation(out=p[:kr_t, 0:nq], in_=psq[:kr_t, 0:nq], func=AF.Exp, scale=A_SCALE),
                              reads=[psq], writes=[p])
                        if diag is not None:
                            kb.op('pool', lambda e: e.memset(p[64:128, 0:64], 0.0), reads=[p], writes=[p])

                        def pv(p=p, h=h, t=t, b=b, kr_t=kr_t, qlo=qlo):
                            vq = [(qi, q0, r) for qi, (q0, r) in enumerate(qts) if q0 >= qlo]
                            for j, (qi, q0, r) in enumerate(vq):
                                self.mm(accs[h], accs[h][:r, qi * 65:qi * 65 + 65], p, p[:kr_t, q0 - qlo:q0 - qlo + r],
                                        VA[b], VA[b][:kr_t, t, h, :], first_acc[h], True, inc=(j == len(vq) - 1), skip_group_check=True)
                                first_acc[h] = False
                        pend.append(pv)
                        if len(pend) > 1:
                            pend.pop(0)()
                        if bg is not None:
                            next(bg, None)
            while pend:
                pend.pop(0)()
            if bg is not None:
                for _ in bg:
                    pass
            self.attn_finish(pes, accs, brT, 0)
            kb.barrier()

    def attn_finish(self, pes, accs, brT, chunk0, qsel=None):
        kb = self.kb
        qts = self.qts if qsel is None else qsel
        recs = [kb.sb("rec%d" % i, [128, 4], F32, es=pes) for i in range(2)]
        obs = [kb.sb("ob%d" % i, [128, 256], BF16, es=pes) for i in range(2)]
        for gi, (q0, r) in enumerate(qts):
            rec, ob = recs[gi % 2], obs[gi % 2]
            for h in range(4):
                kb.op('dve', lambda e, h=h: e.reciprocal(out=rec[:r, h:h + 1], in_=accs[h][:r, gi * 65 + 64:gi * 65 + 65]),
                      reads=[accs[h], rec], writes=[rec])
                kb.op('act', lambda e, h=h: e.activation(out=ob[:r, h * 64:(h + 1) * 64], in_=accs[h][:r, gi * 65:gi * 65 + 64],
                                                         func=AF.Copy, scale=rec[:r, h:h + 1]), reads=[accs[h], rec, ob], writes=[ob])
            pt = kb.ps('misc')
            ptb = pt[:, :].bitcast(BF16)
            for c in range(2):
                self.tr(pt, ptb[:, c * 128:c * 128 + r], ob, ob[:r, c * 128:(c + 1) * 128], r)
            kb.op('act', lambda e: e.copy(out=brT[:, chunk0:chunk0 + 2, q0:q0 + r],
                                          in_=ptb[:, 0:256].rearrange("p (a b) -> p a b", a=2)[:, :, 0:r]),
                  reads=[pt, brT], writes=[brT])

    def dsa_open(self, l, pes, qiT, wqs):
        cfg, kb = self.cfg, self.kb
        Tn, qts, P, kind = self.Tn, self.qts, self.P, self.kind
        NK = sum(t[2] for t in self.ktiles)
        sups = self.key_supers()
        st = {}
        kidxT = kb.sb("kidxT", [32, NK], BF16, es=pes)
        kiin16 = [kb.sb("kiin16_%d" % i, [128, 4, 32], BF16, es=pes) for i in range(2)]
        k0 = 0
        koffs = []
        for si, sup in enumerate(sups):
            b = si % 2
            nt = len(sup)
            a1, d1, kr = self.ksrc("kidx", l, sup)
            nk = sum(t[2] for t in sup)
            kb.dma('pool', kiin16[b][:kr, 0:nt, :], a1, reads=d1, writes=[kiin16[b]])
            pt = kb.ps('misc')
            ptb = pt[:, :].bitcast(BF16)
            for t in range(nt):
                self.tr(pt, ptb[0:32, t * 128:t * 128 + kr], kiin16[b], kiin16[b][:kr, t, :], kr)
            kb.op('act', lambda e: e.copy(out=kidxT[:, k0:k0 + nk], in_=ptb[0:32, 0:nk]), reads=[pt, kidxT], writes=[kidxT])
            koffs.append(k0)
            k0 += nk
        GQ = cfg.GQ
        iscs = [kb.sb("isc%d" % i, [128, NK], F32, es=pes) for i in range(2)]
        mask = kb.sb("mask", [128, GQ, NK], BF16, es=pes)
        mask1 = T(mask.t)
        mtok = [mask, mask1] + [T(mask.t) for _ in range(max(0, GQ - 2))]
        rl = [kb.sb("rl%d" % i, [128, 512], F32, es=pes) for i in range(2)]
        bss = [kb.sb("bs%d" % i, [128, 8 + NIT], F32, es=pes) for i in range(2)]
        st.update(dict(NK=NK, sups=sups, koffs=koffs, kidxT=kidxT, iscs=iscs, mask=mask, mask1=mask1, rl=rl, bss=bss,
                       qiT=qiT, wqs=wqs, GQ=GQ, rli=0, mtok=mtok))
        return st

    def dsa_pass1(self, l, st, g0):
        cfg, kb = self.cfg, self.kb
        Tn, qts, P, kind = self.Tn, self.qts, self.P, self.kind
        NK, sups, koffs, kidxT, iscs, mask, mask1, rl, bss, qiT, wqs, GQ = (st[k] for k in (
            "NK", "sups", "koffs", "kidxT", "iscs", "mask", "mask1", "rl", "bss", "qiT", "wqs", "GQ"))
        rli = st["rli"]
        grp = qts[g0:g0 + 2]
        mtok = st["mtok"]
        ms0 = g0
        qc0 = grp[0][0]
        qc1 = grp[-1][0] + grp[-1][1]
        nvs = []
        for gi, (q0, r) in enumerate(grp):
            qi = g0 + gi
            isc = iscs[gi % 2]
            nvis = (P + q0 + 128) if kind == 'p' else NK
            nvs.append(nvis)
            for si, sup in enumerate(sups):
                kk0 = koffs[si]
                if kk0 >= nvis:
                    break
                nk = min(sum(t[2] for t in sup), nvis - kk0)
                for h in range(4):
                    psi = kb.ps('qk')
                    self.mm(psi, psi[:r, 0:nk], qiT, qiT[:, h, q0:q0 + r], kidxT, kidxT[:, kk0:kk0 + nk], True, True)
                    if h == 0:
                        kb.op('dve', lambda e: e.tensor_scalar(out=isc[:r, kk0:kk0 + nk], in0=psi[:r, 0:nk], scalar1=0.0,
                                                               scalar2=wqs[:r, qi, 0:1], op0=ALU.max, op1=ALU.mult),
                              reads=[psi, wqs, isc], writes=[isc])
                    else:
                        rr = rl[rli % 2]
                        rli += 1
                        kb.op('act', lambda e: e.activation(out=rr[:r, 0:nk], in_=psi[:r, 0:nk], func=AF.Relu),
                              reads=[psi], writes=[rr])
                        kb.op('dve', lambda e: e.scalar_tensor_tensor(out=isc[:r, kk0:kk0 + nk], in0=rr[:r, 0:nk],
                                                                      scalar=wqs[:r, qi, h:h + 1], in1=isc[:r, kk0:kk0 + nk],
                                                                      op0=ALU.mult, op1=ALU.add),
                              reads=[rr, wqs, isc], writes=[isc])
                yield
            bs = bss[gi % 2]
            kb.op('dve', lambda e: e.memset(bs[:], 0.0), writes=[bs])
            kb.op('dve', lambda e: e.tensor_reduce(out=bs[:r, 0:1], in_=isc[:r, 0:nvis], axis=AX.X, op=ALU.max), reads=[isc, bs], writes=[bs])
            kb.op('dve', lambda e: e.tensor_reduce(out=bs[:r, 1:2], in_=isc[:r, 0:nvis], axis=AX.X, op=ALU.min), reads=[isc, bs], writes=[bs])
            if kind == 'p':
                kb.op('dve', lambda e: e.memset(isc[0:64, nvis - 64:nvis], NEG_BIG), reads=[isc], writes=[isc])
            kb.op('dve', lambda e: e.tensor_tensor(out=bs[:r, 2:3], in0=bs[:r, 0:1], in1=bs[:r, 1:2], op=ALU.subtract), reads=[bs], writes=[bs])
            if gi % 2 == 1:
                kb.op('dve', lambda e: e.tensor_scalar(out=bs[:r, 1:2], in0=bs[:r, 1:2], scalar1=-1.0, scalar2=0.0, op0=ALU.mult, op1=ALU.add),
                      reads=[bs], writes=[bs])
        def act_update(it_):
            f_ = 2.0 ** -it_
            (q0, r), isc, bs, nvis = grp[1], iscs[1], bss[1], nvs[1]
            kb.op('dve', lambda e: e.tensor_scalar(out=bs[:r, 4:5], in0=bs[:r, 7 + it_:8 + it_], scalar1=2.0 * self.ktop - nvis - 0.5,
                                                   scalar2=-f_, op0=ALU.is_ge, op1=ALU.mult), reads=[bs], writes=[bs])
            kb.op('dve', lambda e: e.scalar_tensor_tensor(out=bs[:r, 1:2], in0=bs[:r, 4:5], scalar=bs[:r, 2:3], in1=bs[:r, 1:2],
                                                          op0=ALU.mult, op1=ALU.add), reads=[bs], writes=[bs])

        for it in range(1, NIT + 1):
            f = 2.0 ** -it
            yield
            if len(grp) > 1:
                if it > 1:
                    act_update(it - 1)
                (q0, r), isc, bs, nvis = grp[1], iscs[1], bss[1], nvs[1]
                kb.op('dve', lambda e: e.scalar_tensor_tensor(out=bs[:r, 3:4], in0=bs[:r, 2:3], scalar=-f, in1=bs[:r, 1:2],
                                                              op0=ALU.mult, op1=ALU.add), reads=[bs], writes=[bs])
            (q0, r), isc, bs, nvis = grp[0], iscs[0], bss[0], nvs[0]
            kb.op('dve', lambda e: e.scalar_tensor_tensor(out=bs[:r, 3:4], in0=bs[:r, 2:3], scalar=f, in1=bs[:r, 1:2],
                                                          op0=ALU.mult, op1=ALU.add), reads=[bs], writes=[bs])
            kb.op('dve', lambda e: e.tensor_scalar(out=mask[:r, ms0, 0:nvis], in0=isc[:r, 0:nvis], scalar1=bs[:r, 3:4], scalar2=0.0,
                                                   op0=ALU.is_ge, op1=ALU.add, accum_out=bs[:r, 7 + it:8 + it]),
                  reads=[isc, bs, mtok[ms0]], writes=[mtok[ms0], bs])
            kb.op('dve', lambda e: e.tensor_scalar(out=bs[:r, 4:5], in0=bs[:r, 7 + it:8 + it], scalar1=float(self.ktop) - 0.5,
                                                   scalar2=f, op0=ALU.is_ge, op1=ALU.mult), reads=[bs], writes=[bs])
            kb.op('dve', lambda e: e.scalar_tensor_tensor(out=bs[:r, 1:2], in0=bs[:r, 4:5], scalar=bs[:r, 2:3], in1=bs[:r, 1:2],
                                                          op0=ALU.mult, op1=ALU.add), reads=[bs], writes=[bs])
            if len(grp) > 1:
                yield
                (q0, r), isc, bs, nvis = grp[1], iscs[1], bss[1], nvs[1]
                kb.op('act', lambda e: e.activation(out=mask[:r, ms0 + 1, 0:nvis], in_=isc[:r, 0:nvis], func=AF.Sign, bias=bs[:r, 3:4], scale=1.0,
                                                    accum_out=bs[:r, 7 + it:8 + it]), reads=[isc, bs, mtok[ms0 + 1]], writes=[mtok[ms0 + 1], bs])
        if len(grp) > 1:
            yield
            act_update(NIT)
        for gi, (q0, r) in enumerate(grp):
            isc, bs, nvis = iscs[gi % 2], bss[gi % 2], nvs[gi]
            mk = mtok[ms0 + gi]
            if gi % 2 == 1:
                kb.op('dve', lambda e: e.tensor_scalar(out=bs[:r, 5:6], in0=bs[:r, 1:2], scalar1=-1.0, scalar2=0.0, op0=ALU.mult, op1=ALU.add),
                      reads=[bs], writes=[bs])
                tcol = bs[:r, 5:6]
            else:
                tcol = bs[:r, 1:2]
            kb.op('dve', lambda e: e.tensor_scalar(out=mask[:r, ms0 + gi, 0:nvis], in0=isc[:r, 0:nvis], scalar1=tcol, scalar2=1.0,
                                                   op0=ALU.is_ge, op1=ALU.mult), reads=[isc, bs, mk], writes=[mk])
        st["rli"] = rli
        yield

    def dsa_rest(self, l, pes, st, brT, bqT, first_done):
        cfg, kb = self.cfg, self.kb
        Tn, qts, P, kind = self.Tn, self.qts, self.P, self.kind
        NK, sups, koffs, mask, mask1, GQ, mtok = (st[k] for k in ("NK", "sups", "koffs", "mask", "mask1", "GQ", "mtok"))
        bk16 = [kb.sb("bk16_%d" % i, [128, 4, 256], BF16, es=pes) for i in range(2)]
        KBt = [kb.sb("KBt%d" % i, [128, 2, 512], BF16, es=pes) for i in range(2)]
        VB = [kb.sb("VB%d" % i, [128, 4, 4, 65], BF16, es=pes) for i in range(2)]
        Eb = [kb.sb("Eb%d" % i, [128, 512], BF16, es=pes) for i in range(2)]
        tb = [kb.sb("tbias%d" % i, [128, 260], F32, es=pes) for i in range(2)]
        PT = [kb.sb("PTb%d" % i, [128, 512], BF16, es=pes) for i in range(4)]
        self.fin_bufs = [(kb.sb("recb%d" % i, [128, 4], F32, es=pes), kb.sb("obb%d" % i, [128, 256], BF16, es=pes)) for i in range(2)]
        for v in VB:
            kb.op('dve', lambda e, v=v: e.memset(v[:, :, :, 64:65], 1.0), writes=[v])
        VBd = [[T(v.t) for _ in range(4)] for v in VB]
        for g0 in range(0, len(qts), GQ):
            grp = qts[g0:g0 + GQ]
            qc0 = grp[0][0]
            qc1 = grp[-1][0] + grp[-1][1]
            accs = [kb.ps('acc') for _ in range(4)]
            first_acc = [True] * 4
            pti = 0
            pend = []
            nvis_g = (P + qc1) if kind == 'p' else NK
            for si, sup in enumerate(sups):
                kk0 = koffs[si]
                if kk0 >= nvis_g:
                    break
                b = si % 2
                nt = len(sup)
                a1, d1, kr = self.ksrc("bk", l, sup)
                a2, d2, _ = self.ksrc("bv", l, sup)
                kb.dma('pool', bk16[b][:kr, 0:nt, :], a1, reads=d1, writes=[bk16[b]])
                for t in range(nt):
                    kb.dma('pool', VB[b][:kr, t, :, 0:64], a2[:, t, :].rearrange("p (h c) -> p h c", h=4), reads=d2, writes=[VBd[b][t]])
                pt = kb.ps('misc')
                ptb = pt[:, :].bitcast(BF16)
                for t in range(nt):
                    for pr in range(2):
                        self.tr(pt, ptb[:, pr * 512 + t * 128:pr * 512 + t * 128 + kr], bk16[b], bk16[b][:kr, t, pr * 128:(pr + 1) * 128], kr)
                nk = sum(t[2] for t in sup)
                kb.op('act', lambda e: e.copy(out=KBt[b][:, :, 0:nk], in_=ptb[:, :].rearrange("p (a b) -> p a b", a=2)[:, :, 0:nk]),
                      reads=[pt], writes=[KBt[b]])
                for t, (_, _, kr_t, apos) in enumerate(sup):
                    qlo, diag = self.vis(apos, kr_t)
                    qlo = max(qlo, qc0)
                    if qlo >= qc1:
                        continue
                    nq = qc1 - qlo
                    kcol = kk0 + t * 128
                    pm = kb.ps('misc')
                    pmb = pm[:, :].bitcast(BF16)
                    for gi, (q0, r) in enumerate(grp):
                        if q0 < qlo:
                            continue
                        self.tr(pm, pmb[:kr_t, q0 - qc0:q0 - qc0 + r], mtok[gi], mask[:r, gi, kcol:kcol + kr_t], r)
                    a = (apos - P) // 128
                    blo = max(qlo, 128 * a) if a >= -1 else None
                    bhi = min(qc1, 128 * a + 257) if a >= -1 else None
                    for h in range(4):
                        hp, ho = h // 2, (h % 2) * 64
                        psq = kb.ps('qk')
                        self.mm(psq, psq[:kr_t, 0:nq], KBt[b], KBt[b][ho:ho + 64, hp, t * 128:t * 128 + kr_t],
                                bqT, bqT[ho:ho + 64, hp, qlo:qc1], True, True)
                        E = Eb[pti % len(Eb)]
                        p = PT[pti % len(PT)]
                        tbt = tb[pti % len(tb)]
                        pti += 1
                        if blo is not None and blo < bhi:
                            wdt = bhi - blo
                            kb.op('dve', lambda e: e.scalar_tensor_tensor(
                                out=tbt[:kr_t, 0:wdt], in0=psq[:kr_t, blo - qlo:bhi - qlo], scalar=B_SCALE,
                                in1=self.gt[:kr_t, h, blo - 128 * a:bhi - 128 * a], op0=ALU.mult, op1=ALU.add),
                                reads=[psq, self.gt, tbt], writes=[tbt])
                            kb.op('act', lambda e: e.activation(out=E[:kr_t, blo - qlo:bhi - qlo], in_=tbt[:kr_t, 0:wdt],
                                                                func=AF.Exp), reads=[tbt, E], writes=[E])
                            for (c0, c1) in ((qlo, blo), (bhi, qc1)):
                                if c1 > c0:
                                    kb.op('act', lambda e: e.activation(
                                        out=E[:kr_t, c0 - qlo:c1 - qlo], in_=psq[:kr_t, c0 - qlo:c1 - qlo], func=AF.Exp, scale=B_SCALE),
                                        reads=[psq, E], writes=[E])
                        else:
                            kb.op('act', lambda e: e.activation(out=E[:kr_t, 0:nq], in_=psq[:kr_t, 0:nq], func=AF.Exp, scale=B_SCALE),
                                  reads=[psq, E], writes=[E])
                        kb.op('dve', lambda e: e.tensor_tensor(out=p[:kr_t, 0:nq], in0=E[:kr_t, 0:nq],
                                                               in1=pmb[:kr_t, qlo - qc0:qc1 - qc0], op=ALU.mult),
                              reads=[E, pm, p], writes=[p])

                        def pv(p=p, h=h, hp=hp, t=t, b=b, kr_t=kr_t, qlo=qlo):
                            vq = [(gi, q0, r) for gi, (q0, r) in enumerate(grp) if q0 >= qlo]
                            for j, (gi, q0, r) in enumerate(vq):
                                col = gi * 65
                                self.mm(accs[h], accs[h][:r, col:col + 65], p, p[:kr_t, q0 - qlo:q0 - qlo + r],
                                        VBd[b][t], VB[b][:kr_t, t, h, :], first_acc[h], True, inc=(j == len(vq) - 1), skip_group_check=True)
                                first_acc[h] = False
                        pend.append(pv)
                        if len(pend) > 1:
                            pend.pop(0)()
            while pend:
                pend.pop(0)()
            self.dsa_finish(pes, accs, brT, grp, GQ)

    def dsa_finish(self, pes, accs, brT, grp, GQ):
        kb = self.kb
        if not hasattr(self, "_dsa_fin"):
            self._dsa_fin = 0
        for gi, (q0, r) in enumerate(grp):
            rec, ob = self.fin_bufs[gi % 2]
            for h in range(4):
                a = accs[h]
                col = gi * 65
                kb.op('dve', lambda e, h=h, a=a, col=col: e.reciprocal(out=rec[:r, h:h + 1], in_=a[:r, col + 64:col + 65]),
                      reads=[a, rec], writes=[rec])
                kb.op('act', lambda e, h=h, a=a, col=col: e.activation(out=ob[:r, h * 64:(h + 1) * 64], in_=a[:r, col:col + 64],
                                                                       func=AF.Copy, scale=rec[:r, h:h + 1]), reads=[a, rec, ob], writes=[ob])
            pt = kb.ps('misc')
            ptb = pt[:, :].bitcast(BF16)
            for c in range(2):
                self.tr(pt, ptb[:, c * 128:c * 128 + r], ob, ob[:r, c * 128:(c + 1) * 128], r)
            kb.op('act', lambda e: e.copy(out=brT[:, 2:4, q0:q0 + r],
                                          in_=ptb[:, 0:256].rearrange("p (a b) -> p a b", a=2)[:, :, 0:r]),
                  reads=[pt, brT], writes=[brT])
        self._dsa_fin += 1

    def phase_merge(self, l, brT):
        cfg, kb = self.cfg, self.kb
        Tn = self.Tn
        x, xT = self.x, self.xT
        with ExitStack() as pes:
            mixed = kb.sb("mixed", [128, 8, Tn], F32, es=pes)
            mixed16 = kb.sb("mixed16", [128, 8, Tn], BF16, es=pes)
            sgs = [kb.sb("sgm%d" % i, [128, Tn], F32, es=pes) for i in range(2)]
            tmp = [kb.sb("tmpm%d" % i, [128, Tn], F32, es=pes) for i in range(2)]
            i = 0
            for hf in range(2):
                for n in range(4):
                    wbr = self.getw("wbr%d%d" % (n, hf), l)
                    wbr3 = wbr[:, 0:1024].rearrange("p (k n) -> p k n", k=2)
                    wg = self.getw("wg%d%d" % (n, hf), l, hold=1)
                    wg3 = wg[:, :].rearrange("p (k n) -> p k n", k=8)
                    for dl in range(4):
                        dc = hf * 4 + dl
                        pg = kb.ps()
                        for k in range(8):
                            self.mm(pg, pg[:, 0:Tn], wg, wg3[:, k, dl * 128:(dl + 1) * 128], xT, xT[:, k, 0:Tn], k == 0, k == 7, inc=(k == 7))
                        pb = kb.ps()
                        for k in range(2):
                            self.mm(pb, pb[:, 0:Tn], wbr, wbr3[:, k, dl * 128:(dl + 1) * 128], brT, brT[:, n * 2 + k, 0:Tn], k == 0, k == 1, inc=(k == 1))
                        sg = sgs[i % 2]
                        tm = tmp[i % 2]
                        i += 1
                        kb.op('act', lambda e: e.activation(out=sg[:, :], in_=pg[:, 0:Tn], func=AF.Sigmoid), reads=[pg], writes=[sg])
                        if n == 0:
                            kb.op('dve', lambda e: e.tensor_tensor(out=mixed[:, dc, :], in0=pb[:, 0:Tn], in1=sg[:, :], op=ALU.mult),
                                  reads=[pb, sg, mixed], writes=[mixed])
                        else:
                            kb.op('dve', lambda e: e.tensor_tensor(out=tm[:, :], in0=pb[:, 0:Tn], in1=sg[:, :], op=ALU.mult),
                                  reads=[pb, sg], writes=[tm])
                            if n < 3:
                                kb.op('dve', lambda e: e.tensor_tensor(out=mixed[:, dc, :], in0=mixed[:, dc, :], in1=tm[:, :], op=ALU.add),
                                      reads=[tm, mixed], writes=[mixed])
                            else:
                                kb.op('dve', lambda e: e.tensor_tensor(out=mixed16[:, dc, :], in0=mixed[:, dc, :], in1=tm[:, :], op=ALU.add),
                                      reads=[tm, mixed, mixed16], writes=[mixed16])
            s = kb.sb("s_ln1", [128, 8, Tn], F32, es=pes)
            for hf in range(2):
                wo = self.getw("wout%d" % hf, l)
                wo3 = wo[:, :].rearrange("p (k n) -> p k n", k=8)
                for dl in range(4):
                    dc = hf * 4 + dl
                    py = kb.ps()
                    for k in range(8):
                        self.mm(py, py[:, 0:Tn], wo, wo3[:, k, dl * 128:(dl + 1) * 128], mixed16, mixed16[:, k, :], k == 0, k == 7, inc=(k == 7))
                    kb.op('dve', lambda e: e.scalar_tensor_tensor(out=s[:, dc, :], in0=x[:, dc, 0:Tn], scalar=cfg.ALPHA, in1=py[:, 0:Tn],
                                                                  op0=ALU.mult, op1=ALU.add), reads=[x, py, s], writes=[s])
            pv = self.pv_ln
            self.lnfm(pes, s, 8, Tn, self.ones1024, lambda c: pv[:, l, 0, c:c + 1], lambda c: pv[:, l, 1, c:c + 1], pv,
                      [(x, lambda c: x[:, c, 0:Tn]), (xT, lambda c: xT[:, c, 0:Tn])])
            kb.barrier()

    def phase_ffn(self, l):
        cfg, kb, I, O = self.cfg, self.kb, self.I, self.O
        Tn, kind, pre, row0, qts = self.Tn, self.kind, self.pre, self.row0, self.qts
        x, xT = self.x, self.xT
        fcar = self.fcar[kind]
        fcv = fcar[:, l, :, :].rearrange("p (g c) t -> p g c t", g=2)
        pvf = self.pv_f
        with ExitStack() as pes:
            hT = kb.sb("hT", [128, NFC, Tn], BF16, es=pes)
            Es = [kb.sb("Effn%d" % i, [128, 2, 2 + Tn], F32, es=pes) for i in range(3)]
            av = [kb.sb("avf%d" % i, [128, 2, Tn], F32, es=pes) for i in range(3)]
            sgl = [kb.sb("sgf%d" % i, [128, Tn], F32, es=pes) for i in range(3)]
            Ecs = [T(E_.t) for E_ in Es]
            ci = 0
            tails = []
            for g in range(6):
                nch = 4 if g < 5 else 2
                wvv = self.getw("wupv%d" % g, l)
                wgg = self.getw("wupg%d" % g, l, hold=1)
                wts = (wvv, wgg)
                w3s = [w_[:, 0:8 * nch * 128].rearrange("p (k n) -> p k n", k=8) for w_ in wts]
                for cl in range(nch):
                    c = g * 4 + cl
                    E = Es[ci % 3]
                    a = av[ci % 3]
                    sg = sgl[ci % 3]
                    ci += 1
                    Ec = Ecs[(ci - 1) % 3]
                    kb.op('dve', lambda e, E=E, c=c: e.tensor_copy(out=E[:, :, 0:2], in_=fcv[:, :, c, :]), reads=[fcar], writes=[Ec])
                    for gg in range(2):
                        cc = gg * NFC + c
                        ps = kb.ps()
                        for k in range(8):
                            self.mm(ps, ps[:, 0:Tn], wts[gg], w3s[gg][:, k, cl * 128:(cl + 1) * 128], xT, xT[:, k, 0:Tn], k == 0, k == 7, inc=(k == 7))
                        kb.op('act', lambda e: e.copy(out=E[:, gg, 2:2 + Tn], in_=ps[:, 0:Tn]), reads=[ps, E], writes=[E])
                        kb.op('act', lambda e: e.activation(out=a[:, gg, :], in_=ps[:, 0:Tn], func=AF.Identity, scale=pvf[:, l, cc, 2:3],
                                                            bias=pvf[:, l, cc, 3:4]), reads=[ps, pvf, a], writes=[a])
                    kb.op('dve', lambda e: e.tensor_copy(out=fcv[:, :, c, :], in_=E[:, :, Tn:Tn + 2]), reads=[E, fcar], writes=[fcar])
                    for gg in range(2):
                        cc = gg * NFC + c
                        for k in (1, 0):
                            kb.op('dve', lambda e: e.scalar_tensor_tensor(
                                out=a[:, gg, :], in0=E[:, gg, k:k + Tn], scalar=pvf[:, l, cc, k:k + 1], in1=a[:, gg, :], op0=ALU.mult, op1=ALU.add),
                                reads=[E, Ec, pvf, a], writes=[a])
                    def tail(a=a, sg=sg, c=c):
                        kb.op('act', lambda e: e.activation(out=sg[:, :], in_=a[:, 1, :], func=AF.Silu), reads=[a], writes=[sg])
                        kb.op('dve', lambda e: e.tensor_tensor(out=hT[:, c, :], in0=a[:, 0, :], in1=sg[:, :], op=ALU.mult),
                              reads=[a, sg, hT], writes=[hT])
                    tails.append(tail)
                    if len(tails) > 1:
                        tails.pop(0)()
            while tails:
                tails.pop(0)()
            if self.last_block:
                sts = [kb.sb("stf%d" % i, [2, 512], F32, es=pes) for i in range(2)]
                for c0 in range(0, 44, 4):
                    st = sts[(c0 // 4) % 2]
                    ps = kb.ps()
                    for i in range(4):
                        self.tr(ps, ps[0:2, i * 128:(i + 1) * 128], fcar, fcar[:, l, c0 + i, :], 128, bf=False)
                    kb.op('act', lambda e: e.copy(out=st[0:2, :], in_=ps[0:2, :]), reads=[ps, st], writes=[st])
                    kb.dma('sp', O[pre + "_ffn"][l, :, c0 * 128:(c0 + 4) * 128], st[0:2, :], reads=[st])
            s = kb.sb("s_ln2", [128, 8, Tn], F32, es=pes)
            for dc in range(8):
                w = self.getw("wdn%d" % dc, l)
                w3 = w[:, 0:NFC * 128].rearrange("p (k n) -> p k n", k=NFC)
                py = kb.ps()
                for k in range(NFC):
                    self.mm(py, py[:, 0:Tn], w, w3[:, k, :], hT, hT[:, k, :], k == 0, k == NFC - 1, inc=(k == NFC - 1))
                kb.op('dve', lambda e: e.scalar_tensor_tensor(out=s[:, dc, :], in0=x[:, dc, 0:Tn], scalar=cfg.ALPHA, in1=py[:, 0:Tn],
                                                              op0=ALU.mult, op1=ALU.add), reads=[x, py, s], writes=[s])
            pv = self.pv_ln
            self.lnfm(pes, s, 8, Tn, self.ones1024, lambda c: pv[:, l, 2, c:c + 1], lambda c: pv[:, l, 3, c:c + 1], pv,
                      [(x, lambda c: x[:, c, 0:Tn]), (xT, lambda c: xT[:, c, 0:Tn])])
            if l == cfg.L - 1:
                yts = [kb.sb("ytok%d" % i, [128, D], F32, es=pes) for i in range(2)]
                for qi, (q0, r) in enumerate(qts):
                    yt = yts[qi % 2]
                    for half in range(2):
                        ps = kb.ps()
                        for c in range(4):
                            cc = half * 4 + c
                            self.tr(ps, ps[:r, c * 128:(c + 1) * 128], x, x[:, cc, q0:q0 + r], 128, bf=False)
                        kb.op('act', lambda e, ps=ps, half=half, yt=yt, r=r: e.copy(out=yt[:r, half * 512:(half + 1) * 512], in_=ps[:r, :]),
                              reads=[ps, yt], writes=[yt])
                    kb.dma('sp', O["y_" + pre][row0 + q0:row0 + q0 + r, :], yt[:r, :], reads=[yt])
            kb.barrier()


def _rel_bucket(rel):
    nb, max_exact = 16, 8
    ret = np.where(rel > 0, nb, 0)
    n = np.abs(rel)
    nf = np.maximum(n, 1).astype(np.float32)
    large = max_exact + (np.log(nf / np.float32(max_exact)) / np.float32(math.log(128 / max_exact))
                         * np.float32(nb - max_exact)).astype(np.int32)
    large = np.minimum(large, nb - 1)
    return ret + np.where(n < max_exact, n, large)


def _consts(cfg):
    half = 16
    freqs = (10000.0 ** (-np.arange(half, dtype=np.float32) / half)).astype(np.float32)

    def rope_tab(pos):
        ang = pos.astype(np.float32)[:, None] * freqs[None, :]
        c, s = np.cos(ang).astype(np.float32), np.sin(ang).astype(np.float32)
        return np.concatenate([c, c, -s, s], axis=1).astype(np.float32)

    rope_p = rope_tab(np.arange(cfg.SEQ))
    rope_s = rope_tab(cfg.PAST + np.arange(cfg.TS))
    p = np.arange(128)[:, None]
    jj = np.arange(258)[None, :]
    bidx = _rel_bucket(p - jj).astype(np.float32)
    icnt = np.zeros((128, 2, cfg.TB), np.float32)
    t = np.arange(cfg.TB)
    for g, w in enumerate((2, 4, 8, 16)):
        c, lo = g // 2, (g % 2) * 64
        icnt[lo:lo + 64, c, :] = 1.0 / np.minimum(w, t + 1).astype(np.float32)[None, :]
    return rope_p, rope_s, bidx, icnt.reshape(128, 2 * cfg.TB)


_CACHE = {}


def _get_nc(cfg_key):
    if cfg_key not in _CACHE:
        cfg = Cfg(*cfg_key)
        gen = Gen(cfg)
        nc = bass.Bass("TRN2", target_bir_lowering=False)
        gen.build(nc)
        _CACHE[cfg_key] = (cfg, gen, nc)
    return _CACHE[cfg_key]


def run(inputs, cfg_key):
    cfg, gen, nc = _get_nc(cfg_key)
    L = cfg.L
    f = lambda a: np.ascontiguousarray(np.asarray(a, dtype=np.float32))
    rope_p, rope_s, bidx, icnt0 = _consts(cfg)
    B = inputs["x_prompt"].shape[0]
    NS = inputs["x_sample"].shape[0]
    ncores = 8
    shared = {
        "rel_bias": f(inputs["rel_bias"]).reshape(1, 128), "ln_in_g": f(inputs["ln_in_g"]).reshape(1, D),
        "ln_in_b": f(inputs["ln_in_b"]).reshape(1, D), "rope_p": rope_p, "rope_s": rope_s, "bidx": bidx, "icnt0": icnt0,
    }
    for nm in ("w_in", "a_q_norm", "a_kv_norm", "a_w_qup", "a_w_kvup", "pool_w", "pool_scale", "conv_w", "conv_b",
               "conv_ln_g", "conv_ln_b", "w_branch", "w_out", "ln1_g", "ln1_b", "w_up", "ffn_conv_w", "ffn_conv_b",
               "w_down", "ln2_g", "ln2_b"):
        shared[nm] = f(inputs[nm])
    in_maps = []
    for c in range(ncores):
        m = dict(shared)
        m["xp"] = f(inputs["x_prompt"][c % B])
        s = c % NS
        m["xs"] = f(inputs["x_sample"][s])
        m["c_ckv"] = f(inputs["cache_a_ckv"][:, s]); m["c_krope"] = f(inputs["cache_a_krope"][:, s])
        m["c_bk"] = f(inputs["cache_b_k"][:, s]).reshape(L, cfg.PAST, 256)
        m["c_bv"] = f(inputs["cache_b_v"][:, s]).reshape(L, cfg.PAST, 256)
        m["c_kidx"] = f(inputs["cache_b_kidx"][:, s])
        m["st_pool"] = f(inputs["state_pool"][:, s]); m["st_conv"] = f(inputs["state_conv"][:, s])
        m["st_ffn"] = f(inputs["state_ffn"][:, s])
        in_maps.append(m)
    res = run_bass_kernel_spmd(nc, in_maps, core_ids=list(range(ncores)))
    R = res.results

    def stack_p(name, shape_tail):
        return np.stack([R[b]["p_" + name] for b in range(B)], axis=1).reshape((L, B) + shape_tail)

    def stack_s(name, shape_tail):
        return np.stack([R[s]["s_" + name] for s in range(NS)], axis=1).reshape((L, NS) + shape_tail)

    outs = [np.stack([R[b]["y_p"] for b in range(B)], 0), np.stack([R[s]["y_s"] for s in range(NS)], 0)]
    for st, n in ((stack_p, cfg.SEQ), (stack_s, cfg.TS)):
        outs += [st("ckv", (n, 128)), st("krope", (n, 32)), st("bk", (n, 4, 64)), st("bv", (n, 4, 64)), st("kidx", (n, 32)),
                 st("pool", (15, 256)), st("conv", (30, 256)), st("ffn", (2, 2 * DFF))]
    return tuple(np.ascontiguousarray(o.astype(np.float32)) for o in outs)


def kernel(**inputs):
    return run(inputs, (4096, 4, 4096, 16, 512, 4))
```

```python
import math
from contextlib import ExitStack

import numpy as np
import concourse.bass as bass
import concourse.mybir as mybir
from concourse.bass_utils import run_bass_kernel_spmd

F32 = mybir.dt.float32
BF16 = mybir.dt.bfloat16
ALU = mybir.AluOpType
AF = mybir.ActivationFunctionType
AX = mybir.AxisListType

D = 1024
DIN = 6148
DFF = 2816
NFC = DFF // 128
O_CQ, O_CKV, O_KR, O_BQ, O_BK, O_BV, O_QI, O_KI, O_WI, O_UP, O_UC, O_G = (
    0, 192, 320, 352, 608, 864, 1120, 1248, 1280, 1284, 1540, 2052)
A_SCALE = 96 ** -0.5
B_SCALE = 64 ** -0.5
IDX_SCALE = (4 ** -0.5) * (32 ** -0.5)
LN_EPS = 1e-5
NEG_BIG = -1.0e30
NIT = 12


class Cfg:
    def __init__(self, SEQ=4096, DEPTH=4, PAST=4096, DEC_SEQ=16, TB=512, GQ=2):
        self.SEQ, self.L, self.PAST, self.TS, self.TB, self.GQ = SEQ, DEPTH, PAST, DEC_SEQ, TB, GQ
        self.ALPHA = (2 * DEPTH) ** 0.25
        self.NB = SEQ // TB
        self.KP = min(256, SEQ // 4)
        self.KS = min(256, (PAST + DEC_SEQ) // 4)
        self.NKMAX = max(SEQ, PAST + 128)


class Dep:
    __slots__ = ("w", "r")

    def __init__(self):
        self.w = None
        self.r = []


class T:
    __slots__ = ("t", "d", "ex")

    def __init__(self, t, d=None, ex=False):
        self.t = t
        self.d = d if d is not None else Dep()
        self.ex = ex

    def __getitem__(self, k):
        return self.t[k]


class KB:
    NR = 8

    def __init__(self, nc, es):
        self.nc = nc
        self.es = es
        self.E = {'pe': nc.tensor, 'act': nc.scalar, 'dve': nc.vector, 'pool': nc.gpsimd, 'sp': nc.sync}
        self.sem = {e: es.enter_context(nc.semaphore("s_" + e)) for e in ['pe', 'act', 'dve', 'pool']}
        self.cnt = {e: 0 for e in self.sem}
        self.waited = {e: {} for e in self.E}
        self.dring = {q: [es.enter_context(nc.semaphore("d_%s%d" % (q, i))) for i in range(self.NR)]
                      for q in ['sp', 'pool']}
        self.dcnt = {q: 0 for q in self.dring}
        self.n_ins = 0
        self.n_wait = 0
        self._pend = []
        self.pending_dma = []
        self.ps_banks = [T(es.enter_context(nc.psum_tensor("psb%d" % i, [128, 512], F32)), ex=True) for i in range(8)]
        self.ps_groups = {'any': list(range(8))}
        self.ps_ctr = {}

    def sb(self, name, shape, dt=F32, es=None):
        self._nm = getattr(self, "_nm", 0) + 1
        return T((es or self.es).enter_context(self.nc.sbuf_tensor("%s_%d" % (name, self._nm), list(shape), dt)))

    def ps(self, group='any'):
        banks = self.ps_groups[group]
        i = self.ps_ctr.get(group, 0)
        self.ps_ctr[group] = i + 1
        return self.ps_banks[banks[i % len(banks)]]

    def _wait(self, e, evt):
        if evt is None:
            return
        key, sem, val = evt
        if key == e and e == 'pe':
            return
        w = self.waited[e]
        if w.get(key, 0) >= val:
            return
        self.E[e].wait_ge(sem, val)
        w[key] = val
        self.n_wait += 1

    def _deps(self, e, reads, writes):
        for t in reads:
            self._wait(e, t.d.w)
        for t in writes:
            self._wait(e, t.d.w)
            for ev in t.d.r:
                self._wait(e, ev)

    @staticmethod
    def _mark(evt, reads, writes):
        for t in reads:
            t.d.r.append(evt)
        for t in writes:
            t.d.w = evt
            t.d.r = []

    def op(self, e, fn, reads=(), writes=(), inc=True):
        ex = [t for t in reads if t.ex]
        if ex:
            reads = [t for t in reads if not t.ex]
            writes = list(writes) + ex
        self._deps(e, reads, writes)
        ins = fn(self.E[e])
        self.n_ins += 1
        if not inc:
            self._pend.append((list(reads), list(writes)))
            return None
        ins.then_inc(self.sem[e], 1)
        self.cnt[e] += 1
        evt = (e, self.sem[e], self.cnt[e])
        if e == 'pe' and self._pend:
            for (r_, w_) in self._pend:
                self._mark(evt, r_, w_)
            self._pend = []
        self._mark(evt, reads, writes)
        return evt

    def dma(self, q, out, in_, reads=(), writes=(), **kw):
        self._deps(q, reads, writes)
        i = self.dcnt[q] % self.NR
        gen = self.dcnt[q] // self.NR + 1
        sem = self.dring[q][i]
        key = "d_%s%d" % (q, i)
        if gen > 1:
            self._wait(q, (key, sem, 16 * (gen - 1)))
        self.E[q].dma_start(out=out, in_=in_, **kw).then_inc(sem, 16)
        self.dcnt[q] += 1
        evt = (key, sem, 16 * gen)
        self._mark(evt, reads, writes)
        self.pending_dma.append(evt)
        self.n_ins += 1
        return evt

    def barrier(self, engines=('pe', 'act', 'dve', 'sp', 'pool')):
        evs = [(e, self.sem[e], self.cnt[e]) for e in self.sem if self.cnt[e] > 0]
        evs += self.pending_dma
        self.pending_dma = []
        for e in engines:
            for ev in evs:
                self._wait(e, ev)

    def finish(self):
        self.barrier(engines=('sp',))


class Gen:
    def __init__(self, cfg):
        self.cfg = cfg

    def declare(self, nc):
        c = self.cfg
        L = c.L
        I = {}
        O = {}

        def inp(name, shape):
            I[name] = nc.dram_tensor(name, list(shape), F32, kind="ExternalInput").ap()

        def outp(name, shape):
            O[name] = nc.dram_tensor(name, list(shape), F32, kind="ExternalOutput").ap()

        inp("xp", [c.SEQ, D]); inp("xs", [c.TS, D])
        inp("c_ckv", [L, c.PAST, 128]); inp("c_krope", [L, c.PAST, 32]); inp("c_bk", [L, c.PAST, 256])
        inp("c_bv", [L, c.PAST, 256]); inp("c_kidx", [L, c.PAST, 32])
        inp("st_pool", [L, 15, 256]); inp("st_conv", [L, 30, 256]); inp("st_ffn", [L, 2, 2 * DFF])
        inp("rel_bias", [1, 128]); inp("ln_in_g", [1, D]); inp("ln_in_b", [1, D])
        inp("w_in", [L, D, DIN]); inp("a_q_norm", [L, 192]); inp("a_kv_norm", [L, 128])
        inp("a_w_qup", [L, 192, 384]); inp("a_w_kvup", [L, 128, 512]); inp("pool_w", [L, 4, 64, 64])
        inp("pool_scale", [L, 256]); inp("conv_w", [L, 31, 256]); inp("conv_b", [L, 256])
        inp("conv_ln_g", [L, 256]); inp("conv_ln_b", [L, 256]); inp("w_branch", [L, 4, 256, D])
        inp("w_out", [L, D, D]); inp("ln1_g", [L, D]); inp("ln1_b", [L, D]); inp("w_up", [L, D, 2 * DFF])
        inp("ffn_conv_w", [L, 3, 2 * DFF]); inp("ffn_conv_b", [L, 2 * DFF]); inp("w_down", [L, DFF, D])
        inp("ln2_g", [L, D]); inp("ln2_b", [L, D])
        inp("rope_p", [c.SEQ, 64]); inp("rope_s", [c.TS, 64]); inp("bidx", [128, 258]); inp("icnt0", [128, 2 * c.TB])
        outp("y_p", [c.SEQ, D]); outp("y_s", [c.TS, D])
        for pre, n in (("p", c.SEQ), ("s", c.TS)):
            outp(pre + "_ckv", [L, n, 128]); outp(pre + "_krope", [L, n, 32]); outp(pre + "_bk", [L, n, 256])
            outp(pre + "_bv", [L, n, 256]); outp(pre + "_kidx", [L, n, 32])
            outp(pre + "_pool", [L, 15, 256]); outp(pre + "_conv", [L, 30, 256]); outp(pre + "_ffn", [L, 2, 2 * DFF])
        self.I, self.O = I, O

    def mm(self, ps, out_ap, lt, lhsT, rt, rhs, start, stop, extra_r=(), inc=True, **kw):
        self.kb.op('pe', lambda e: e.matmul(out_ap, lhsT=lhsT, rhs=rhs, start=start, stop=stop, **kw),
                   reads=[lt, rt] + list(extra_r), writes=[ps], inc=inc)

    def tr(self, ps, out_ap, it, in_ap, rows, bf=True):
        idt = self.ident16 if bf else self.ident
        self.kb.op('pe', lambda e: e.transpose(out=out_ap, in_=in_ap, identity=idt[:rows, :rows]),
                   reads=[it, idt], writes=[ps])

    def build(self, nc):
        cfg = self.cfg
        self.declare(nc)
        es = ExitStack()
        with es:
            kb = KB(nc, es)
            self.kb = kb
            self.nc = nc
            self.setup_persistent()
            self.wplan = []
            self.wpos = 0
            self.plan_weights()
            self.wissue = 0
            for j in range(cfg.NB):
                self.block('p', j)
            self.block('s', 0)
            assert self.wpos == len(self.wplan), (self.wpos, len(self.wplan))
            kb.finish()
            self.stats = (kb.n_ins, kb.n_wait)
        return nc

    def setup_persistent(self):
        kb, cfg, I = self.kb, self.cfg, self.I
        L = cfg.L
        sb = kb.sb
        self.ident = sb("ident", [128, 128])
        self.ident16 = sb("ident16", [128, 128], BF16)
        kb.op('dve', lambda e: e.memset(self.ident[:], 1.0), writes=[self.ident])
        kb.op('pool', lambda e: e.affine_select(out=self.ident[:], in_=self.ident[:], pattern=[[-1, 128]],
                                                compare_op=ALU.is_equal, fill=0.0, base=0, channel_multiplier=1),
              reads=[self.ident], writes=[self.ident])
        kb.op('dve', lambda e: e.tensor_copy(out=self.ident16[:], in_=self.ident[:]), reads=[self.ident],
              writes=[self.ident16])
        self.ones1024 = sb("ones1024", [128, 128])
        self.ones256 = sb("ones256", [128, 128])
        kb.op('dve', lambda e: e.memset(self.ones1024[:], 1.0 / 1024), writes=[self.ones1024])
        kb.op('dve', lambda e: e.memset(self.ones256[:], 1.0 / 256), writes=[self.ones256])
        o16a, o16b = sb("ones1024b", [128, 128], BF16), sb("ones256b", [128, 128], BF16)
        kb.op('dve', lambda e: e.memset(o16a[:], 1.0 / 1024), writes=[o16a])
        kb.op('dve', lambda e: e.memset(o16b[:], 1.0 / 256), writes=[o16b])
        self.ones16 = {self.ones1024: o16a, self.ones256: o16b}
        self.esel = sb("esel", [32, 96], BF16)
        kb.op('dve', lambda e: e.memset(self.esel[:], 0.0), writes=[self.esel])
        kb.op('dve', lambda e: e.tensor_copy(out=self.esel[:, 64:96], in_=self.ident[0:32, 0:32]),
              reads=[self.ident, self.esel], writes=[self.esel])
        TB = cfg.TB
        self.ropet = sb("ropet", [128, 4, 64])
        self.x = sb("x", [128, 8, TB])
        self.xT = sb("xT", [128, 8, TB], BF16)
        self.qn = sb("qn", [128, L, 320])
        for l in range(L):
            kb.dma('sp', self.qn[:, l, 0:192], I["a_q_norm"][l:l + 1, :].partition_broadcast(128), writes=[self.qn])
            kb.dma('sp', self.qn[:, l, 192:320], I["a_kv_norm"][l:l + 1, :].partition_broadcast(128), writes=[self.qn])
        self.rb = sb("rb", [128, 128])
        kb.dma('sp', self.rb[:], I["rel_bias"].partition_broadcast(128), writes=[self.rb])
        self.pv_ln = sb("pv_ln", [128, L, 4, 8])
        self.pv_lnin = sb("pv_lnin", [128, 2, 8])
        self.pv_c = sb("pv_c", [128, L, 2, 35])
        self.pv_f = sb("pv_f", [128, L, 44, 4])
        self.pcar = {k: sb("pcar_" + k, [128, L, 2, 15]) for k in 'ps'}
        self.ccar = {k: sb("ccar_" + k, [128, L, 2, 30]) for k in 'ps'}
        self.fcar = {k: sb("fcar_" + k, [128, L, 44, 2]) for k in 'ps'}
        for t in (self.pcar['p'], self.ccar['p'], self.fcar['p']):
            kb.op('dve', lambda e, t=t: e.memset(t[:], 0.0), writes=[t])
        self.gt = sb("gt", [128, 4, 258])
        self.NW = 4
        self.wring = [sb("wring%d" % i, [128, 4096], BF16) for i in range(self.NW)]
        self._wsm_ctr = 0
        self._wr_ctr = 0
        self._wbr_ctr = 0
        self.w_released = 0
        self.w_ringreq = 0
        self.wsm = [sb("wsm%d" % i, [128, 1536], BF16) for i in range(2)]
        for t in self.wsm:
            kb.op('dve', lambda e, t=t: e.memset(t[:], 0.0), writes=[t])
        with ExitStack() as pes:
            stg = kb.sb("pstage", [32, 2 * DFF], F32, es=pes)

            def colvec(dst_fn, src_ap, R, C, dst_t):
                kb.dma('sp', stg[:R, 0:C * 128], src_ap, writes=[stg])
                c0 = 0
                while c0 < C:
                    nch = min(512 // max(R, 1), C - c0, 16)
                    nch = max(1, min(nch, 512 // R))
                    ps = kb.ps()
                    for i in range(nch):
                        self.tr(ps, ps[:, i * R:(i + 1) * R], stg, stg[:R, (c0 + i) * 128:(c0 + i + 1) * 128], R, bf=False)
                    for i in range(nch):
                        kb.op('act', lambda e, i=i: e.copy(out=dst_fn(c0 + i), in_=ps[:, i * R:(i + 1) * R]),
                              reads=[ps], writes=[dst_t])
                    c0 += nch

            colvec(lambda c: self.pv_lnin[:, 0, c:c + 1], I["ln_in_g"], 1, 8, self.pv_lnin)
            colvec(lambda c: self.pv_lnin[:, 1, c:c + 1], I["ln_in_b"], 1, 8, self.pv_lnin)
            for l in range(L):
                for i, nm in enumerate(("ln1_g", "ln1_b", "ln2_g", "ln2_b")):
                    colvec(lambda c, i=i: self.pv_ln[:, l, i, c:c + 1], I[nm][l:l + 1, :], 1, 8, self.pv_ln)
                colvec(lambda c: self.pv_c[:, l, c, 0:31], I["conv_w"][l], 31, 2, self.pv_c)
                for i, nm in enumerate(("conv_b", "conv_ln_g", "conv_ln_b", "pool_scale")):
                    colvec(lambda c, i=i: self.pv_c[:, l, c, 31 + i:32 + i], I[nm][l:l + 1, :], 1, 2, self.pv_c)
                colvec(lambda c: self.pv_f[:, l, c, 0:3], I["ffn_conv_w"][l], 3, 44, self.pv_f)
                colvec(lambda c: self.pv_f[:, l, c, 3:4], I["ffn_conv_b"][l:l + 1, :], 1, 44, self.pv_f)
                colvec(lambda c: self.pcar['s'][:, l, c, :], I["st_pool"][l], 15, 2, self.pcar['s'])
                colvec(lambda c: self.ccar['s'][:, l, c, :], I["st_conv"][l], 30, 2, self.ccar['s'])
                colvec(lambda c: self.fcar['s'][:, l, c, :], I["st_ffn"][l], 2, 44, self.fcar['s'])
            bidx = kb.sb("bidx_sb", [128, 258], F32, es=pes)
            tmpg = kb.sb("tmpg", [128, 258], F32, es=pes)
            kb.dma('sp', bidx[:], I["bidx"], writes=[bidx])
            for h in range(4):
                kb.op('dve', lambda e, h=h: e.tensor_scalar(out=self.gt[:, h, :], in0=bidx[:], scalar1=0.0,
                                                            scalar2=self.rb[:, 15 * 4 + h:15 * 4 + h + 1],
                                                            op0=ALU.mult, op1=ALU.subtract),
                      reads=[bidx, self.rb], writes=[self.gt])
                for b in range(32):
                    kb.op('dve', lambda e, h=h, b=b: e.tensor_scalar(out=tmpg[:], in0=bidx[:], scalar1=float(b),
                                                                    scalar2=self.rb[:, b * 4 + h:b * 4 + h + 1],
                                                                    op0=ALU.is_equal, op1=ALU.mult),
                          reads=[bidx, self.rb], writes=[tmpg])
                    kb.op('dve', lambda e, h=h: e.tensor_tensor(out=self.gt[:, h, :], in0=self.gt[:, h, :], in1=tmpg[:],
                                                                op=ALU.add),
                          reads=[tmpg, self.gt], writes=[self.gt])
            kb.barrier()
    def plan_weights(self):
        cfg, I = self.cfg, self.I
        plan = []
        nblocks = cfg.NB + 1
        for b in range(nblocks):
            for l in range(cfg.L):
                w_in = I["w_in"][l].rearrange("(k p) n -> p k n", p=128)
                plan.append(("wsm", l, None))
                plan.append(("wtok_a0", l, [(w_in[:, :, 0:352], (8, 352), 0)]))
                plan.append(("wtok_a1", l, [(w_in[:, :, 352:864], (8, 512), 0)]))
                plan.append(("wtok_b", l, [(w_in[:, :, 864:1284], (8, 420), 0)]))
                plan.append(("wpc0", l, [(w_in[:, :, 1284:1668], (8, 384), 0)]))
                plan.append(("wpc1", l, [(w_in[:, :, 1668:2052], (8, 384), 0)]))
                wbr = I["w_branch"][l].rearrange("n (k p) d -> p (n k) d", p=128)
                for hf in range(2):
                    for n in range(4):
                        c0 = O_G + n * 1024 + hf * 512
                        plan.append(("wbr%d%d" % (n, hf), l, [(wbr[:, 2 * n:2 * n + 2, hf * 512:(hf + 1) * 512], (2, 512), 0)]))
                        plan.append(("wg%d%d" % (n, hf), l, [(w_in[:, :, c0:c0 + 512], (8, 512), 0)]))
                w_o = I["w_out"][l].rearrange("(k p) n -> p k n", p=128)
                for hf in range(2):
                    plan.append(("wout%d" % hf, l, [(w_o[:, :, hf * 512:(hf + 1) * 512], (8, 512), 0)]))
                w_up = I["w_up"][l].rearrange("(k p) n -> p k n", p=128)
                for g in range(6):
                    nch = 4 if g < 5 else 2
                    plan.append(("wupv%d" % g, l, [(w_up[:, :, g * 512:g * 512 + nch * 128], (8, nch * 128), 0)]))
                    plan.append(("wupg%d" % g, l, [(w_up[:, :, DFF + g * 512:DFF + g * 512 + nch * 128], (8, nch * 128), 0)]))
                w_dn = I["w_down"][l].rearrange("(k p) n -> p k n", p=128)
                for g in range(8):
                    plan.append(("wdn%d" % g, l, [(w_dn[:, :, g * 128:(g + 1) * 128], (NFC, 128), 0)]))
        self.wplan = plan
        self.wslot = [None] * len(plan)

    def issue_next_weight(self):
        i = self.wissue
        if i >= len(self.wplan):
            return False
        kb, I = self.kb, self.I
        tag, l, parts = self.wplan[i]
        if tag == "wsm":
            t = self.wsm[self._wsm_ctr % 2]
            self._wsm_ctr += 1
            kb.dma('pool', t[:, 0:384], I["a_w_qup"][l, 0:128, :], writes=[t])
            kb.dma('pool', t[0:64, 384:768], I["a_w_qup"][l, 128:192, :], writes=[t])
            kb.dma('pool', t[:, 768:1280], I["a_w_kvup"][l], writes=[t])
            for g in range(4):
                cc, hh = g // 2, g % 2
                kb.dma('pool', t[hh * 64:hh * 64 + 64, 1280 + cc * 128 + hh * 64:1280 + cc * 128 + hh * 64 + 64],
                       I["pool_w"][l, g], writes=[t])
            self.wslot[i] = t
        else:
            if self._wr_ctr - self.NW >= self.w_released:
                return False
            t = self.wring[self._wr_ctr % self.NW]
            self._wr_ctr += 1
            for (src, (a, b), off) in parts:
                dst = t[:, off:off + a * b].rearrange("p (a b) -> p a b", a=a)
                kb.dma('pool', dst, src, writes=[t])
            self.wslot[i] = t
        self.wissue += 1
        return True

    def getw(self, tag, l, hold=0):
        ptag, pl, _ = self.wplan[self.wpos]
        assert ptag == tag and pl == l, (ptag, pl, tag, l)
        is_ring = tag != "wsm"
        if is_ring:
            self.w_ringreq += 1
        self.w_released = max(self.w_released, self.w_ringreq - hold - (1 if is_ring else 0))
        while self.wissue < len(self.wplan) and self.wissue <= self.wpos + 6:
            if self.issue_next_weight() is False:
                break
        assert self.wissue > self.wpos, (tag, l)
        t = self.wslot[self.wpos]
        self.wpos += 1
        return t

    def lnfm(self, es, s, C, T, ones, gfn, bfn, gb_t, outs, func=AF.Identity):
        kb = self.kb
        sqs = [kb.sb("ln_sq%d" % i, [128, T], BF16, es=es) for i in range(2)]
        s16s = [kb.sb("ln_s16%d" % i, [128, T], BF16, es=es) for i in range(2)]
        ones16 = self.ones16[ones]
        pm = kb.ps()
        pq = kb.ps()
        for c in range(C):
            sq, s16 = sqs[c % 2], s16s[c % 2]
            kb.op('dve', lambda e: e.tensor_copy(out=s16[:, :], in_=s[:, c, 0:T]), reads=[s], writes=[s16])
            kb.op('act', lambda e: e.activation(out=sq[:, :], in_=s[:, c, 0:T], func=AF.Square), reads=[s], writes=[sq])
            self.mm(pm, pm[:, 0:T], ones16, ones16[:, :], s16, s16[:, :], c == 0, c == C - 1)
            self.mm(pq, pq[:, 0:T], ones16, ones16[:, :], sq, sq[:, :], c == 0, c == C - 1)
        mean = kb.sb("ln_mean", [128, T], F32, es=es)
        rstd = kb.sb("ln_rstd", [128, T], F32, es=es)
        kb.op('act', lambda e: e.copy(out=mean[:], in_=pm[:, 0:T]), reads=[pm], writes=[mean])
        kb.op('dve', lambda e: e.tensor_tensor(out=rstd[:], in0=mean[:], in1=mean[:], op=ALU.mult), reads=[mean], writes=[rstd])
        kb.op('dve', lambda e: e.tensor_tensor(out=rstd[:], in0=pq[:, 0:T], in1=rstd[:], op=ALU.subtract), reads=[pq, rstd], writes=[rstd])
        kb.op('dve', lambda e: e.tensor_scalar(out=rstd[:], in0=rstd[:], scalar1=0.0, scalar2=LN_EPS, op0=ALU.max, op1=ALU.add),
              reads=[rstd], writes=[rstd])
        kb.op('act', lambda e: e.activation(out=rstd[:], in_=rstd[:], func=AF.Sqrt), reads=[rstd], writes=[rstd])
        kb.op('dve', lambda e: e.reciprocal(out=rstd[:], in_=rstd[:]), reads=[rstd], writes=[rstd])
        for c in range(C):
            sc = type(s)(s.t)
            kb.op('dve', lambda e: e.tensor_tensor(out=s[:, c, 0:T], in0=s[:, c, 0:T], in1=mean[:], op=ALU.subtract),
                  reads=[s, mean], writes=[sc])
            kb.op('dve', lambda e: e.tensor_tensor(out=s[:, c, 0:T], in0=s[:, c, 0:T], in1=rstd[:], op=ALU.mult),
                  reads=[sc, rstd], writes=[sc])
            for (ot, ofn) in outs:
                kb.op('act', lambda e: e.activation(out=ofn(c), in_=s[:, c, 0:T], func=func, bias=bfn(c), scale=gfn(c)),
                      reads=[sc, gb_t], writes=[ot])

    def block(self, kind, j):
        cfg, kb, I, O = self.cfg, self.kb, self.I, self.O
        L = cfg.L
        if kind == 'p':
            Tn = cfg.TB
            P = j * cfg.TB
            xin = I["xp"][P:P + Tn, :]
            rope = I["rope_p"][P:P + Tn, :]
            pre = "p"
            row0 = P
            ktop = cfg.KP
        else:
            Tn = cfg.TS
            P = cfg.PAST
            xin = I["xs"]
            rope = I["rope_s"]
            pre = "s"
            row0 = 0
            ktop = cfg.KS
        self.kind, self.Tn, self.P, self.pre, self.row0, self.ktop, self.bj = kind, Tn, P, pre, row0, ktop, j
        qts = [(q0, min(128, Tn - q0)) for q0 in range(0, Tn, 128)]
        self.qts = qts
        if kind == 'p':
            self.ktiles = [('o', t * 128, 128, t * 128) for t in range((P + Tn) // 128)]
        else:
            self.ktiles = [('c', t * 128, 128, t * 128) for t in range(P // 128)] + [('o', 0, Tn, P)]
        self.last_block = (kind == 's') or (j == cfg.NB - 1)
        x, xT = self.x, self.xT
        with ExitStack() as pes:
            for qi, (q0, r) in enumerate(qts):
                kb.dma('sp', self.ropet[:r, qi, :], rope[q0:q0 + r, :], writes=[self.ropet])
            xraw = kb.sb("xraw", [128, 8, Tn], F32, es=pes)
            xtoks = [kb.sb("xtok%d" % i, [128, D], F32, es=pes) for i in range(2)]
            for qi, (q0, r) in enumerate(qts):
                xtok = xtoks[qi % 2]
                kb.dma('sp', xtok[:r, :], xin[q0:q0 + r, :], writes=[xtok])
                for half in range(2):
                    ps = kb.ps()
                    for c in range(4):
                        cc = half * 4 + c
                        self.tr(ps, ps[:, c * 128:c * 128 + r], xtok, xtok[:r, cc * 128:(cc + 1) * 128], r, bf=False)
                    kb.op('act', lambda e, half=half, ps=ps, q0=q0, r=r: e.copy(
                        out=xraw[:, half * 4:half * 4 + 4, q0:q0 + r],
                        in_=ps[:, :].rearrange("p (c t) -> p c t", c=4)[:, :, 0:r]), reads=[ps], writes=[xraw])
            self.lnfm(pes, xraw, 8, Tn, self.ones1024,
                      lambda c: self.pv_lnin[:, 0, c:c + 1], lambda c: self.pv_lnin[:, 1, c:c + 1], self.pv_lnin,
                      [(x, lambda c: x[:, c, 0:Tn]), (xT, lambda c: xT[:, c, 0:Tn])])
            kb.barrier()
        for l in range(L):
            self.layer(l)

    def layer(self, l):
        cfg, kb, I, O = self.cfg, self.kb, self.I, self.O
        kind, Tn, P, pre, row0, qts = self.kind, self.Tn, self.P, self.pre, self.row0, self.qts
        x, xT = self.x, self.xT
        NQ = len(qts)
        wsm = self.getw("wsm", l)
        wqup = lambda kc, ksz: wsm[0:ksz, kc * 384:(kc + 1) * 384]
        with ExitStack() as mes:
            brT = kb.sb("brT", [128, 8, Tn], BF16, es=mes)
            qaT = kb.sb("qaT", [96, 4, Tn], BF16, es=mes)
            bqT = kb.sb("bqT", [128, 2, Tn], BF16, es=mes)
            qiT = kb.sb("qiT", [32, 4, Tn], BF16, es=mes)
            wqs = kb.sb("wqs", [128, NQ, 4], F32, es=mes)
            wn = kb.sb("wn", [128, 4, 96], BF16, es=mes)
            wv = kb.sb("wv", [128, 256], BF16, es=mes)
            kvv = wsm[:, 768:1280].rearrange("p (h c) -> p h c", h=4)
            kb.op('dve', lambda e: e.memset(wn[:], 0.0), writes=[wn])
            kb.op('dve', lambda e: e.tensor_copy(out=wn[:, :, 0:64], in_=kvv[:, :, 0:64]), reads=[wsm, wn], writes=[wn])
            kb.op('dve', lambda e: e.tensor_copy(out=wv[:, :].rearrange("p (h c) -> p h c", h=4), in_=kvv[:, :, 64:128]),
                  reads=[wsm], writes=[wv])
            self.wn, self.wv = wn, wv
            odeps = {nm: T(None) for nm in ("ckv", "krope", "bk", "bv", "kidx")}
            wa0 = self.getw("wtok_a0", l)
            wa1 = self.getw("wtok_a1", l, hold=1)
            wb = self.getw("wtok_b", l, hold=2)
            wa03 = wa0[:, 0:8 * 352].rearrange("p (k n) -> p k n", k=8)
            wa13 = wa1[:, 0:8 * 512].rearrange("p (k n) -> p k n", k=8)
            wb3 = wb[:, 0:8 * 420].rearrange("p (k n) -> p k n", k=8)
            with ExitStack() as pes:
                NQ_ = len(qts)
                rp = self.ropet
                Bf = []
                for i in range(NQ_):
                    sfx = str(i)
                    Bf.append(dict(
                        toka=kb.sb("toka" + sfx, [128, 352], F32, es=pes), tokb=kb.sb("tokb" + sfx, [128, 512], F32, es=pes),
                        tokc=kb.sb("tokc" + sfx, [128, 420], F32, es=pes), rows=kb.sb("rowsA" + sfx, [128, 160], F32, es=pes),
                        ss=kb.sb("ss" + sfx, [128, 4], F32, es=pes), junk=kb.sb("junk" + sfx, [128, 192], F32, es=pes),
                        cqn=kb.sb("cqn" + sfx, [128, 192], BF16, es=pes), bq16=kb.sb("bq16" + sfx, [128, 384], BF16, es=pes),
                        cqT=kb.sb("cqT" + sfx, [128, 2, 128], BF16, es=pes), qa16=kb.sb("qa16" + sfx, [128, 4, 96], BF16, es=pes),
                        rt1=kb.sb("rt1" + sfx, [128, 4, 32], F32, es=pes), rt2=kb.sb("rt2" + sfx, [128, 4, 32], F32, es=pes)))
                for qi, (q0, r) in enumerate(qts):
                    b_ = Bf[qi]
                    tpa, tpb, tpc = kb.ps(), kb.ps(), kb.ps()
                    for (ps, wt, w3, c0, c1) in ((tpa, wa0, wa03, 0, 352), (tpb, wa1, wa13, 0, 512), (tpc, wb, wb3, 0, 420)):
                        for k in range(8):
                            self.mm(ps, ps[:r, 0:c1 - c0], xT, xT[:, k, q0:q0 + r], wt, w3[:, k, c0:c1], k == 0, k == 7, inc=(k == 7))
                    kb.op('act', lambda e: e.copy(out=b_['toka'][:r, :], in_=tpa[:r, 0:352]), reads=[tpa], writes=[b_['toka']])
                    kb.op('dve', lambda e: e.tensor_copy(out=b_['tokb'][:r, :], in_=tpb[:r, :]), reads=[tpb], writes=[b_['tokb']])
                    kb.op('act', lambda e: e.copy(out=b_['tokc'][:r, :], in_=tpc[:r, 0:420]), reads=[tpc], writes=[b_['tokc']])
                pc = self.phase_poolconv(l, brT)
                next(pc, None)
                for qi, (q0, r) in enumerate(qts):
                    b_ = Bf[qi]
                    ss, junk, toka = b_['ss'], b_['junk'], b_['toka']
                    kb.op('dve', lambda e: e.memset(ss[:], 0.0), writes=[ss])
                    kb.op('act', lambda e: e.activation(out=junk[:r, 0:192], in_=toka[:r, 0:192], func=AF.Square,
                                                        scale=192 ** -0.5, accum_out=ss[:r, 0:1]), reads=[toka, ss], writes=[junk, ss])
                    kb.op('act', lambda e: e.activation(out=junk[:r, 0:128], in_=toka[:r, 192:320], func=AF.Square,
                                                        scale=128 ** -0.5, accum_out=ss[:r, 1:2]), reads=[toka, ss], writes=[junk, ss])
                for qi, (q0, r) in enumerate(qts):
                    ss = Bf[qi]['ss']
                    kb.op('dve', lambda e: e.tensor_scalar(out=ss[:r, 2:4], in0=ss[:r, 0:2], scalar1=LN_EPS, scalar2=1.0, op0=ALU.add, op1=ALU.mult),
                          reads=[ss], writes=[ss])
                for qi, (q0, r) in enumerate(qts):
                    ss = Bf[qi]['ss']
                    kb.op('act', lambda e: e.activation(out=ss[:r, 2:4], in_=ss[:r, 2:4], func=AF.Sqrt), reads=[ss], writes=[ss])
                for qi, (q0, r) in enumerate(qts):
                    ss = Bf[qi]['ss']
                    kb.op('dve', lambda e: e.reciprocal(out=ss[:r, 2:4], in_=ss[:r, 2:4]), reads=[ss], writes=[ss])
                next(pc, None)
                for qi, (q0, r) in enumerate(qts):
                    b_ = Bf[qi]
                    ss, toka, tokb, tokc, rows, cqn, rt1, rt2, bq16 = (b_[k] for k in ('ss', 'toka', 'tokb', 'tokc', 'rows', 'cqn', 'rt1', 'rt2', 'bq16'))
                    kb.op('dve', lambda e: e.scalar_tensor_tensor(out=cqn[:r, :], in0=toka[:r, 0:192], scalar=ss[:r, 2:3],
                                                                  in1=self.qn[:r, l, 0:192], op0=ALU.mult, op1=ALU.mult),
                          reads=[toka, ss, self.qn], writes=[cqn])
                    kb.op('dve', lambda e: e.scalar_tensor_tensor(out=rows[:r, 0:128], in0=toka[:r, 192:320], scalar=ss[:r, 3:4],
                                                                  in1=self.qn[:r, l, 192:320], op0=ALU.mult, op1=ALU.mult),
                          reads=[toka, ss, self.qn], writes=[rows])
                    kb.op('dve', lambda e: e.tensor_tensor(out=rt1[:r, 0, :], in0=toka[:r, 320:352], in1=rp[:r, qi, 0:32], op=ALU.mult),
                          reads=[toka, rp], writes=[rt1])
                    kb.op('dve', lambda e: e.tensor_tensor(out=rt2[:r, 0, 0:16], in0=toka[:r, 336:352], in1=rp[:r, qi, 32:48], op=ALU.mult),
                          reads=[toka, rp], writes=[rt2])
                    kb.op('dve', lambda e: e.tensor_tensor(out=rt2[:r, 0, 16:32], in0=toka[:r, 320:336], in1=rp[:r, qi, 48:64], op=ALU.mult),
                          reads=[toka, rp, rt2], writes=[rt2])
                    kb.op('dve', lambda e: e.tensor_tensor(out=rows[:r, 128:160], in0=rt1[:r, 0, :], in1=rt2[:r, 0, :], op=ALU.add),
                          reads=[rt1, rt2, rows], writes=[rows])
                    rr = slice(row0 + q0, row0 + q0 + r)
                    kb.dma('sp', O[pre + "_ckv"][l, rr, :], rows[:r, 0:128], reads=[rows], writes=[odeps["ckv"]])
                    kb.dma('sp', O[pre + "_krope"][l, rr, :], rows[:r, 128:160], reads=[rows], writes=[odeps["krope"]])
                    kb.dma('sp', O[pre + "_bk"][l, rr, :], tokb[:r, 256:512], reads=[tokb], writes=[odeps["bk"]])
                    kb.dma('sp', O[pre + "_bv"][l, rr, :], tokc[:r, 0:256], reads=[tokc], writes=[odeps["bv"]])
                    kb.dma('sp', O[pre + "_kidx"][l, rr, :], tokc[:r, 384:416], reads=[tokc], writes=[odeps["kidx"]])
                    kb.op('dve', lambda e: e.tensor_copy(out=bq16[:r, 0:256], in_=tokb[:r, 0:256]), reads=[tokb], writes=[bq16])
                    kb.op('dve', lambda e: e.tensor_copy(out=bq16[:r, 256:384], in_=tokc[:r, 256:384]), reads=[tokc, bq16], writes=[bq16])
                    kb.op('dve', lambda e: e.tensor_scalar(out=wqs[:r, qi, :], in0=tokc[:r, 416:420], scalar1=IDX_SCALE, scalar2=0.0,
                                                           op0=ALU.mult, op1=ALU.add), reads=[tokc], writes=[wqs])
                next(pc, None)
                pts = []
                for qi, (q0, r) in enumerate(qts):
                    b_ = Bf[qi]
                    bq16, cqn = b_['bq16'], b_['cqn']
                    pt = kb.ps()
                    ptb = pt[:, :].bitcast(BF16)
                    for pr in range(2):
                        self.tr(pt, ptb[:, pr * 128:pr * 128 + r], bq16, bq16[:r, pr * 128:(pr + 1) * 128], r)
                    for h in range(4):
                        self.tr(pt, ptb[0:32, 256 + h * 128:256 + h * 128 + r], bq16, bq16[:r, 256 + h * 32:256 + (h + 1) * 32], r)
                    self.tr(pt, ptb[:, 768:768 + r], cqn, cqn[:r, 0:128], r)
                    self.tr(pt, ptb[0:64, 896:896 + r], cqn, cqn[:r, 128:192], r)
                    pts.append((pt, ptb))
                for qi, (q0, r) in enumerate(qts):
                    pt, ptb = pts[qi]
                    cqT = Bf[qi]['cqT']
                    kb.op('act', lambda e: e.copy(out=bqT[:, :, q0:q0 + r],
                                                  in_=ptb[:, 0:256].rearrange("p (a b) -> p a b", a=2)[:, :, 0:r]),
                          reads=[pt], writes=[bqT])
                    kb.op('dve', lambda e: e.tensor_copy(out=qiT[:, :, q0:q0 + r],
                                                         in_=ptb[0:32, 256:768].rearrange("p (a b) -> p a b", a=4)[:, :, 0:r]),
                          reads=[pt], writes=[qiT])
                    kb.op('act', lambda e: e.copy(out=cqT[:, 0, 0:r], in_=ptb[:, 768:768 + r]), reads=[pt], writes=[cqT])
                    kb.op('dve', lambda e: e.tensor_copy(out=cqT[0:64, 1, 0:r], in_=ptb[0:64, 896:896 + r]), reads=[pt, cqT], writes=[cqT])
                next(pc, None)
                pqs = []
                for qi, (q0, r) in enumerate(qts):
                    cqT = Bf[qi]['cqT']
                    pq = kb.ps()
                    self.mm(pq, pq[:r, 0:384], cqT, cqT[:, 0, 0:r], wsm, wqup(0, 128), True, False)
                    self.mm(pq, pq[:r, 0:384], cqT, cqT[0:64, 1, 0:r], wsm, wqup(1, 64), False, True)
                    pqs.append(pq)
                for qi, (q0, r) in enumerate(qts):
                    b_ = Bf[qi]
                    qa16, rt1, rt2 = b_['qa16'], b_['rt1'], b_['rt2']
                    pq = pqs[qi]
                    pq3 = pq[:, 0:384].rearrange("p (h c) -> p h c", h=4)
                    kb.op('act', lambda e: e.copy(out=qa16[:r, :, 0:64], in_=pq3[:r, :, 0:64]), reads=[pq], writes=[qa16])
                    for h in range(4):
                        kb.op('dve', lambda e: e.tensor_tensor(out=rt1[:r, h, :], in0=pq3[:r, h, 64:96], in1=rp[:r, qi, 0:32], op=ALU.mult),
                              reads=[pq, rp, rt1], writes=[rt1])
                        kb.op('dve', lambda e: e.tensor_tensor(out=rt2[:r, h, 0:16], in0=pq3[:r, h, 80:96], in1=rp[:r, qi, 32:48], op=ALU.mult),
                              reads=[pq, rp, rt2], writes=[rt2])
                        kb.op('dve', lambda e: e.tensor_tensor(out=rt2[:r, h, 16:32], in0=pq3[:r, h, 64:80], in1=rp[:r, qi, 48:64], op=ALU.mult),
                              reads=[pq, rp, rt2], writes=[rt2])
                    kb.op('dve', lambda e: e.tensor_tensor(out=qa16[:r, :, 64:96], in0=rt1[:r, :, :], in1=rt2[:r, :, :], op=ALU.add),
                          reads=[rt1, rt2, qa16], writes=[qa16])
                next(pc, None)
                pts = []
                for qi, (q0, r) in enumerate(qts):
                    qa16 = Bf[qi]['qa16']
                    pt3 = kb.ps()
                    pt3b = pt3[:, :].bitcast(BF16)
                    for h in range(4):
                        self.tr(pt3, pt3b[0:96, h * 128:h * 128 + r], qa16, qa16[:r, h, :], r)
                    pts.append((pt3, pt3b))
                for qi, (q0, r) in enumerate(qts):
                    pt3, pt3b = pts[qi]
                    kb.op('act', lambda e: e.copy(out=qaT[:, :, q0:q0 + r],
                                                  in_=pt3b[0:96, 0:512].rearrange("p (a b) -> p a b", a=4)[:, :, 0:r]),
                          reads=[pt3], writes=[qaT])
                for _ in pc:
                    pass
                kb.barrier()
            self.odeps = odeps
            kb.ps_groups.update({'acc': [0, 1, 2, 3], 'qk': [4, 5], 'misc': [6, 7]})
            with ExitStack() as des:
                st = self.dsa_open(l, des, qiT, wqs)
                def chain():
                    for t0 in range(0, len(qts), 2):
                        yield from self.dsa_pass1(l, st, t0)
                self.phase_mla(l, brT, qaT, bg=chain())
                self.dsa_rest(l, des, st, brT, bqT, True)
                kb.barrier()
            self.phase_merge(l, brT)
        self.phase_ffn(l)

    def phase_poolconv(self, l, brT):
        cfg, kb, I, O = self.cfg, self.kb, self.I, self.O
        kind, Tn, pre = self.kind, self.Tn, self.pre
        xT = self.xT
        with ExitStack() as pes:
            wpcs = [self.getw("wpc0", l), self.getw("wpc1", l, hold=1)]
            wpc3 = [w[:, 0:8 * 384].rearrange("p (k n) -> p k n", k=8) for w in wpcs]
            wsm = self.cur_wsm(l)
            pext = kb.sb("pext", [128, 2, 15 + Tn], F32, es=pes)
            cext = kb.sb("cext", [128, 2, 30 + Tn], F32, es=pes)
            sg = kb.sb("sgl", [128, 2, Tn], F32, es=pes)
            pcar, ccar = self.pcar[kind], self.ccar[kind]
            kb.op('dve', lambda e: e.tensor_copy(out=pext[:, :, 0:15], in_=pcar[:, l, :, :]), reads=[pcar], writes=[pext])
            kb.op('dve', lambda e: e.tensor_copy(out=cext[:, :, 0:30], in_=ccar[:, l, :, :]), reads=[ccar], writes=[cext])
            pss = []
            for c in range(6):
                ps = kb.ps()
                for k in range(8):
                    self.mm(ps, ps[:, 0:Tn], wpcs[c // 3], wpc3[c // 3][:, k, (c % 3) * 128:(c % 3 + 1) * 128], xT, xT[:, k, 0:Tn], k == 0, k == 7, inc=(k == 7))
                pss.append(ps)
                if c < 2:
                    kb.op('act', lambda e, c=c, ps=ps: e.copy(out=pext[:, c, 15:15 + Tn], in_=ps[:, 0:Tn]), reads=[ps, pext], writes=[pext])
                elif c >= 4:
                    kb.op('act', lambda e, c=c, ps=ps: e.activation(out=sg[:, c - 4, :], in_=ps[:, 0:Tn], func=AF.Sigmoid),
                          reads=[ps, sg], writes=[sg])
                    kb.op('dve', lambda e, c=c: e.tensor_tensor(out=cext[:, c - 4, 30:30 + Tn], in0=pss[c - 2][:, 0:Tn], in1=sg[:, c - 4, :],
                                                                op=ALU.mult), reads=[pss[c - 2], sg, cext], writes=[cext])
            yield
            kb.op('dve', lambda e: e.tensor_copy(out=pcar[:, l, :, :], in_=pext[:, :, Tn:Tn + 15]), reads=[pext, pcar], writes=[pcar])
            kb.op('dve', lambda e: e.tensor_copy(out=ccar[:, l, :, :], in_=cext[:, :, Tn:Tn + 30]), reads=[cext, ccar], writes=[ccar])
            if self.last_block:
                st = kb.sb("st_out", [32, 512], F32, es=pes)
                ps = kb.ps()
                for c in range(2):
                    self.tr(ps, ps[0:15, c * 128:(c + 1) * 128], pcar, pcar[:, l, c, :], 128, bf=False)
                    self.tr(ps, ps[0:30, 256 + c * 128:256 + (c + 1) * 128], ccar, ccar[:, l, c, :], 128, bf=False)
                kb.op('act', lambda e: e.copy(out=st[0:30, :], in_=ps[0:30, :]), reads=[ps], writes=[st])
                kb.dma('sp', O[pre + "_pool"][l], st[0:15, 0:256], reads=[st])
                kb.dma('sp', O[pre + "_conv"][l], st[0:30, 256:512], reads=[st])
            yield
            bA = kb.sb("paA", [128, 2, 15 + Tn], F32, es=pes)
            bB = kb.sb("paB", [128, 2, 15 + Tn], F32, es=pes)
            n = 15 + Tn
            pl = kb.sb("pl", [128, 2, Tn], F32, es=pes)
            pl16 = kb.sb("pl16", [128, 2, Tn], BF16, es=pes)
            first = (kind == 'p' and self.bj == 0)
            if first:
                self.icnt0 = kb.sb("icnt0", [128, 2, cfg.TB], F32, es=pes)
                kb.dma('sp', self.icnt0[:], I["icnt0"].rearrange("p (c t) -> p c t", c=2), writes=[self.icnt0])

            def pooled(g, src, wdt):
                c, lo = g // 2, (g % 2) * 64
                if first:
                    kb.op('dve', lambda e: e.tensor_tensor(out=pl[lo:lo + 64, c, :], in0=src[lo:lo + 64, c, 15:15 + Tn],
                                                           in1=self.icnt0[lo:lo + 64, c, 0:Tn], op=ALU.mult),
                          reads=[src, self.icnt0, pl], writes=[pl])
                    kb.op('dve', lambda e: e.tensor_tensor(out=pl[lo:lo + 64, c, :], in0=pl[lo:lo + 64, c, :],
                                                           in1=pext[lo:lo + 64, c, 15:15 + Tn], op=ALU.subtract),
                          reads=[pl, pext], writes=[pl])
                else:
                    kb.op('dve', lambda e: e.scalar_tensor_tensor(
                        out=pl[lo:lo + 64, c, :], in0=src[lo:lo + 64, c, 15:15 + Tn], scalar=1.0 / wdt,
                        in1=pext[lo:lo + 64, c, 15:15 + Tn], op0=ALU.mult, op1=ALU.subtract), reads=[src, pext, pl], writes=[pl])

            kb.op('dve', lambda e: e.tensor_tensor(out=bA[:, :, 1:n], in0=pext[:, :, 1:n], in1=pext[:, :, 0:n - 1], op=ALU.add),
                  reads=[pext], writes=[bA])
            kb.op('dve', lambda e: e.tensor_tensor(out=bB[:, :, 3:n], in0=bA[:, :, 3:n], in1=bA[:, :, 1:n - 2], op=ALU.add),
                  reads=[bA], writes=[bB])
            pooled(0, bA, 2)
            pooled(1, bB, 4)
            yield
            kb.op('dve', lambda e: e.tensor_tensor(out=bA[:, :, 7:n], in0=bB[:, :, 7:n], in1=bB[:, :, 3:n - 4], op=ALU.add),
                  reads=[bB, bA], writes=[bA])
            kb.op('dve', lambda e: e.tensor_tensor(out=bB[:, :, 15:n], in0=bA[:, :, 15:n], in1=bA[:, :, 7:n - 8], op=ALU.add),
                  reads=[bA, bB], writes=[bB])
            pooled(2, bA, 8)
            pooled(3, bB, 16)
            yield
            kb.op('dve', lambda e: e.tensor_copy(out=pl16[:], in_=pl[:]), reads=[pl], writes=[pl16])
            pvc = self.pv_c
            for c in range(2):
                ps = kb.ps()
                self.mm(ps, ps[:, 0:Tn], wsm, wsm[:, 1280 + c * 128:1280 + (c + 1) * 128], pl16, pl16[:, c, :], True, True)
                kb.op('act', lambda e, c=c, ps=ps: e.activation(out=brT[:, 4 + c, 0:Tn], in_=ps[:, 0:Tn], func=AF.Copy,
                                                                scale=pvc[:, l, c, 34:35]), reads=[ps, pvc, brT], writes=[brT])
            yield
            cacc = kb.sb("cacc", [128, 2, Tn], F32, es=pes)
            cext16 = kb.sb("cext16", [128, 2, 30 + Tn], BF16, es=pes)
            kb.op('act', lambda e: e.copy(out=cext16[:], in_=cext[:]), reads=[cext], writes=[cext16])
            for c in range(2):
                dg = kb.sb("dg%d" % c, [128, 31, 128], BF16, es=pes)
                for k in range(31):
                    kb.op('dve', lambda e: e.tensor_scalar(out=dg[:, k, :], in0=self.ident16[:, :], scalar1=pvc[:, l, c, k:k + 1],
                                                            scalar2=0.0, op0=ALU.mult, op1=ALU.add),
                          reads=[self.ident16, pvc, dg], writes=[dg])
                yield
                ps = kb.ps()
                for k in range(31):
                    self.mm(ps, ps[:, 0:Tn], dg, dg[:, k, :], cext16, cext16[:, c, k:k + Tn], k == 0, k == 30, inc=(k == 30))
                kb.op('act', lambda e: e.activation(out=cacc[:, c, :], in_=ps[:, 0:Tn], func=AF.Identity, bias=pvc[:, l, c, 31:32]),
                      reads=[ps, pvc, cacc], writes=[cacc])
            yield
            self.lnfm(pes, cacc, 2, Tn, self.ones256, lambda c: pvc[:, l, c, 32:33], lambda c: pvc[:, l, c, 33:34], pvc,
                      [(brT, lambda c: brT[:, 6 + c, 0:Tn])], func=AF.Silu)

    def cur_wsm(self, l):
        for i in range(self.wpos - 1, -1, -1):
            if self.wplan[i][0] == "wsm":
                return self.wslot[i]
        raise AssertionError

    def key_supers(self):
        sups, cur = [], []
        for kt in self.ktiles:
            if kt[2] < 128 or (cur and cur[0][0] != kt[0]):
                if cur:
                    sups.append(cur)
                    cur = []
            cur.append(kt)
            if len(cur) == 4 or kt[2] < 128:
                sups.append(cur)
                cur = []
        if cur:
            sups.append(cur)
        return sups

    def ksrc(self, name, l, sup):
        I, O = self.I, self.O
        kindsrc, r0, kr, _ = sup[0]
        nrows = sum(t[2] for t in sup)
        if kindsrc == 'c':
            ap = I["c_" + name][l, r0:r0 + nrows, :]
            dep = []
        else:
            ap = O[self.pre + "_" + name][l, r0:r0 + nrows, :]
            dep = [self.odeps[name]]
        if kr == 128:
            return ap.rearrange("(t p) f -> p t f", p=128), dep, 128
        return ap.rearrange("(t p) f -> p t f", t=1), dep, kr

    def vis(self, abs_pos, kr):
        if self.kind == 's':
            return 0, None
        a = (abs_pos - self.P) // 128
        if a < 0:
            return 0, None
        return a * 128, a

    def phase_mla(self, l, brT, qaT, bg=None):
        cfg, kb = self.cfg, self.kb
        Tn, qts = self.Tn, self.qts
        NQ = len(qts)
        wn, wv = self.wn, self.wv
        with ExitStack() as pes:
            accs = [kb.ps('acc') for _ in range(4)]
            first_acc = [True] * 4
            kin16 = [kb.sb("kinA16_%d" % i, [128, 4, 160], BF16, es=pes) for i in range(2)]
            kin2 = [T(k.t) for k in kin16]
            ckvT = [kb.sb("ckvT%d" % i, [128, 512], BF16, es=pes) for i in range(2)]
            krT = [kb.sb("krT%d" % i, [32, 512], BF16, es=pes) for i in range(2)]
            KA = [kb.sb("KA%d" % i, [96, 4, 512], BF16, es=pes) for i in range(2)]
            VA = [kb.sb("VA%d" % i, [128, 4, 4, 65], BF16, es=pes) for i in range(2)]
            PT = [kb.sb("PTa%d" % i, [128, 512], BF16, es=pes) for i in range(4)]
            for v in VA:
                kb.op('dve', lambda e, v=v: e.memset(v[:, :, :, 64:65], 1.0), writes=[v])
            pace = max(1, -(-64 // (4 * len(self.ktiles))))
            pti = 0
            pend = []
            for si, sup in enumerate(self.key_supers()):
                b = si % 2
                nt = len(sup)
                a1, d1, kr = self.ksrc("ckv", l, sup)
                a2, d2, _ = self.ksrc("krope", l, sup)
                nk = sum(t[2] for t in sup)
                kb.dma('pool', kin16[b][:kr, 0:nt, 0:128], a1, reads=d1, writes=[kin16[b]])
                kb.dma('pool', kin16[b][:kr, 0:nt, 128:160], a2, reads=d2, writes=[kin2[b]])
                pt = kb.ps('misc')
                ptb = pt[:, :].bitcast(BF16)
                for t in range(nt):
                    self.tr(pt, ptb[:, t * 128:t * 128 + kr], kin16[b], kin16[b][:kr, t, 0:128], kr)
                    self.tr(pt, ptb[0:32, 512 + t * 128:512 + t * 128 + kr], kin2[b], kin16[b][:kr, t, 128:160], kr)
                kb.op('act', lambda e: e.copy(out=ckvT[b][:, 0:nk], in_=ptb[:, 0:nk]), reads=[pt], writes=[ckvT[b]])
                kb.op('act', lambda e: e.copy(out=krT[b][:, 0:nk], in_=ptb[0:32, 512:512 + nk]), reads=[pt], writes=[krT[b]])
                for h in range(4):
                    pk = kb.ps('misc')
                    self.mm(pk, pk[0:96, 0:nk], wn, wn[:, h, :], ckvT[b], ckvT[b][:, 0:nk], True, False)
                    self.mm(pk, pk[0:96, 0:nk], self.esel, self.esel[:, :], krT[b], krT[b][:, 0:nk], False, True)
                    kb.op('act', lambda e, h=h, pk=pk: e.copy(out=KA[b][:, h, 0:nk], in_=pk[0:96, 0:nk]), reads=[pk, KA[b]], writes=[KA[b]])
                for t in range(nt):
                    pvp = kb.ps('misc')
                    self.mm(pvp, pvp[:kr, 0:256], ckvT[b], ckvT[b][:, t * 128:t * 128 + kr], wv, wv[:, :], True, True)
                    kb.op('act', lambda e, t=t, pvp=pvp: e.copy(out=VA[b][:kr, t, :, 0:64],
                                                                in_=pvp[:kr, 0:256].rearrange("p (h c) -> p h c", h=4)),
                          reads=[pvp, VA[b]], writes=[VA[b]])
                for t, (_, _, kr_t, apos) in enumerate(sup):
                    qlo, diag = self.vis(apos, kr_t)
                    if qlo >= Tn:
                        continue
                    nq = Tn - qlo
                    for h in range(4):
                        psq = kb.ps('qk')
                        self.mm(psq, psq[:kr_t, 0:nq], KA[b], KA[b][:, h, t * 128:t * 128 + kr_t], qaT, qaT[:, h, qlo:Tn], True, True)
                        p = PT[pti % len(PT)]
                        pti += 1
                        kb.op('act', lambda e: e.activation(out=p[:kr_t, 0:nq], in_=psq[:kr_t, 0:nq], func=AF.Exp, scale=A_SCALE),
                              reads=[psq], writes=[p])
                        if diag is not None:
                            kb.op('pool', lambda e: e.memset(p[64:128, 0:64], 0.0), reads=[p], writes=[p])

                        def pv(p=p, h=h, t=t, b=b, kr_t=kr_t, qlo=qlo):
                            vq = [(qi, q0, r) for qi, (q0, r) in enumerate(qts) if q0 >= qlo]
                            for j, (qi, q0, r) in enumerate(vq):
                                self.mm(accs[h], accs[h][:r, qi * 65:qi * 65 + 65], p, p[:kr_t, q0 - qlo:q0 - qlo + r],
                                        VA[b], VA[b][:kr_t, t, h, :], first_acc[h], True, inc=(j == len(vq) - 1), skip_group_check=True)
                                first_acc[h] = False
                        pend.append(pv)
                        if len(pend) > 1:
                            pend.pop(0)()
                        if bg is not None:
                            for _ in range(pace):
                                next(bg, None)
            while pend:
                pend.pop(0)()
            if bg is not None:
                for _ in bg:
                    pass
            self.attn_finish(pes, accs, brT, 0)
            kb.barrier()

    def attn_finish(self, pes, accs, brT, chunk0, qsel=None):
        kb = self.kb
        qts = self.qts if qsel is None else qsel
        recs = [kb.sb("rec%d" % i, [128, 4], F32, es=pes) for i in range(2)]
        obs = [kb.sb("ob%d" % i, [128, 256], BF16, es=pes) for i in range(2)]
        for gi, (q0, r) in enumerate(qts):
            rec, ob = recs[gi % 2], obs[gi % 2]
            for h in range(4):
                kb.op('dve', lambda e, h=h: e.reciprocal(out=rec[:r, h:h + 1], in_=accs[h][:r, gi * 65 + 64:gi * 65 + 65]),
                      reads=[accs[h], rec], writes=[rec])
                kb.op('act', lambda e, h=h: e.activation(out=ob[:r, h * 64:(h + 1) * 64], in_=accs[h][:r, gi * 65:gi * 65 + 64],
                                                         func=AF.Copy, scale=rec[:r, h:h + 1]), reads=[accs[h], rec, ob], writes=[ob])
            pt = kb.ps('misc')
            ptb = pt[:, :].bitcast(BF16)
            for c in range(2):
                self.tr(pt, ptb[:, c * 128:c * 128 + r], ob, ob[:r, c * 128:(c + 1) * 128], r)
            kb.op('act', lambda e: e.copy(out=brT[:, chunk0:chunk0 + 2, q0:q0 + r],
                                          in_=ptb[:, 0:256].rearrange("p (a b) -> p a b", a=2)[:, :, 0:r]),
                  reads=[pt, brT], writes=[brT])

    def dsa_open(self, l, pes, qiT, wqs):
        cfg, kb = self.cfg, self.kb
        Tn, qts, P, kind = self.Tn, self.qts, self.P, self.kind
        NK = sum(t[2] for t in self.ktiles)
        sups = self.key_supers()
        st = {}
        kidxT = kb.sb("kidxT", [32, NK], BF16, es=pes)
        kiin16 = [kb.sb("kiin16_%d" % i, [128, 4, 32], BF16, es=pes) for i in range(2)]
        k0 = 0
        koffs = []
        for si, sup in enumerate(sups):
            b = si % 2
            nt = len(sup)
            a1, d1, kr = self.ksrc("kidx", l, sup)
            nk = sum(t[2] for t in sup)
            kb.dma('pool', kiin16[b][:kr, 0:nt, :], a1, reads=d1, writes=[kiin16[b]])
            pt = kb.ps('misc')
            ptb = pt[:, :].bitcast(BF16)
            for t in range(nt):
                self.tr(pt, ptb[0:32, t * 128:t * 128 + kr], kiin16[b], kiin16[b][:kr, t, :], kr)
            kb.op('act', lambda e: e.copy(out=kidxT[:, k0:k0 + nk], in_=ptb[0:32, 0:nk]), reads=[pt, kidxT], writes=[kidxT])
            koffs.append(k0)
            k0 += nk
        GQ = cfg.GQ
        iscs = [kb.sb("isc%d" % i, [128, NK], F32, es=pes) for i in range(2)]
        mask = kb.sb("mask", [128, GQ, NK], BF16, es=pes)
        mask1 = T(mask.t)
        mtok = [mask, mask1] + [T(mask.t) for _ in range(max(0, GQ - 2))]
        rl = [kb.sb("rl%d" % i, [128, 512], F32, es=pes) for i in range(2)]
        bss = [kb.sb("bs%d" % i, [128, 8 + NIT], F32, es=pes) for i in range(2)]
        st.update(dict(NK=NK, sups=sups, koffs=koffs, kidxT=kidxT, iscs=iscs, mask=mask, mask1=mask1, rl=rl, bss=bss,
                       qiT=qiT, wqs=wqs, GQ=GQ, rli=0, mtok=mtok))
        return st

    def dsa_pass1(self, l, st, g0):
        cfg, kb = self.cfg, self.kb
        Tn, qts, P, kind = self.Tn, self.qts, self.P, self.kind
        NK, sups, koffs, kidxT, iscs, mask, mask1, rl, bss, qiT, wqs, GQ = (st[k] for k in (
            "NK", "sups", "koffs", "kidxT", "iscs", "mask", "mask1", "rl", "bss", "qiT", "wqs", "GQ"))
        rli = st["rli"]
        grp = qts[g0:g0 + 2]
        mtok = st["mtok"]
        ms0 = g0
        qc0 = grp[0][0]
        qc1 = grp[-1][0] + grp[-1][1]
        nvs = []
        for gi, (q0, r) in enumerate(grp):
            qi = g0 + gi
            isc = iscs[gi % 2]
            nvis = (P + q0 + 128) if kind == 'p' else NK
            nvs.append(nvis)
            for si, sup in enumerate(sups):
                kk0 = koffs[si]
                if kk0 >= nvis:
                    break
                nk = min(sum(t[2] for t in sup), nvis - kk0)
                for h in range(4):
                    psi = kb.ps('qk')
                    self.mm(psi, psi[:r, 0:nk], qiT, qiT[:, h, q0:q0 + r], kidxT, kidxT[:, kk0:kk0 + nk], True, True)
                    if h == 0:
                        kb.op('dve', lambda e: e.tensor_scalar(out=isc[:r, kk0:kk0 + nk], in0=psi[:r, 0:nk], scalar1=0.0,
                                                               scalar2=wqs[:r, qi, 0:1], op0=ALU.max, op1=ALU.mult),
                              reads=[psi, wqs, isc], writes=[isc])
                    else:
                        rr = rl[rli % 2]
                        rli += 1
                        kb.op('act', lambda e: e.activation(out=rr[:r, 0:nk], in_=psi[:r, 0:nk], func=AF.Relu),
                              reads=[psi], writes=[rr])
                        kb.op('dve', lambda e: e.scalar_tensor_tensor(out=isc[:r, kk0:kk0 + nk], in0=rr[:r, 0:nk],
                                                                      scalar=wqs[:r, qi, h:h + 1], in1=isc[:r, kk0:kk0 + nk],
                                                                      op0=ALU.mult, op1=ALU.add),
                              reads=[rr, wqs, isc], writes=[isc])
                yield
            bs = bss[gi % 2]
            kb.op('dve', lambda e: e.memset(bs[:], 0.0), writes=[bs])
            kb.op('dve', lambda e: e.tensor_reduce(out=bs[:r, 0:1], in_=isc[:r, 0:nvis], axis=AX.X, op=ALU.max), reads=[isc, bs], writes=[bs])
            kb.op('dve', lambda e: e.tensor_reduce(out=bs[:r, 1:2], in_=isc[:r, 0:nvis], axis=AX.X, op=ALU.min), reads=[isc, bs], writes=[bs])
            if kind == 'p':
                kb.op('dve', lambda e: e.memset(isc[0:64, nvis - 64:nvis], NEG_BIG), reads=[isc], writes=[isc])
            kb.op('dve', lambda e: e.tensor_tensor(out=bs[:r, 2:3], in0=bs[:r, 0:1], in1=bs[:r, 1:2], op=ALU.subtract), reads=[bs], writes=[bs])
            if gi % 2 == 1:
                kb.op('dve', lambda e: e.tensor_scalar(out=bs[:r, 1:2], in0=bs[:r, 1:2], scalar1=-1.0, scalar2=0.0, op0=ALU.mult, op1=ALU.add),
                      reads=[bs], writes=[bs])
        def act_update(it_):
            f_ = 2.0 ** -it_
            (q0, r), isc, bs, nvis = grp[1], iscs[1], bss[1], nvs[1]
            kb.op('dve', lambda e: e.tensor_scalar(out=bs[:r, 4:5], in0=bs[:r, 7 + it_:8 + it_], scalar1=2.0 * self.ktop - nvis - 0.5,
                                                   scalar2=-f_, op0=ALU.is_ge, op1=ALU.mult), reads=[bs], writes=[bs])
            kb.op('dve', lambda e: e.scalar_tensor_tensor(out=bs[:r, 1:2], in0=bs[:r, 4:5], scalar=bs[:r, 2:3], in1=bs[:r, 1:2],
                                                          op0=ALU.mult, op1=ALU.add), reads=[bs], writes=[bs])

        for it in range(1, NIT + 1):
            f = 2.0 ** -it
            yield
            if len(grp) > 1:
                if it > 1:
                    act_update(it - 1)
                (q0, r), isc, bs, nvis = grp[1], iscs[1], bss[1], nvs[1]
                kb.op('dve', lambda e: e.scalar_tensor_tensor(out=bs[:r, 3:4], in0=bs[:r, 2:3], scalar=-f, in1=bs[:r, 1:2],
                                                              op0=ALU.mult, op1=ALU.add), reads=[bs], writes=[bs])
            (q0, r), isc, bs, nvis = grp[0], iscs[0], bss[0], nvs[0]
            kb.op('dve', lambda e: e.scalar_tensor_tensor(out=bs[:r, 3:4], in0=bs[:r, 2:3], scalar=f, in1=bs[:r, 1:2],
                                                          op0=ALU.mult, op1=ALU.add), reads=[bs], writes=[bs])
            kb.op('dve', lambda e: e.tensor_scalar(out=mask[:r, ms0, 0:nvis], in0=isc[:r, 0:nvis], scalar1=bs[:r, 3:4], scalar2=0.0,
                                                   op0=ALU.is_ge, op1=ALU.add, accum_out=bs[:r, 7 + it:8 + it]),
                  reads=[isc, bs, mtok[ms0]], writes=[mtok[ms0], bs])
            kb.op('dve', lambda e: e.tensor_scalar(out=bs[:r, 4:5], in0=bs[:r, 7 + it:8 + it], scalar1=float(self.ktop) - 0.5,
                                                   scalar2=f, op0=ALU.is_ge, op1=ALU.mult), reads=[bs], writes=[bs])
            kb.op('dve', lambda e: e.scalar_tensor_tensor(out=bs[:r, 1:2], in0=bs[:r, 4:5], scalar=bs[:r, 2:3], in1=bs[:r, 1:2],
                                                          op0=ALU.mult, op1=ALU.add), reads=[bs], writes=[bs])
            if len(grp) > 1:
                yield
                (q0, r), isc, bs, nvis = grp[1], iscs[1], bss[1], nvs[1]
                kb.op('act', lambda e: e.activation(out=mask[:r, ms0 + 1, 0:nvis], in_=isc[:r, 0:nvis], func=AF.Sign, bias=bs[:r, 3:4], scale=1.0,
                                                    accum_out=bs[:r, 7 + it:8 + it]), reads=[isc, bs, mtok[ms0 + 1]], writes=[mtok[ms0 + 1], bs])
        if len(grp) > 1:
            yield
            act_update(NIT)
        for gi, (q0, r) in enumerate(grp):
            isc, bs, nvis = iscs[gi % 2], bss[gi % 2], nvs[gi]
            mk = mtok[ms0 + gi]
            if gi % 2 == 1:
                kb.op('dve', lambda e: e.tensor_scalar(out=bs[:r, 5:6], in0=bs[:r, 1:2], scalar1=-1.0, scalar2=0.0, op0=ALU.mult, op1=ALU.add),
                      reads=[bs], writes=[bs])
                tcol = bs[:r, 5:6]
            else:
                tcol = bs[:r, 1:2]
            kb.op('dve', lambda e: e.tensor_scalar(out=mask[:r, ms0 + gi, 0:nvis], in0=isc[:r, 0:nvis], scalar1=tcol, scalar2=1.0,
                                                   op0=ALU.is_ge, op1=ALU.mult), reads=[isc, bs, mk], writes=[mk])
        st["rli"] = rli
        yield

    def dsa_rest(self, l, pes, st, brT, bqT, first_done):
        cfg, kb = self.cfg, self.kb
        Tn, qts, P, kind = self.Tn, self.qts, self.P, self.kind
        NK, sups, koffs, mask, mask1, GQ, mtok = (st[k] for k in ("NK", "sups", "koffs", "mask", "mask1", "GQ", "mtok"))
        bk16 = [kb.sb("bk16_%d" % i, [128, 4, 256], BF16, es=pes) for i in range(2)]
        KBt = [kb.sb("KBt%d" % i, [128, 2, 512], BF16, es=pes) for i in range(2)]
        VB = [kb.sb("VB%d" % i, [128, 4, 4, 65], BF16, es=pes) for i in range(2)]
        Eb = [kb.sb("Eb%d" % i, [128, 512], BF16, es=pes) for i in range(2)]
        tb = [kb.sb("tbias%d" % i, [128, 260], F32, es=pes) for i in range(2)]
        PT = [kb.sb("PTb%d" % i, [128, 512], BF16, es=pes) for i in range(4)]
        self.fin_bufs = [(kb.sb("recb%d" % i, [128, 4], F32, es=pes), kb.sb("obb%d" % i, [128, 256], BF16, es=pes)) for i in range(2)]
        for v in VB:
            kb.op('dve', lambda e, v=v: e.memset(v[:, :, :, 64:65], 1.0), writes=[v])
        VBd = [[T(v.t) for _ in range(4)] for v in VB]
        for g0 in range(0, len(qts), GQ):
            grp = qts[g0:g0 + GQ]
            qc0 = grp[0][0]
            qc1 = grp[-1][0] + grp[-1][1]
            accs = [kb.ps('acc') for _ in range(4)]
            first_acc = [True] * 4
            pti = 0
            pend = []
            nvis_g = (P + qc1) if kind == 'p' else NK
            for si, sup in enumerate(sups):
                kk0 = koffs[si]
                if kk0 >= nvis_g:
                    break
                b = si % 2
                nt = len(sup)
                a1, d1, kr = self.ksrc("bk", l, sup)
                a2, d2, _ = self.ksrc("bv", l, sup)
                kb.dma('pool', bk16[b][:kr, 0:nt, :], a1, reads=d1, writes=[bk16[b]])
                for t in range(nt):
                    kb.dma('pool', VB[b][:kr, t, :, 0:64], a2[:, t, :].rearrange("p (h c) -> p h c", h=4), reads=d2, writes=[VBd[b][t]])
                pt = kb.ps('misc')
                ptb = pt[:, :].bitcast(BF16)
                for t in range(nt):
                    for pr in range(2):
                        self.tr(pt, ptb[:, pr * 512 + t * 128:pr * 512 + t * 128 + kr], bk16[b], bk16[b][:kr, t, pr * 128:(pr + 1) * 128], kr)
                nk = sum(t[2] for t in sup)
                kb.op('act', lambda e: e.copy(out=KBt[b][:, :, 0:nk], in_=ptb[:, :].rearrange("p (a b) -> p a b", a=2)[:, :, 0:nk]),
                      reads=[pt], writes=[KBt[b]])
                for t, (_, _, kr_t, apos) in enumerate(sup):
                    qlo, diag = self.vis(apos, kr_t)
                    qlo = max(qlo, qc0)
                    if qlo >= qc1:
                        continue
                    nq = qc1 - qlo
                    kcol = kk0 + t * 128
                    pm = kb.ps('misc')
                    pmb = pm[:, :].bitcast(BF16)
                    for gi, (q0, r) in enumerate(grp):
                        if q0 < qlo:
                            continue
                        self.tr(pm, pmb[:kr_t, q0 - qc0:q0 - qc0 + r], mtok[gi], mask[:r, gi, kcol:kcol + kr_t], r)
                    a = (apos - P) // 128
                    blo = max(qlo, 128 * a) if a >= -1 else None
                    bhi = min(qc1, 128 * a + 257) if a >= -1 else None
                    for h in range(4):
                        hp, ho = h // 2, (h % 2) * 64
                        psq = kb.ps('qk')
                        self.mm(psq, psq[:kr_t, 0:nq], KBt[b], KBt[b][ho:ho + 64, hp, t * 128:t * 128 + kr_t],
                                bqT, bqT[ho:ho + 64, hp, qlo:qc1], True, True)
                        E = Eb[pti % len(Eb)]
                        p = PT[pti % len(PT)]
                        tbt = tb[pti % len(tb)]
                        pti += 1
                        if blo is not None and blo < bhi:
                            wdt = bhi - blo
                            kb.op('dve', lambda e: e.scalar_tensor_tensor(
                                out=tbt[:kr_t, 0:wdt], in0=psq[:kr_t, blo - qlo:bhi - qlo], scalar=B_SCALE,
                                in1=self.gt[:kr_t, h, blo - 128 * a:bhi - 128 * a], op0=ALU.mult, op1=ALU.add),
                                reads=[psq, self.gt, tbt], writes=[tbt])
                            kb.op('act', lambda e: e.activation(out=E[:kr_t, blo - qlo:bhi - qlo], in_=tbt[:kr_t, 0:wdt],
                                                                func=AF.Exp), reads=[tbt, E], writes=[E])
                            for (c0, c1) in ((qlo, blo), (bhi, qc1)):
                                if c1 > c0:
                                    kb.op('act', lambda e: e.activation(
                                        out=E[:kr_t, c0 - qlo:c1 - qlo], in_=psq[:kr_t, c0 - qlo:c1 - qlo], func=AF.Exp, scale=B_SCALE),
                                        reads=[psq, E], writes=[E])
                        else:
                            kb.op('act', lambda e: e.activation(out=E[:kr_t, 0:nq], in_=psq[:kr_t, 0:nq], func=AF.Exp, scale=B_SCALE),
                                  reads=[psq, E], writes=[E])
                        kb.op('dve', lambda e: e.tensor_tensor(out=p[:kr_t, 0:nq], in0=E[:kr_t, 0:nq],
                                                               in1=pmb[:kr_t, qlo - qc0:qc1 - qc0], op=ALU.mult),
                              reads=[E, pm, p], writes=[p])

                        def pv(p=p, h=h, hp=hp, t=t, b=b, kr_t=kr_t, qlo=qlo):
                            vq = [(gi, q0, r) for gi, (q0, r) in enumerate(grp) if q0 >= qlo]
                            for j, (gi, q0, r) in enumerate(vq):
                                col = gi * 65
                                self.mm(accs[h], accs[h][:r, col:col + 65], p, p[:kr_t, q0 - qlo:q0 - qlo + r],
                                        VBd[b][t], VB[b][:kr_t, t, h, :], first_acc[h], True, inc=(j == len(vq) - 1), skip_group_check=True)
                                first_acc[h] = False
                        pend.append(pv)
                        if len(pend) > 1:
                            pend.pop(0)()
            while pend:
                pend.pop(0)()
            self.dsa_finish(pes, accs, brT, grp, GQ)

    def dsa_finish(self, pes, accs, brT, grp, GQ):
        kb = self.kb
        if not hasattr(self, "_dsa_fin"):
            self._dsa_fin = 0
        for gi, (q0, r) in enumerate(grp):
            rec, ob = self.fin_bufs[gi % 2]
            for h in range(4):
                a = accs[h]
                col = gi * 65
                kb.op('dve', lambda e, h=h, a=a, col=col: e.reciprocal(out=rec[:r, h:h + 1], in_=a[:r, col + 64:col + 65]),
                      reads=[a, rec], writes=[rec])
                kb.op('act', lambda e, h=h, a=a, col=col: e.activation(out=ob[:r, h * 64:(h + 1) * 64], in_=a[:r, col:col + 64],
                                                                       func=AF.Copy, scale=rec[:r, h:h + 1]), reads=[a, rec, ob], writes=[ob])
            pt = kb.ps('misc')
            ptb = pt[:, :].bitcast(BF16)
            for c in range(2):
                self.tr(pt, ptb[:, c * 128:c * 128 + r], ob, ob[:r, c * 128:(c + 1) * 128], r)
            kb.op('act', lambda e: e.copy(out=brT[:, 2:4, q0:q0 + r],
                                          in_=ptb[:, 0:256].rearrange("p (a b) -> p a b", a=2)[:, :, 0:r]),
                  reads=[pt, brT], writes=[brT])
        self._dsa_fin += 1

    def phase_merge(self, l, brT):
        cfg, kb = self.cfg, self.kb
        Tn = self.Tn
        x, xT = self.x, self.xT
        with ExitStack() as pes:
            mixed = kb.sb("mixed", [128, 8, Tn], F32, es=pes)
            mixed16 = kb.sb("mixed16", [128, 8, Tn], BF16, es=pes)
            sgs = [kb.sb("sgm%d" % i, [128, Tn], F32, es=pes) for i in range(2)]
            tmp = [kb.sb("tmpm%d" % i, [128, Tn], F32, es=pes) for i in range(2)]
            i = 0
            for hf in range(2):
                for n in range(4):
                    wbr = self.getw("wbr%d%d" % (n, hf), l)
                    wbr3 = wbr[:, 0:1024].rearrange("p (k n) -> p k n", k=2)
                    wg = self.getw("wg%d%d" % (n, hf), l, hold=1)
                    wg3 = wg[:, :].rearrange("p (k n) -> p k n", k=8)
                    for dl in range(4):
                        dc = hf * 4 + dl
                        pg = kb.ps()
                        for k in range(8):
                            self.mm(pg, pg[:, 0:Tn], wg, wg3[:, k, dl * 128:(dl + 1) * 128], xT, xT[:, k, 0:Tn], k == 0, k == 7, inc=(k == 7))
                        pb = kb.ps()
                        for k in range(2):
                            self.mm(pb, pb[:, 0:Tn], wbr, wbr3[:, k, dl * 128:(dl + 1) * 128], brT, brT[:, n * 2 + k, 0:Tn], k == 0, k == 1, inc=(k == 1))
                        sg = sgs[i % 2]
                        tm = tmp[i % 2]
                        i += 1
                        kb.op('act', lambda e: e.activation(out=sg[:, :], in_=pg[:, 0:Tn], func=AF.Sigmoid), reads=[pg], writes=[sg])
                        if n == 0:
                            kb.op('dve', lambda e: e.tensor_tensor(out=mixed[:, dc, :], in0=pb[:, 0:Tn], in1=sg[:, :], op=ALU.mult),
                                  reads=[pb, sg, mixed], writes=[mixed])
                        else:
                            kb.op('dve', lambda e: e.tensor_tensor(out=tm[:, :], in0=pb[:, 0:Tn], in1=sg[:, :], op=ALU.mult),
                                  reads=[pb, sg], writes=[tm])
                            if n < 3:
                                kb.op('dve', lambda e: e.tensor_tensor(out=mixed[:, dc, :], in0=mixed[:, dc, :], in1=tm[:, :], op=ALU.add),
                                      reads=[tm, mixed], writes=[mixed])
                            else:
                                kb.op('dve', lambda e: e.tensor_tensor(out=mixed16[:, dc, :], in0=mixed[:, dc, :], in1=tm[:, :], op=ALU.add),
                                      reads=[tm, mixed, mixed16], writes=[mixed16])
            s = kb.sb("s_ln1", [128, 8, Tn], F32, es=pes)
            for hf in range(2):
                wo = self.getw("wout%d" % hf, l)
                wo3 = wo[:, :].rearrange("p (k n) -> p k n", k=8)
                for dl in range(4):
                    dc = hf * 4 + dl
                    py = kb.ps()
                    for k in range(8):
                        self.mm(py, py[:, 0:Tn], wo, wo3[:, k, dl * 128:(dl + 1) * 128], mixed16, mixed16[:, k, :], k == 0, k == 7, inc=(k == 7))
                    kb.op('dve', lambda e: e.scalar_tensor_tensor(out=s[:, dc, :], in0=x[:, dc, 0:Tn], scalar=cfg.ALPHA, in1=py[:, 0:Tn],
                                                                  op0=ALU.mult, op1=ALU.add), reads=[x, py, s], writes=[s])
            pv = self.pv_ln
            self.lnfm(pes, s, 8, Tn, self.ones1024, lambda c: pv[:, l, 0, c:c + 1], lambda c: pv[:, l, 1, c:c + 1], pv,
                      [(x, lambda c: x[:, c, 0:Tn]), (xT, lambda c: xT[:, c, 0:Tn])])
            kb.barrier()

    def phase_ffn(self, l):
        cfg, kb, I, O = self.cfg, self.kb, self.I, self.O
        Tn, kind, pre, row0, qts = self.Tn, self.kind, self.pre, self.row0, self.qts
        x, xT = self.x, self.xT
        fcar = self.fcar[kind]
        fcv = fcar[:, l, :, :].rearrange("p (g c) t -> p g c t", g=2)
        pvf = self.pv_f
        with ExitStack() as pes:
            hT = kb.sb("hT", [128, NFC, Tn], BF16, es=pes)
            Es = [kb.sb("Effn%d" % i, [128, 2, 2 + Tn], F32, es=pes) for i in range(3)]
            av = [kb.sb("avf%d" % i, [128, 2, Tn], F32, es=pes) for i in range(3)]
            sgl = [kb.sb("sgf%d" % i, [128, Tn], F32, es=pes) for i in range(3)]
            Ecs = [T(E_.t) for E_ in Es]
            ci = 0
            tails = []
            for g in range(6):
                nch = 4 if g < 5 else 2
                wvv = self.getw("wupv%d" % g, l)
                wgg = self.getw("wupg%d" % g, l, hold=1)
                wts = (wvv, wgg)
                w3s = [w_[:, 0:8 * nch * 128].rearrange("p (k n) -> p k n", k=8) for w_ in wts]
                for cl in range(nch):
                    c = g * 4 + cl
                    E = Es[ci % 3]
                    a = av[ci % 3]
                    sg = sgl[ci % 3]
                    ci += 1
                    Ec = Ecs[(ci - 1) % 3]
                    kb.op('dve', lambda e, E=E, c=c: e.tensor_copy(out=E[:, :, 0:2], in_=fcv[:, :, c, :]), reads=[fcar], writes=[Ec])
                    for gg in range(2):
                        cc = gg * NFC + c
                        ps = kb.ps()
                        for k in range(8):
                            self.mm(ps, ps[:, 0:Tn], wts[gg], w3s[gg][:, k, cl * 128:(cl + 1) * 128], xT, xT[:, k, 0:Tn], k == 0, k == 7, inc=(k == 7))
                        kb.op('act', lambda e: e.copy(out=E[:, gg, 2:2 + Tn], in_=ps[:, 0:Tn]), reads=[ps, E], writes=[E])
                        kb.op('act', lambda e: e.activation(out=a[:, gg, :], in_=ps[:, 0:Tn], func=AF.Identity, scale=pvf[:, l, cc, 2:3],
                                                            bias=pvf[:, l, cc, 3:4]), reads=[ps, pvf, a], writes=[a])
                    kb.op('dve', lambda e: e.tensor_copy(out=fcv[:, :, c, :], in_=E[:, :, Tn:Tn + 2]), reads=[E, fcar], writes=[fcar])
                    for gg in range(2):
                        cc = gg * NFC + c
                        for k in (1, 0):
                            kb.op('dve', lambda e: e.scalar_tensor_tensor(
                                out=a[:, gg, :], in0=E[:, gg, k:k + Tn], scalar=pvf[:, l, cc, k:k + 1], in1=a[:, gg, :], op0=ALU.mult, op1=ALU.add),
                                reads=[E, Ec, pvf, a], writes=[a])
                    def tail(a=a, sg=sg, c=c):
                        kb.op('act', lambda e: e.activation(out=sg[:, :], in_=a[:, 1, :], func=AF.Silu), reads=[a], writes=[sg])
                        kb.op('dve', lambda e: e.tensor_tensor(out=hT[:, c, :], in0=a[:, 0, :], in1=sg[:, :], op=ALU.mult),
                              reads=[a, sg, hT], writes=[hT])
                    tails.append(tail)
                    if len(tails) > 1:
                        tails.pop(0)()
            while tails:
                tails.pop(0)()
            if self.last_block:
                sts = [kb.sb("stf%d" % i, [2, 512], F32, es=pes) for i in range(2)]
                for c0 in range(0, 44, 4):
                    st = sts[(c0 // 4) % 2]
                    ps = kb.ps()
                    for i in range(4):
                        self.tr(ps, ps[0:2, i * 128:(i + 1) * 128], fcar, fcar[:, l, c0 + i, :], 128, bf=False)
                    kb.op('act', lambda e: e.copy(out=st[0:2, :], in_=ps[0:2, :]), reads=[ps, st], writes=[st])
                    kb.dma('sp', O[pre + "_ffn"][l, :, c0 * 128:(c0 + 4) * 128], st[0:2, :], reads=[st])
            s = kb.sb("s_ln2", [128, 8, Tn], F32, es=pes)
            for dc in range(8):
                w = self.getw("wdn%d" % dc, l)
                w3 = w[:, 0:NFC * 128].rearrange("p (k n) -> p k n", k=NFC)
                py = kb.ps()
                for k in range(NFC):
                    self.mm(py, py[:, 0:Tn], w, w3[:, k, :], hT, hT[:, k, :], k == 0, k == NFC - 1, inc=(k == NFC - 1))
                kb.op('dve', lambda e: e.scalar_tensor_tensor(out=s[:, dc, :], in0=x[:, dc, 0:Tn], scalar=cfg.ALPHA, in1=py[:, 0:Tn],
                                                              op0=ALU.mult, op1=ALU.add), reads=[x, py, s], writes=[s])
            pv = self.pv_ln
            self.lnfm(pes, s, 8, Tn, self.ones1024, lambda c: pv[:, l, 2, c:c + 1], lambda c: pv[:, l, 3, c:c + 1], pv,
                      [(x, lambda c: x[:, c, 0:Tn]), (xT, lambda c: xT[:, c, 0:Tn])])
            if l == cfg.L - 1:
                yts = [kb.sb("ytok%d" % i, [128, D], F32, es=pes) for i in range(2)]
                for qi, (q0, r) in enumerate(qts):
                    yt = yts[qi % 2]
                    for half in range(2):
                        ps = kb.ps()
                        for c in range(4):
                            cc = half * 4 + c
                            self.tr(ps, ps[:r, c * 128:(c + 1) * 128], x, x[:, cc, q0:q0 + r], 128, bf=False)
                        kb.op('act', lambda e, ps=ps, half=half, yt=yt, r=r: e.copy(out=yt[:r, half * 512:(half + 1) * 512], in_=ps[:r, :]),
                              reads=[ps, yt], writes=[yt])
                    kb.dma('sp', O["y_" + pre][row0 + q0:row0 + q0 + r, :], yt[:r, :], reads=[yt])
            kb.barrier()


def _rel_bucket(rel):
    nb, max_exact = 16, 8
    ret = np.where(rel > 0, nb, 0)
    n = np.abs(rel)
    nf = np.maximum(n, 1).astype(np.float32)
    large = max_exact + (np.log(nf / np.float32(max_exact)) / np.float32(math.log(128 / max_exact))
                         * np.float32(nb - max_exact)).astype(np.int32)
    large = np.minimum(large, nb - 1)
    return ret + np.where(n < max_exact, n, large)


def _consts(cfg):
    half = 16
    freqs = (10000.0 ** (-np.arange(half, dtype=np.float32) / half)).astype(np.float32)

    def rope_tab(pos):
        ang = pos.astype(np.float32)[:, None] * freqs[None, :]
        c, s = np.cos(ang).astype(np.float32), np.sin(ang).astype(np.float32)
        return np.concatenate([c, c, -s, s], axis=1).astype(np.float32)

    rope_p = rope_tab(np.arange(cfg.SEQ))
    rope_s = rope_tab(cfg.PAST + np.arange(cfg.TS))
    p = np.arange(128)[:, None]
    jj = np.arange(258)[None, :]
    bidx = _rel_bucket(p - jj).astype(np.float32)
    icnt = np.zeros((128, 2, cfg.TB), np.float32)
    t = np.arange(cfg.TB)
    for g, w in enumerate((2, 4, 8, 16)):
        c, lo = g // 2, (g % 2) * 64
        icnt[lo:lo + 64, c, :] = 1.0 / np.minimum(w, t + 1).astype(np.float32)[None, :]
    return rope_p, rope_s, bidx, icnt.reshape(128, 2 * cfg.TB)


_CACHE = {}


def _get_nc(cfg_key):
    if cfg_key not in _CACHE:
        cfg = Cfg(*cfg_key)
        gen = Gen(cfg)
        nc = bass.Bass("TRN2", target_bir_lowering=False)
        gen.build(nc)
        _CACHE[cfg_key] = (cfg, gen, nc)
    return _CACHE[cfg_key]


def run(inputs, cfg_key):
    cfg, gen, nc = _get_nc(cfg_key)
    L = cfg.L
    f = lambda a: np.ascontiguousarray(np.asarray(a, dtype=np.float32))
    rope_p, rope_s, bidx, icnt0 = _consts(cfg)
    B = inputs["x_prompt"].shape[0]
    NS = inputs["x_sample"].shape[0]
    ncores = 8
    shared = {
        "rel_bias": f(inputs["rel_bias"]).reshape(1, 128), "ln_in_g": f(inputs["ln_in_g"]).reshape(1, D),
        "ln_in_b": f(inputs["ln_in_b"]).reshape(1, D), "rope_p": rope_p, "rope_s": rope_s, "bidx": bidx, "icnt0": icnt0,
    }
    for nm in ("w_in", "a_q_norm", "a_kv_norm", "a_w_qup", "a_w_kvup", "pool_w", "pool_scale", "conv_w", "conv_b",
               "conv_ln_g", "conv_ln_b", "w_branch", "w_out", "ln1_g", "ln1_b", "w_up", "ffn_conv_w", "ffn_conv_b",
               "w_down", "ln2_g", "ln2_b"):
        shared[nm] = f(inputs[nm])
    in_maps = []
    for c in range(ncores):
        m = dict(shared)
        m["xp"] = f(inputs["x_prompt"][c % B])
        s = c % NS
        m["xs"] = f(inputs["x_sample"][s])
        m["c_ckv"] = f(inputs["cache_a_ckv"][:, s]); m["c_krope"] = f(inputs["cache_a_krope"][:, s])
        m["c_bk"] = f(inputs["cache_b_k"][:, s]).reshape(L, cfg.PAST, 256)
        m["c_bv"] = f(inputs["cache_b_v"][:, s]).reshape(L, cfg.PAST, 256)
        m["c_kidx"] = f(inputs["cache_b_kidx"][:, s])
        m["st_pool"] = f(inputs["state_pool"][:, s]); m["st_conv"] = f(inputs["state_conv"][:, s])
        m["st_ffn"] = f(inputs["state_ffn"][:, s])
        in_maps.append(m)
    res = run_bass_kernel_spmd(nc, in_maps, core_ids=list(range(ncores)))
    R = res.results

    def stack_p(name, shape_tail):
        return np.stack([R[b]["p_" + name] for b in range(B)], axis=1).reshape((L, B) + shape_tail)

    def stack_s(name, shape_tail):
        return np.stack([R[s]["s_" + name] for s in range(NS)], axis=1).reshape((L, NS) + shape_tail)

    outs = [np.stack([R[b]["y_p"] for b in range(B)], 0), np.stack([R[s]["y_s"] for s in range(NS)], 0)]
    for st, n in ((stack_p, cfg.SEQ), (stack_s, cfg.TS)):
        outs += [st("ckv", (n, 128)), st("krope", (n, 32)), st("bk", (n, 4, 64)), st("bv", (n, 4, 64)), st("kidx", (n, 32)),
                 st("pool", (15, 256)), st("conv", (30, 256)), st("ffn", (2, 2 * DFF))]
    return tuple(np.ascontiguousarray(o.astype(np.float32)) for o in outs)


def kernel(**inputs):
    return run(inputs, (4096, 4, 4096, 16, 512, 4))
```
